# Optimizing a Trainium2 kernel written in Bass

```python
import math
import jax, jax.numpy as jnp
from jax import lax
import numpy as np

D_MODEL = 1024
BATCH = 2
SEQ = 8192
DEPTH = 4
DEC_BATCH = 32
DEC_SEQ = 1
PAST_LEN = 8192
PAGE_SIZE = 128

SSM_WIDTH = D_MODEL // 2
SSM_GROUP = 16
SSM_GROUPS = SSM_WIDTH // SSM_GROUP
SSM_N = 64
CONV_WIDTH = D_MODEL // 2
CONV_K = 31
HEAD_DIM = 64
HEADS_PER_GROUP = 4
ATTN_GROUPS = ((128, 1), (512, 4), (2048, 16))
ATTN_HEADS = HEADS_PER_GROUP * len(ATTN_GROUPS)
ATTN_WIDTH = ATTN_HEADS * HEAD_DIM
ATTN_OUT = HEADS_PER_GROUP * HEAD_DIM
ATTN_BLOCK = 128
ROPE_THETA = 10000.0
FFN_HIDDEN = 2816
FFN_CONV_K = 3
IN_WIDTH = SSM_WIDTH + 2 * CONV_WIDTH + 3 * ATTN_WIDTH
EPS = 1e-6
F32 = jnp.float32

kernel_name = 'hybrid_s5_conformer_dilated_attn_step'


def rmsnorm(x, g):
    xf = x.astype(F32)
    y = xf * lax.rsqrt(jnp.mean(xf * xf, axis=-1, keepdims=True) + EPS)
    return (y * g.astype(F32)).astype(x.dtype)


def layernorm(x, g, b):
    xf = x.astype(F32)
    mu = jnp.mean(xf, axis=-1, keepdims=True)
    var = jnp.mean(jnp.square(xf - mu), axis=-1, keepdims=True)
    return ((xf - mu) * lax.rsqrt(var + EPS) * g.astype(F32) + b.astype(F32)).astype(x.dtype)


def rope(x, pos):
    half = HEAD_DIM // 2
    inv = ROPE_THETA ** (-jnp.arange(half, dtype=F32) / half)
    ang = pos.astype(F32)[:, None] * inv[None, :]
    cos = jnp.cos(ang)[:, None, :]
    sin = jnp.sin(ang)[:, None, :]
    xf = x.astype(F32)
    x1, x2 = xf[..., :half], xf[..., half:]
    return jnp.concatenate([x1 * cos - x2 * sin, x1 * sin + x2 * cos], axis=-1).astype(x.dtype)


def causal_dwconv(x, buf, w, b):
    k = w.shape[0]
    ch = x.shape[-1]
    xp = jnp.concatenate([buf.astype(x.dtype), x], axis=1)
    y = lax.conv_general_dilated(xp, w.astype(x.dtype)[:, None, :], window_strides=(1,), padding='VALID',
                                 dimension_numbers=('NWC', 'WIO', 'NWC'), feature_group_count=ch)
    return y + b.astype(x.dtype), xp[:, -(k - 1):]


def ssm_branch(u, s0, a_re, a_im, log_dt, b_re, b_im, c_re, c_im, d_skip, w_glu, b_glu):
    bsz, seq, _ = u.shape
    uf = u.astype(F32).reshape(bsz, seq, SSM_GROUPS, SSM_GROUP)
    lam = lax.complex(a_re.astype(F32), a_im.astype(F32))
    dt = jnp.exp(log_dt.astype(F32))[:, None]
    lam_bar = jnp.exp(lam * dt)
    bmat = lax.complex(b_re.astype(F32), b_im.astype(F32))
    cmat = lax.complex(c_re.astype(F32), c_im.astype(F32))
    b_bar = ((lam_bar - 1.0) / lam)[..., None] * bmat
    bu = jnp.einsum('gnc,bsgc->bsgn', b_bar, uf.astype(jnp.complex64))
    init = lax.complex(s0[..., 0].astype(F32), s0[..., 1].astype(F32))
    bu = bu.at[:, 0].add(lam_bar * init)
    a = jnp.broadcast_to(lam_bar, bu.shape)

    def combine(left, right):
        a_l, b_l = left
        a_r, b_r = right
        return a_l * a_r, a_r * b_l + b_r

    _, states = lax.associative_scan(combine, (a, bu), axis=1)
    y = jnp.einsum('gcn,bsgn->bsgc', cmat, states).real + d_skip.astype(F32).reshape(SSM_GROUPS, SSM_GROUP) * uf
    y = jax.nn.gelu(y.reshape(bsz, seq, SSM_WIDTH))
    y = y * jax.nn.sigmoid(y @ w_glu.astype(F32) + b_glu.astype(F32))
    last = states[:, -1]
    new_state = jnp.stack([last.real, last.imag], axis=-1).astype(s0.dtype)
    return y.astype(u.dtype), new_state


def conv_branch(u, buf, w, b, ln_g, ln_b):
    a = u[..., :CONV_WIDTH] * jax.nn.sigmoid(u[..., CONV_WIDTH:])
    y, new_buf = causal_dwconv(a, buf, w, b)
    return jax.nn.silu(layernorm(y, ln_g, ln_b)), new_buf


def dilated_band_attention(q, k, v, window, dil):
    bsz, seq, nh, hd = q.shape
    n_back = window // dil
    span = dil * ATTN_BLOCK
    seq_pad = -(-seq // span) * span
    m_len = seq_pad // dil
    nb = m_len // ATTN_BLOCK

    def strided(t):
        t = jnp.pad(t.astype(F32), ((0, 0), (0, seq_pad - seq), (0, 0), (0, 0)))
        t = t.reshape(bsz, m_len, dil, nh, hd).transpose(0, 2, 1, 3, 4)
        return t.reshape(bsz * dil, nb, ATTN_BLOCK, nh, hd)

    def with_prev(t):
        prev = jnp.pad(t[:, :-1], ((0, 0), (1, 0), (0, 0), (0, 0), (0, 0)))
        return jnp.concatenate([prev, t], axis=2)

    qb = strided(q)
    kb = with_prev(strided(k))
    vb = with_prev(strided(v))
    s = jnp.einsum('znqhd,znkhd->znhqk', qb, kb) * (hd ** -0.5)
    q_idx = jnp.arange(ATTN_BLOCK)[:, None] + ATTN_BLOCK
    k_idx = jnp.arange(2 * ATTN_BLOCK)[None, :]
    dist = q_idx - k_idx
    key_m = jnp.arange(nb)[:, None, None] * ATTN_BLOCK + k_idx[None] - ATTN_BLOCK
    valid = ((dist >= 0) & (dist <= n_back))[None] & (key_m >= 0)
    s = jnp.where(valid[None, :, None], s, -jnp.inf)
    m = jnp.max(s, axis=-1, keepdims=True)
    p = jnp.exp(s - m)
    l = jnp.sum(p, axis=-1, keepdims=True)
    o = jnp.einsum('znhqk,znkhd->znqhd', p / l, vb)
    lse = (m + jnp.log(l))[..., 0].transpose(0, 1, 3, 2)

    def unstrided(t):
        rest = t.shape[3:]
        t = t.reshape(bsz, dil, m_len, *rest)
        return jnp.swapaxes(t, 1, 2).reshape(bsz, seq_pad, *rest)[:, :seq]

    return unstrided(o), unstrided(lse)


def dilated_step_attention(q, k, v, buf_k, buf_v, window, dil):
    buf_len = buf_k.shape[1]
    t_new, hd = q.shape[1], q.shape[-1]
    kc = jnp.concatenate([buf_k.astype(k.dtype), k], axis=1)
    vc = jnp.concatenate([buf_v.astype(v.dtype), v], axis=1)
    n_keys = window // dil + 1
    idx = buf_len + jnp.arange(t_new)[:, None] - dil * jnp.arange(n_keys)[None, :]
    valid = idx >= 0
    idx = jnp.maximum(idx, 0)
    kg = kc[:, idx].astype(F32)
    vg = vc[:, idx].astype(F32)
    s = jnp.einsum('bthd,btkhd->bthk', q.astype(F32), kg) * (hd ** -0.5)
    s = jnp.where(valid[None, :, None, :], s, -jnp.inf)
    m = jnp.max(s, axis=-1, keepdims=True)
    p = jnp.exp(s - m)
    l = jnp.sum(p, axis=-1, keepdims=True)
    o = jnp.einsum('bthk,btkhd->bthd', p / l, vg)
    lse = (m + jnp.log(l))[..., 0]
    new_buf = jnp.stack([kc[:, -buf_len:], vc[:, -buf_len:]], axis=2).astype(buf_k.dtype)
    return o, lse, new_buf


def attn_branch(zq, zk, zv, pos, kv_bufs, gq, gk):
    bsz, seq, _ = zq.shape
    q = rope(rmsnorm(zq.reshape(bsz, seq, ATTN_HEADS, HEAD_DIM), gq), pos)
    k = rope(rmsnorm(zk.reshape(bsz, seq, ATTN_HEADS, HEAD_DIM), gk), pos)
    v = zv.reshape(bsz, seq, ATTN_HEADS, HEAD_DIM)
    outs, lses, new_bufs = [], [], []
    for gi, (window, dil) in enumerate(ATTN_GROUPS):
        hs = slice(gi * HEADS_PER_GROUP, (gi + 1) * HEADS_PER_GROUP)
        qg, kg, vg = q[:, :, hs], k[:, :, hs], v[:, :, hs]
        if kv_bufs is None:
            o, lse = dilated_band_attention(qg, kg, vg, window, dil)
            keep = min(window, seq)
            new_bufs.append(jnp.stack([kg[:, -keep:], vg[:, -keep:]], axis=2))
        else:
            buf = kv_bufs[gi]
            o, lse, nbuf = dilated_step_attention(qg, kg, vg, buf[:, :, 0], buf[:, :, 1], window, dil)
            new_bufs.append(nbuf)
        outs.append(o)
        lses.append(lse)
    wts = jax.nn.softmax(jnp.stack(lses, axis=0), axis=0)
    o = jnp.sum(wts[..., None] * jnp.stack(outs, axis=0), axis=0)
    return o.reshape(bsz, seq, ATTN_OUT).astype(zq.dtype), new_bufs


def trunk_layer(x, c, pos, ssm_st, conv_st, kv_st, ffn_st, p):
    dt = x.dtype
    mod = jax.nn.silu(c) @ p['w_ada'] + p['b_ada']
    sh1, sc1, gt1, sh2, sc2, gt2 = jnp.split(mod[:, None, :], 6, axis=-1)
    h = rmsnorm(x, p['g_norm_mix']) * (1.0 + sc1) + sh1
    z = h @ p['w_in']
    c1 = SSM_WIDTH
    c2 = c1 + 2 * CONV_WIDTH
    c3 = c2 + ATTN_WIDTH
    c4 = c3 + ATTN_WIDTH
    y_ssm, new_ssm = ssm_branch(z[..., :c1], ssm_st, p['ssm_a_re'], p['ssm_a_im'], p['ssm_log_dt'],
                                p['ssm_b_re'], p['ssm_b_im'], p['ssm_c_re'], p['ssm_c_im'], p['ssm_d'],
                                p['ssm_w_glu'], p['ssm_b_glu'])
    y_conv, new_conv = conv_branch(z[..., c1:c2], conv_st, p['conv_w'], p['conv_b'],
                                   p['conv_ln_g'], p['conv_ln_b'])
    y_attn, new_kv = attn_branch(z[..., c2:c3], z[..., c3:c4], z[..., c4:], pos, kv_st,
                                 p['attn_gq'], p['attn_gk'])
    g_ssm, g_conv, g_attn = jnp.split(jax.nn.sigmoid(h @ p['w_gate'] + p['b_gate']), 3, axis=-1)
    merged = (g_ssm * (y_ssm @ p['w_br_ssm']) + g_conv * (y_conv @ p['w_br_conv'])
              + g_attn * (y_attn @ p['w_br_attn']))
    x = x + gt1 * (merged @ p['w_out'])
    h2 = rmsnorm(x, p['g_norm_ffn']) * (1.0 + sc2) + sh2
    up, new_ffn = causal_dwconv(h2 @ p['ffn_w_up'], ffn_st, p['ffn_conv_w'], p['ffn_conv_b'])
    a, b = jnp.split(up, 2, axis=-1)
    x = x + gt2 * ((jax.nn.gelu(a) * b) @ p['ffn_w_down'])
    return x.astype(dt), (new_ssm, new_conv, new_kv[0], new_kv[1], new_kv[2], new_ffn)


def setup_inputs(seed: int = 0) -> dict:
    key = jax.random.key(seed)
    ks = iter(jax.random.split(key, 64))

    def nrm(shape, scale):
        return scale * jax.random.normal(next(ks), shape, F32)

    D = D_MODEL
    F2 = 2 * FFN_HIDDEN
    kv_shape = lambda w: (DEPTH, DEC_BATCH, min(w, PAST_LEN), 2, HEADS_PER_GROUP, HEAD_DIM)
    return {
        'x_prompt': nrm((BATCH, SEQ, D), 1.0),
        'x_sample': nrm((DEC_BATCH, DEC_SEQ, D), 1.0),
        'state_ssm': nrm((DEPTH, DEC_BATCH, SSM_GROUPS, SSM_N, 2), 0.1),
        'cache_conv': nrm((DEPTH, DEC_BATCH, CONV_K - 1, CONV_WIDTH), 0.5),
        'cache_kv_w128': nrm(kv_shape(128), 1.0),
        'cache_kv_w512': nrm(kv_shape(512), 1.0),
        'cache_kv_w2048': nrm(kv_shape(2048), 1.0),
        'cache_ffn': nrm((DEPTH, DEC_BATCH, FFN_CONV_K - 1, F2), 0.5),
        'c_prompt': nrm((BATCH, D), 1.0),
        'c_sample': nrm((DEC_BATCH, D), 1.0),
        'w_ada': nrm((DEPTH, D, 6 * D), 0.5 * D ** -0.5),
        'b_ada': nrm((DEPTH, 6 * D), 0.02),
        'g_norm_mix': 1.0 + nrm((DEPTH, D), 0.02),
        'w_in': nrm((DEPTH, D, IN_WIDTH), D ** -0.5),
        'ssm_a_re': -0.5 + nrm((DEPTH, SSM_GROUPS, SSM_N), 0.01),
        'ssm_a_im': math.pi * jnp.arange(SSM_N, dtype=F32)[None, None, :] + nrm((DEPTH, SSM_GROUPS, SSM_N), 0.01),
        'ssm_log_dt': jax.random.uniform(next(ks), (DEPTH, SSM_GROUPS), F32, math.log(1e-3), math.log(1e-1)),
        'ssm_b_re': nrm((DEPTH, SSM_GROUPS, SSM_N, SSM_GROUP), (2 * SSM_GROUP) ** -0.5),
        'ssm_b_im': nrm((DEPTH, SSM_GROUPS, SSM_N, SSM_GROUP), (2 * SSM_GROUP) ** -0.5),
        'ssm_c_re': nrm((DEPTH, SSM_GROUPS, SSM_GROUP, SSM_N), SSM_N ** -0.5),
        'ssm_c_im': nrm((DEPTH, SSM_GROUPS, SSM_GROUP, SSM_N), SSM_N ** -0.5),
        'ssm_d': nrm((DEPTH, SSM_WIDTH), 0.5),
        'ssm_w_glu': nrm((DEPTH, SSM_WIDTH, SSM_WIDTH), SSM_WIDTH ** -0.5),
        'ssm_b_glu': nrm((DEPTH, SSM_WIDTH), 0.02),
        'conv_w': nrm((DEPTH, CONV_K, CONV_WIDTH), CONV_K ** -0.5),
        'conv_b': nrm((DEPTH, CONV_WIDTH), 0.02),
        'conv_ln_g': 1.0 + nrm((DEPTH, CONV_WIDTH), 0.02),
        'conv_ln_b': nrm((DEPTH, CONV_WIDTH), 0.02),
        'attn_gq': 1.0 + nrm((DEPTH, HEAD_DIM), 0.02),
        'attn_gk': 1.0 + nrm((DEPTH, HEAD_DIM), 0.02),
        'w_gate': nrm((DEPTH, D, 3 * D), D ** -0.5),
        'b_gate': nrm((DEPTH, 3 * D), 0.02),
        'w_br_ssm': nrm((DEPTH, SSM_WIDTH, D), SSM_WIDTH ** -0.5),
        'w_br_conv': nrm((DEPTH, CONV_WIDTH, D), CONV_WIDTH ** -0.5),
        'w_br_attn': nrm((DEPTH, ATTN_OUT, D), ATTN_OUT ** -0.5),
        'w_out': nrm((DEPTH, D, D), D ** -0.5),
        'g_norm_ffn': 1.0 + nrm((DEPTH, D), 0.02),
        'ffn_w_up': nrm((DEPTH, D, F2), D ** -0.5),
        'ffn_conv_w': nrm((DEPTH, FFN_CONV_K, F2), FFN_CONV_K ** -0.5),
        'ffn_conv_b': nrm((DEPTH, F2), 0.02),
        'ffn_w_down': nrm((DEPTH, FFN_HIDDEN, D), FFN_HIDDEN ** -0.5),
    }


def reference(x_prompt, x_sample, state_ssm, cache_conv, cache_kv_w128, cache_kv_w512, cache_kv_w2048,
              cache_ffn, c_prompt, c_sample, w_ada, b_ada, g_norm_mix, w_in, ssm_a_re, ssm_a_im, ssm_log_dt,
              ssm_b_re, ssm_b_im, ssm_c_re, ssm_c_im, ssm_d, ssm_w_glu, ssm_b_glu, conv_w, conv_b, conv_ln_g,
              conv_ln_b, attn_gq, attn_gk, w_gate, b_gate, w_br_ssm, w_br_conv, w_br_attn, w_out, g_norm_ffn,
              ffn_w_up, ffn_conv_w, ffn_conv_b, ffn_w_down):
    bp, seq_p, _ = x_prompt.shape
    dt = x_prompt.dtype
    pos_p = jnp.arange(seq_p, dtype=jnp.int32)
    pos_s = PAST_LEN + jnp.arange(x_sample.shape[1], dtype=jnp.int32)
    yp, ys = x_prompt, x_sample
    new_p = [[] for _ in range(6)]
    new_s = [[] for _ in range(6)]
    for l in range(DEPTH):
        lp = dict(w_ada=w_ada[l], b_ada=b_ada[l], g_norm_mix=g_norm_mix[l], w_in=w_in[l],
                  ssm_a_re=ssm_a_re[l], ssm_a_im=ssm_a_im[l], ssm_log_dt=ssm_log_dt[l],
                  ssm_b_re=ssm_b_re[l], ssm_b_im=ssm_b_im[l], ssm_c_re=ssm_c_re[l], ssm_c_im=ssm_c_im[l],
                  ssm_d=ssm_d[l], ssm_w_glu=ssm_w_glu[l], ssm_b_glu=ssm_b_glu[l],
                  conv_w=conv_w[l], conv_b=conv_b[l], conv_ln_g=conv_ln_g[l], conv_ln_b=conv_ln_b[l],
                  attn_gq=attn_gq[l], attn_gk=attn_gk[l], w_gate=w_gate[l], b_gate=b_gate[l],
                  w_br_ssm=w_br_ssm[l], w_br_conv=w_br_conv[l], w_br_attn=w_br_attn[l], w_out=w_out[l],
                  g_norm_ffn=g_norm_ffn[l], ffn_w_up=ffn_w_up[l], ffn_conv_w=ffn_conv_w[l],
                  ffn_conv_b=ffn_conv_b[l], ffn_w_down=ffn_w_down[l])
        yp, st_p = trunk_layer(yp, c_prompt, pos_p,
                               jnp.zeros((bp, SSM_GROUPS, SSM_N, 2), F32),
                               jnp.zeros((bp, CONV_K - 1, CONV_WIDTH), dt),
                               None,
                               jnp.zeros((bp, FFN_CONV_K - 1, 2 * FFN_HIDDEN), dt), lp)
        ys, st_s = trunk_layer(ys, c_sample, pos_s, state_ssm[l], cache_conv[l],
                               (cache_kv_w128[l], cache_kv_w512[l], cache_kv_w2048[l]), cache_ffn[l], lp)
        for i in range(6):
            new_p[i].append(st_p[i])
            new_s[i].append(st_s[i])
    ssm_p, conv_p, kv128_p, kv512_p, kv2048_p, ffn_p = [jnp.stack(t, axis=0) for t in new_p]
    ssm_s, conv_s, kv128_s, kv512_s, kv2048_s, ffn_s = [jnp.stack(t, axis=0) for t in new_s]
    return (yp, ys, ssm_p, ssm_s, conv_p, conv_s, kv128_p, kv128_s, kv512_p, kv512_s, kv2048_p, kv2048_s, ffn_p, ffn_s)
```

```python
import contextlib
import numpy as np
import concourse.bass as bass
import concourse.mybir as mybir
from concourse.bass_utils import run_bass_kernel_spmd

F32 = mybir.dt.float32
BF16 = mybir.dt.bfloat16
AF = mybir.ActivationFunctionType
ALU = mybir.AluOpType
AX = mybir.AxisListType

D = 1024
NCH = 8
TB = 512
SSM_W = 512
CONV_W = 512
CONV_K = 31
FFN_H = 2816
F2 = 2 * FFN_H
IN_W = 3840
EPS = 1e-6
WINS = ((128, 1), (512, 4), (2048, 16))
PAST = 8192
SEM_CAP = 24000
DMA_SLOTS = 8
DEBUG = [False]
ATT_GROUPS = [0, 1, 2]
SAMPLE = [True]


class Buf:
    __slots__ = ("name", "w", "r")

    def __init__(self, name):
        self.name = name
        self.w = None
        self.r = {}


class Op:
    __slots__ = ("fn", "waits", "signal", "dma", "sigpos")

    def __init__(self, fn, dma=None):
        self.fn = fn
        self.waits = []
        self.signal = False
        self.dma = dma
        self.sigpos = None


class Prog:
    ENGS = ("pe", "act", "dve", "pool", "sp")

    def __init__(self, nc):
        self.nc = nc
        self.ops = {e: [] for e in self.ENGS}
        self.seen = {e: {} for e in self.ENGS}
        self.ndma = {e: 0 for e in self.ENGS}

    def _need(self, eng, tok, op):
        if tok[0] == "E":
            key = ("E", tok[1])
            if self.seen[eng].get(key, -1) >= tok[2]:
                return
            self.seen[eng][key] = tok[2]
            self.ops[tok[1]][tok[2]].signal = True
        else:
            key = ("D", tok[1], tok[2] % DMA_SLOTS)
            if self.seen[eng].get(key, -1) >= tok[2]:
                return
            self.seen[eng][key] = tok[2]
        op.waits.append(tok)

    def add(self, eng, fn, R=(), W=(), dma=False):
        ops = self.ops[eng]
        idx = len(ops)
        if dma:
            n = self.ndma[eng]
            self.ndma[eng] += 1
            op = Op(fn, dma=n)
            me = ("D", eng, n)
            if n >= DMA_SLOTS:
                self._need(eng, ("D", eng, n - DMA_SLOTS), op)
        else:
            op = Op(fn)
            me = ("E", eng, idx)
        deps = []
        for b in R:
            if b.w is not None:
                deps.append((b.w, True))
        for b in W:
            if b.w is not None:
                deps.append((b.w, False))
            for t in b.r.values():
                deps.append((t, False))
        for t, raw in deps:
            if t[0] == "E" and t[1] == eng and not dma:
                if eng == "pe" or not raw:
                    continue
            if t == me:
                continue
            self._need(eng, t, op)
        ops.append(op)
        for b in W:
            b.w = me
            b.r = {}
        for b in R:
            b.r[(me[0], me[1])] = me
        return me

    def emit(self, stack):
        nc = self.nc
        nsem = {}
        for e in self.ENGS:
            c = 0
            for op in self.ops[e]:
                if op.signal:
                    op.sigpos = c
                    c += 1
            nsem[e] = max(1, -(-c // SEM_CAP))
        sems = {e: [stack.enter_context(nc.semaphore(f"s_{e}_{i}")) for i in range(nsem[e])] for e in self.ENGS}
        dsems = {e: [stack.enter_context(nc.semaphore(f"d_{e}_{i}")) for i in range(DMA_SLOTS)]
                 for e in self.ENGS if self.ndma[e] > 0}
        block = stack.enter_context(nc.Block())
        hooks = {"pe": block.tensor, "act": block.scalar, "dve": block.vector, "pool": block.gpsimd, "sp": block.sync}

        def body(e):
            def run(h):
                for op in self.ops[e]:
                    for t in op.waits:
                        if t[0] == "E":
                            pos = self.ops[t[1]][t[2]].sigpos
                            h.wait_ge(sems[t[1]][pos // SEM_CAP], pos % SEM_CAP + 1)
                        else:
                            h.wait_ge(dsems[t[1]][t[2] % DMA_SLOTS], 16 * (t[2] // DMA_SLOTS + 1))
                    ins = op.fn(h)
                    if op.dma is not None:
                        ins.then_inc(dsems[e][op.dma % DMA_SLOTS], 16)
                    elif op.signal:
                        ins.then_inc(sems[e][op.sigpos // SEM_CAP], 1)
                n = self.ndma[e]
                for s in range(min(n, DMA_SLOTS)):
                    last = ((n - 1 - s) // DMA_SLOTS) * DMA_SLOTS + s
                    h.wait_ge(dsems[e][s], 16 * (last // DMA_SLOTS + 1))
            return run

        for e in self.ENGS:
            hooks[e](body(e))


class Tile:
    def __init__(self, t, nsub=1):
        self.t = t
        self.bufs = [Buf(None) for _ in range(nsub)]

    def b(self, i=0):
        return self.bufs[i]

    def all(self):
        return self.bufs


class K:
    def __init__(self, SEQ, DEPTH, NS):
        self.SEQ, self.DEPTH, self.NS = SEQ, DEPTH, NS
        self.NB = SEQ // TB
        self.nc = bass.Bass("TRN2", target_bir_lowering=False)
        self.P = Prog(self.nc)
        self.stack = contextlib.ExitStack()
        self.bank_rr = 0

    def din(self, name, shape, dt=F32):
        return self.nc.dram_tensor(name, list(shape), dt, kind="ExternalInput").ap()

    def dout(self, name, shape, dt=F32):
        return self.nc.dram_tensor(name, list(shape), dt, kind="ExternalOutput").ap()

    def dscr(self, name, shape, dt, nsub=1):
        return Tile(self.nc.dram_tensor(name, list(shape), dt, kind="Internal").ap(), nsub)

    def sb(self, name, shape, dt, nsub=1):
        return Tile(self.stack.enter_context(self.nc.sbuf_tensor(name, list(shape), dt)), nsub)

    def mm(self, out, lhsT, rhs, start, stop, R, W):
        self.P.add("pe", lambda h: h.matmul(out, lhsT, rhs, start=start, stop=stop, skip_group_check=True), R, W)

    def tr(self, out, in_, ident, R, W):
        self.P.add("pe", lambda h: h.transpose(out, in_, ident), R, W)

    def act(self, out, in_, func, R, W, scale=None, bias=None):
        kw = {}
        if scale is not None:
            kw["scale"] = scale
        if bias is not None:
            kw["bias"] = bias
        self.P.add("act", lambda h: h.activation(out=out, in_=in_, func=func, **kw), R, W)

    def ve(self, eng, fn, R, W):
        self.P.add(eng, fn, R, W)

    def tt(self, eng, out, a, b, op, R, W):
        self.P.add(eng, lambda h: h.tensor_tensor(out=out, in0=a, in1=b, op=op), R, W)

    def ts(self, eng, out, a, s1, op0, R, W, s2=None, op1=None):
        if op1 is None:
            self.P.add(eng, lambda h: h.tensor_scalar(out=out, in0=a, scalar1=s1, scalar2=None, op0=op0), R, W)
        else:
            self.P.add(eng, lambda h: h.tensor_scalar(out=out, in0=a, scalar1=s1, scalar2=s2, op0=op0, op1=op1), R, W)

    def stt(self, out, a, s, b, op0, op1, R, W):
        self.P.add("dve", lambda h: h.scalar_tensor_tensor(out=out, in0=a, scalar=s, in1=b, op0=op0, op1=op1), R, W)

    def cp(self, eng, out, in_, R, W):
        if eng == "act":
            self.P.add("act", lambda h: h.activation(out=out, in_=in_, func=AF.Identity), R, W)
        else:
            self.P.add(eng, lambda h: h.tensor_copy(out=out, in_=in_), R, W)

    def dma(self, q, out, in_, R, W, nc_ok=False):
        nc = self.nc
        if nc_ok:
            def fn(h):
                with nc.allow_non_contiguous_dma(reason="small strided parameter/state transfer"):
                    return h.dma_start(out=out, in_=in_)
        else:
            def fn(h):
                return h.dma_start(out=out, in_=in_)
        self.P.add(q, fn, R, W, dma=True)

    def bank(self):
        i = self.pool_banks[self.bank_rr % len(self.pool_banks)]
        self.bank_rr += 1
        return i


def rr(ap, s, **kw):
    return ap.rearrange(s, **kw)


def build(SEQ, DEPTH, NS, with_sample=True):
    k = K(SEQ, DEPTH, NS)
    nc, P, NB = k.nc, k.P, k.NB
    L = DEPTH
    I = {}
    def inp(name, shape):
        I[name] = Tile(k.din(name, shape))
        return I[name]
    inp("xp", [SEQ, D]); inp("cp", [1, D]); inp("xs", [NS, D]); inp("cs", [NS, D])
    inp("st_ssm", [L, NS, 32, 64, 2]); inp("c_conv", [L, NS, 30, 512])
    for w, _ in WINS:
        inp(f"kv{w}", [L, NS, w, 512])
    inp("c_ffn", [L, NS, 2, F2])
    wshapes = dict(w_ada=[D, 6 * D], b_ada=[6 * D], g_norm_mix=[D], w_in=[D, IN_W], ssm_a_re=[32, 64], ssm_a_im=[32, 64],
                   ssm_log_dt=[32], ssm_b_re=[32, 64, 16], ssm_b_im=[32, 64, 16], ssm_c_re=[32, 16, 64],
                   ssm_c_im=[32, 16, 64], ssm_d=[512], ssm_w_glu=[512, 512], ssm_b_glu=[512], conv_w=[31, 512],
                   conv_b=[512], conv_ln_g=[512], conv_ln_b=[512], attn_gq=[64], attn_gk=[64], w_gate=[D, 3 * D],
                   b_gate=[3 * D], w_br_ssm=[512, D], w_br_conv=[512, D], w_br_attn=[256, D], w_out=[D, D],
                   g_norm_ffn=[D], ffn_w_up=[D, F2], ffn_conv_w=[3, F2], ffn_conv_b=[F2], ffn_w_down=[FFN_H, D])
    for n, s in wshapes.items():
        inp(n, [L] + s)
    inp("c_ident", [128, 128]); inp("c_masks", [5, 128, 128]); inp("c_rope", [NB, 128, 12, 64]); inp("c_rope_s", [1, 64])
    inp("c_iota", [512])
    O = {}
    def outp(name, shape):
        O[name] = Tile(k.dout(name, shape))
        return O[name]
    outp("yp", [SEQ, D]); outp("ys", [NS, D]); outp("ssm_p", [L, 32, 64, 2]); outp("ssm_s", [L, NS, 32, 64, 2])
    outp("conv_p", [L, 30, 512]); outp("conv_s", [L, NS, 30, 512])
    for w, _ in WINS:
        outp(f"kv{w}_p", [L, min(w, SEQ), 512]); outp(f"kv{w}_s", [L, NS, w, 512])
    outp("ffn_p", [L, 2, F2]); outp("ffn_s", [L, NS, 2, F2])
    k.dbg = {}
    if DEBUG[0]:
        k.debug = True
        for nm, shp in (("d_yssm", [128, 4, TB]), ("d_yconv", [128, 4, TB]), ("d_yattn", [64, 4, TB]), ("d_yg", [128, 4, TB]),
                        ("d_merged", [128, 8, TB]), ("d_x", [128, 8, TB]), ("d_acc", [64, 1024])):
            k.dbg[nm] = Tile(k.dout(nm, shp))
    big = ["w_ada", "w_in", "ssm_w_glu", "w_gate", "w_br_ssm", "w_br_conv", "w_br_attn", "w_out", "ffn_w_up", "ffn_w_down"]
    S = {n: k.dscr("s_" + n, [L] + wshapes[n], BF16, nsub=L) for n in big}
    S["cdiag"] = k.dscr("s_cdiag", [L, 4, 128, 32, 128], BF16, nsub=L)
    S["ssmw"] = k.dscr("s_ssmw", [L, 128, 16, 4, 128], BF16, nsub=L)
    S["rot"] = k.dscr("s_rot", [L, 128, 16, 2, TB], BF16, nsub=L)
    xscr = k.dscr("s_x", [2, NB, 128, NCH, TB], F32, nsub=2 * NB)
    k.I, k.O, k.S = I, O, S

    sb = k.sb
    ps = [Tile(k.stack.enter_context(nc.psum_tensor(f"ps{i}", [128, 512], F32))) for i in range(8)]
    k.ps = ps
    k.pool_banks = list(range(8))
    ident = sb("ident", [128, 128], F32)
    identb = sb("identb", [128, 128], BF16)
    onesb = sb("onesb", [128, 128], BF16)
    masks = sb("masks", [128, 5, 128], BF16)
    iota = sb("iota", [128, 512], F32)
    epsc = sb("epsc", [128, 1], F32)
    NWS, WSZ = 6, 2048
    wsl = [sb(f"wsl{i}", [128, WSZ], BF16) for i in range(NWS)]
    wrr = [0]

    def wload(src, shape):
        t = wsl[wrr[0] % NWS]
        wrr[0] += 1
        n = int(np.prod(shape[1:]))
        assert n <= WSZ
        view = t.t[0:shape[0], 0:n]
        if len(shape) == 3:
            view = view.rearrange("p (a b) -> p a b", b=shape[2])
        elif len(shape) == 4:
            view = view.rearrange("p (a b c) -> p a b c", b=shape[2], c=shape[3])
        k.dma("sp", view, src[0], [src[1]], [t.b()], nc_ok=True)
        return view, t.b()

    x = sb("x", [128, NCH, TB], F32, nsub=NCH)
    h = sb("h", [128, NCH, TB], BF16, nsub=NCH)
    sqb = [sb(f"sqb{i}", [128, TB], BF16) for i in range(3)]
    rstd = sb("rstd", [128, TB], F32)
    mean = sb("mean", [128, TB], F32)
    tmpf = [sb(f"tmpf{i}", [128, TB], F32) for i in range(4)]
    tf = [0]

    def tmp():
        t = tmpf[tf[0] % len(tmpf)]
        tf[0] += 1
        return t

    sq_i = [0]

    def sqt():
        t = sqb[sq_i[0] % len(sqb)]
        sq_i[0] += 1
        return t

    uT = sb("uT", [128, 4, TB], BF16, nsub=4)
    aT = sb("aT", [128, 4, 30 + TB], BF16, nsub=4)
    qT = sb("qT", [128, 12, 2, 128], BF16, nsub=12)
    NSL = (8, 8, 20)
    kT = [sb(f"kT{g}", [128, NSL[g], 2, 128], BF16, nsub=NSL[g]) for g in range(3)]
    vR = [sb(f"vR{g}", [128, NSL[g], 256], BF16, nsub=NSL[g]) for g in range(3)]
    ropeT = sb("ropeT", [128, 12, 64], F32)
    gqk = sb("gqk", [128, 2, 64], F32)
    qkn = sb("qkn", [128, 8, 64], F32)
    qkr = sb("qkr", [128, 8, 64], F32)
    vstage = sb("vstage", [128, 256], F32)
    ssq = sb("ssq", [128, 8], F32)
    yssm = sb("yssm", [128, 4, TB], BF16, nsub=4)
    yg = sb("yg", [128, 4, TB], BF16, nsub=4)
    yconv = sb("yconv", [128, 4, TB], BF16, nsub=4)
    ycf = sb("ycf", [128, 4, TB], BF16, nsub=4)
    yattn = sb("yattn", [64, 4, TB], BF16, nsub=4)
    actT = sb("actT", [128, 22, TB], BF16, nsub=22)
    upsb = [sb(f"upsb{i}", [128, 2 + TB], F32) for i in range(2)]
    uphalo = sb("uphalo", [128, 44, 2], F32)
    pT = [sb(f"pT{i}", [128, 5 * 128], BF16) for i in range(2)]
    pTm = [sb(f"pTm{i}", [128, 5 * 128], BF16) for i in range(2)]
    xio = sb("xio", [128, D], F32)
    ddiag = sb("ddiag", [128, 4, 128], BF16)
    sm = {n: sb("sm_" + n, [128, 16], F32) for n in
          ["are", "aim", "dt", "th", "r", "c1", "s1", "c511", "s511", "c512", "s512", "kre", "kim", "t0", "t1", "t2", "t3",
           "rc1", "rs1", "den"]}
    vin = sb("vin", [128, 16, 2], F32)
    vend = sb("vend", [128, 16, 2], F32)
    sstate = sb("sstate", [128, 16, 2], F32)
    v0 = [sb(f"v0_{i}", [128, TB], F32) for i in range(2)]
    vv = [sb(f"vv_{i}", [128, TB], F32) for i in range(2)]
    sre = [sb(f"sre{i}", [128, TB], BF16) for i in range(2)]
    sim = [sb(f"sim{i}", [128, TB], BF16) for i in range(2)]
    pv = {}
    for n, c in [("gmix", 8), ("gffn", 8), ("bada", 48), ("convw", 4 * 31), ("convb", 4), ("lng", 4), ("lnb", 4),
                 ("bglu", 4), ("bgate", 24), ("fcw", 44 * 3), ("fcb", 44), ("dsk", 4), ("gs1", 8), ("gs2", 8)]:
        pv[n] = sb("pv_" + n, [128, c], F32)
    NCOL = 1 + NS
    cT = sb("cT", [128, NCH, NCOL], F32)
    cTb = sb("cTb", [128, NCH, NCOL], BF16)
    modT = sb("modT", [128, 48, NCOL], F32)
    def tview(t, s, **kw):
        v = Tile(t.t[:, :].rearrange(s, **kw), 1)
        v.bufs = t.bufs
        return v
    def aview(c0, s, **kw):
        v = Tile(actT.t[:, c0:c0 + 2, :].rearrange("p a b -> p (a b)").bitcast(F32).rearrange(s, **kw), 1)
        v.bufs = [actT.b(c0)]
        return v
    braw = aview(0, "p (a b c) -> p a b c", a=2, b=16)
    bbar = aview(2, "p (a b c) -> p a b c", a=2, b=16)
    craw = aview(4, "p (a b c) -> p a b c", a=2, b=16)
    zst = aview(6, "p (a b) -> p a b", a=4)
    wst = sb("wst", [128, 4, 128], BF16)
    cdg = sb("cdg", [128, 8, 128], BF16)
    rotst = sb("rotst", [128, 2, TB], BF16)
    xS = sb("xS", [128, NCH, NS], F32)
    hS = sb("hS", [128, NCH, NS], BF16)
    zS = sb("zS", [128, 30, NS], F32)
    sqS = sb("sqS", [128, NCH, NS], BF16)
    rsS = sb("rsS", [128, NS], F32)
    mnS = sb("mnS", [128, NS], F32)
    t8S = [sb(f"t8S{i}", [128, NCH, NS], F32) for i in range(3)]
    uSb = sb("uSb", [128, 4, NS], BF16)
    ygS = sb("ygS", [128, 4, NS], BF16)
    ysS = sb("ysS", [128, 4, NS], BF16)
    ycS = sb("ycS", [128, 4, NS], BF16)
    yaS = sb("yaS", [64, 4, NS], BF16)
    mgS = sb("mgS", [128, NCH, NS], BF16)
    upS = Tile(iota.t[:, 256:256 + 44 * NS].rearrange("p (c s) -> p c s", s=NS), 1)
    upS.bufs = iota.bufs
    ucS = sb("ucS", [128, 44, NS], F32)
    acS = sb("acS", [128, 22, NS], BF16)
    s0S = Tile(iota.t[:, 0:128].rearrange("p (a s r) -> p a s r", a=16, s=NS), 1)
    s1S = Tile(iota.t[:, 128:256].rearrange("p (a s r) -> p a s r", a=16, s=NS), 1)
    s0S.bufs = iota.bufs
    s1S.bufs = iota.bufs
    s1b = sb("s1b", [128, 2, 16, NS], BF16)
    qS = sb("qS", [128, 3, 2, NS], BF16)
    ropeS = sb("ropeS", [NS, 64], F32)
    sself = sb("sself", [NS, 12], F32)
    pdS = sb("pdS", [NS, 12, NS], BF16)
    oacc = sb("oacc", [64, 2, 3, 4], F32)
    convc = Tile(rotst.t[:, :, :].rearrange("p a b -> p (a b)").bitcast(F32)[:, 0:4 * NS * 30].rearrange("p (c s j) -> p c s j", c=4, s=NS), 1)
    convc.bufs = rotst.bufs
    ffnc = Tile(cdg.t[:, :, :].rearrange("p a b -> p (a b)").bitcast(F32)[:, 0:44 * NS * 2].rearrange("p (c s j) -> p c s j", c=44, s=NS), 1)
    ffnc.bufs = cdg.bufs
    vnb = Tile(actT.t[:, 8:10, :].rearrange("p a b -> p (a b)")[0:NS, 0:768], 1)
    vnb.bufs = [actT.b(8)]

    k.dma("sp", ident.t[:], I["c_ident"].t[:, :], [I["c_ident"].b()], [ident.b()])
    k.cp("dve", identb.t[:], ident.t[:], [ident.b()], [identb.b()])
    k.ve("pool", lambda e: e.memset(onesb.t[:], 1.0), [], [onesb.b()])
    k.ve("pool", lambda e: e.memset(epsc.t[:], EPS), [], [epsc.b()])
    k.dma("pool", masks.t[:], rr(I["c_masks"].t, "m p q -> p m q"), [I["c_masks"].b()], [masks.b()], nc_ok=True)
    def cast_layer(l):
        for n in big:
            src = I[n].t[l]
            dst = S[n].t[l]
            rows = wshapes[n][0]
            step = max(1, rows // 4)
            for r0 in range(0, rows, step):
                k.dma("pool", dst[r0:r0 + step], src[r0:r0 + step], [I[n].b()], [S[n].b(l)])
    k.cast_layer = cast_layer
    PI = float(np.pi)
    TWO_PI = float(2 * np.pi)

    def sincos(dst_sin, dst_cos, ang, shape, bufs_r, bw_sin, bw_cos, eng="dve"):
        t_k = tmp(); t_y = tmp()
        n = int(np.prod(shape[1:]))
        ki = t_k.t[:, 0:n].bitcast(mybir.dt.int32)
        kf = t_k.t[:, 0:n]
        y = t_y.t[:, 0:n]
        a2 = ang if len(shape) == 2 else ang.rearrange("p a b -> p (a b)")
        k.ts(eng, y, a2, 1.0 / TWO_PI, ALU.mult, bufs_r, [t_y.b()])
        k.cp(eng, ki, y, [t_y.b()], [t_k.b()])
        k.cp(eng, y, ki, [t_k.b()], [t_y.b()])
        k.stt(kf, y, -TWO_PI, a2, ALU.mult, ALU.add, [t_y.b()] + bufs_r, [t_k.b()])
        for shift, dst, bw in ((0.0, dst_sin, bw_sin), (PI / 2, dst_cos, bw_cos)):
            if dst is None:
                continue
            d2 = dst if len(shape) == 2 else dst.rearrange("p a b -> p (a b)")
            k.ts(eng, y, kf, shift, ALU.add, [t_k.b()], [t_y.b()])
            t_m = tmp(); m = t_m.t[:, 0:n]
            k.ts(eng, m, y, PI, ALU.is_gt, [t_y.b()], [t_m.b()])
            k.stt(y, m, -TWO_PI, y, ALU.mult, ALU.add, [t_m.b(), t_y.b()], [t_y.b()])
            k.ts(eng, m, y, -PI, ALU.is_lt, [t_y.b()], [t_m.b()])
            k.stt(y, m, TWO_PI, y, ALU.mult, ALU.add, [t_m.b(), t_y.b()], [t_y.b()])
            k.act(d2, y, AF.Sin, [t_y.b()], [bw])

    def pvload(name, src_ap, srcbuf, pat, **kw):
        k.dma("sp", pv[name].t[:], src_ap.rearrange(pat, **kw), [srcbuf], [pv[name].b()], nc_ok=True)

    def layer_prep(l):
        k.dma("sp", iota.t[:], I["c_iota"].t.partition_broadcast(128), [I["c_iota"].b()], [iota.b()], nc_ok=True)
        pvload("gmix", I["g_norm_mix"].t[l], I["g_norm_mix"].b(), "(c p) -> p c", p=128)
        pvload("gffn", I["g_norm_ffn"].t[l], I["g_norm_ffn"].b(), "(c p) -> p c", p=128)
        pvload("bada", I["b_ada"].t[l], I["b_ada"].b(), "(c p) -> p c", p=128)
        for c4 in range(4):
            k.dma("sp", pv["convw"].t[:, c4 * 31:(c4 + 1) * 31], I["conv_w"].t[l][:, c4 * 128:(c4 + 1) * 128].rearrange("j p -> p j"),
                  [I["conv_w"].b()], [pv["convw"].b()], nc_ok=True)
        pvload("convb", I["conv_b"].t[l], I["conv_b"].b(), "(c p) -> p c", p=128)
        pvload("lng", I["conv_ln_g"].t[l], I["conv_ln_g"].b(), "(c p) -> p c", p=128)
        pvload("lnb", I["conv_ln_b"].t[l], I["conv_ln_b"].b(), "(c p) -> p c", p=128)
        pvload("bglu", I["ssm_b_glu"].t[l], I["ssm_b_glu"].b(), "(c p) -> p c", p=128)
        pvload("bgate", I["b_gate"].t[l], I["b_gate"].b(), "(c p) -> p c", p=128)
        for j in range(3):
            k.dma("sp", pv["fcw"].t[:].rearrange("p (c j) -> p c j", j=3)[:, :, j], I["ffn_conv_w"].t[l][j].rearrange("(c p) -> p c", p=128),
                  [I["ffn_conv_w"].b()], [pv["fcw"].b()], nc_ok=True)
        pvload("fcb", I["ffn_conv_b"].t[l], I["ffn_conv_b"].b(), "(c p) -> p c", p=128)
        pvload("dsk", I["ssm_d"].t[l], I["ssm_d"].b(), "(c p) -> p c", p=128)
        k.dma("sp", gqk.t[:, 0, :], I["attn_gq"].t[l].partition_broadcast(128), [I["attn_gq"].b()], [gqk.b()], nc_ok=True)
        k.dma("sp", gqk.t[:, 1, :], I["attn_gk"].t[l].partition_broadcast(128), [I["attn_gk"].b()], [gqk.b()], nc_ok=True)
        if l == 0:
            k.dma("sp", cT.t[:, :, 0], I["cp"].t[0].rearrange("(c p) -> p c", p=128), [I["cp"].b()], [cT.b()], nc_ok=True)
            for s in range(NS):
                k.dma("sp", cT.t[:, :, 1 + s], I["cs"].t[s].rearrange("(c p) -> p c", p=128), [I["cs"].b()], [cT.b()], nc_ok=True)
            k.act(cTb.t[:], cT.t[:], AF.Silu, [cT.b()], [cTb.b()])
        for cg in range(24):
            wv, wb = wload((S["w_ada"].t[l][:, cg * 256:(cg + 1) * 256].rearrange("(kc p) n -> p kc n", p=128), S["w_ada"].b(l)),
                           [128, 8, 256])
            for mi in range(2):
                m = cg * 2 + mi
                bk = k.bank()
                for kc in range(8):
                    k.mm(ps[bk].t[:, 0:NCOL], wv[:, kc, mi * 128:(mi + 1) * 128], cTb.t[:, kc, :], kc == 0, kc == 7,
                         [wb, cTb.b()], [ps[bk].b()])
                k.ts("dve", modT.t[:, m, :], ps[bk].t[:, 0:NCOL], pv["bada"].t[:, m:m + 1], ALU.add,
                     [ps[bk].b(), pv["bada"].b()], [modT.b()])
        for nm, g, c0 in (("gs1", "gmix", 8), ("gs2", "gffn", 32)):
            k.stt(pv[nm].t[:], modT.t[:, c0:c0 + 8, 0], 1.0, pv[g].t[:], ALU.add, ALU.mult,
                  [modT.b(), pv[g].b()], [pv[nm].b()])
        for half in range(2):
            hp = slice(64 * half, 64 * half + 64)
            for nm, src in (("are", "ssm_a_re"), ("aim", "ssm_a_im")):
                k.dma("sp", sm[nm].t[hp, :], I[src].t[l].rearrange("(p g) n -> g n p", g=2)[half], [I[src].b()], [sm[nm].b()],
                      nc_ok=True)
            k.dma("sp", sm["dt"].t[hp, :], I["ssm_log_dt"].t[l].rearrange("(p g) -> g p", g=2)[half].partition_broadcast(64),
                  [I["ssm_log_dt"].b()], [sm["dt"].b()], nc_ok=True)
            for nm, src, tl in (("b", "ssm_b_re", braw), ("b", "ssm_b_im", braw), ("c", "ssm_c_re", craw), ("c", "ssm_c_im", craw)):
                ri = 0 if src.endswith("re") else 1
                if nm == "b":
                    sap = I[src].t[l].rearrange("(p g) n c -> g n p c", g=2)[half]
                    k.dma("sp", tl.t[hp, ri, :, :], sap, [I[src].b()], [tl.b()], nc_ok=True)
                else:
                    for p in range(16):
                        sap = I[src].t[l][2 * p + half].rearrange("c n -> n c")
                        k.dma("sp", tl.t[hp, ri, p, :], sap, [I[src].b()], [tl.b()], nc_ok=True)
        A = lambda n: sm[n].t[:]
        B_ = lambda n: sm[n].b()
        k.act(A("dt"), A("dt"), AF.Exp, [B_("dt")], [B_("dt")])
        k.tt("dve", A("th"), A("aim"), A("dt"), ALU.mult, [B_("aim"), B_("dt")], [B_("th")])
        k.tt("dve", A("t0"), A("are"), A("dt"), ALU.mult, [B_("are"), B_("dt")], [B_("t0")])
        k.act(A("r"), A("t0"), AF.Exp, [B_("t0")], [B_("r")])
        sincos(A("s1"), A("c1"), A("th"), [128, 16], [B_("th")], B_("s1"), B_("c1"))
        for mult_, sn, cn in ((511.0, "s511", "c511"), (512.0, "s512", "c512")):
            k.ts("dve", A("t1"), A("th"), mult_, ALU.mult, [B_("th")], [B_("t1")])
            sincos(A(sn), A(cn), A("t1"), [128, 16], [B_("t1")], B_(sn), B_(cn))
        k.tt("dve", A("rc1"), A("r"), A("c1"), ALU.mult, [B_("r"), B_("c1")], [B_("rc1")])
        k.tt("dve", A("rs1"), A("r"), A("s1"), ALU.mult, [B_("r"), B_("s1")], [B_("rs1")])
        k.ts("dve", A("t0"), A("rc1"), -1.0, ALU.add, [B_("rc1")], [B_("t0")])
        k.tt("dve", A("t1"), A("are"), A("are"), ALU.mult, [B_("are")], [B_("t1")])
        k.tt("dve", A("t2"), A("aim"), A("aim"), ALU.mult, [B_("aim")], [B_("t2")])
        k.tt("dve", A("den"), A("t1"), A("t2"), ALU.add, [B_("t1"), B_("t2")], [B_("den")])
        k.ve("dve", lambda e: e.reciprocal(out=A("den"), in_=A("den")), [B_("den")], [B_("den")])
        k.tt("dve", A("t1"), A("t0"), A("are"), ALU.mult, [B_("t0"), B_("are")], [B_("t1")])
        k.tt("dve", A("t2"), A("rs1"), A("aim"), ALU.mult, [B_("rs1"), B_("aim")], [B_("t2")])
        k.tt("dve", A("kre"), A("t1"), A("t2"), ALU.add, [B_("t1"), B_("t2")], [B_("kre")])
        k.tt("dve", A("kre"), A("kre"), A("den"), ALU.mult, [B_("kre"), B_("den")], [B_("kre")])
        k.tt("dve", A("t1"), A("rs1"), A("are"), ALU.mult, [B_("rs1"), B_("are")], [B_("t1")])
        k.tt("dve", A("t2"), A("t0"), A("aim"), ALU.mult, [B_("t0"), B_("aim")], [B_("t2")])
        k.tt("dve", A("kim"), A("t1"), A("t2"), ALU.subtract, [B_("t1"), B_("t2")], [B_("kim")])
        k.tt("dve", A("kim"), A("kim"), A("den"), ALU.mult, [B_("kim"), B_("den")], [B_("kim")])
        kre_b = sm["kre"].t[:, :].unsqueeze(2).to_broadcast([128, 16, 16])
        kim_b = sm["kim"].t[:, :].unsqueeze(2).to_broadcast([128, 16, 16])
        t_a = tmp(); ta = t_a.t[:, 0:256].rearrange("p (a b) -> p a b", b=16)
        k.tt("dve", bbar.t[:, 0], braw.t[:, 0], kre_b, ALU.mult, [braw.b(), B_("kre")], [bbar.b()])
        k.tt("dve", ta, braw.t[:, 1], kim_b, ALU.mult, [braw.b(), B_("kim")], [t_a.b()])
        k.tt("dve", bbar.t[:, 0], bbar.t[:, 0], ta, ALU.subtract, [bbar.b(), t_a.b()], [bbar.b()])
        k.tt("dve", bbar.t[:, 1], braw.t[:, 1], kre_b, ALU.mult, [braw.b(), B_("kre")], [bbar.b()])
        k.tt("dve", ta, braw.t[:, 0], kim_b, ALU.mult, [braw.b(), B_("kim")], [t_a.b()])
        k.tt("dve", bbar.t[:, 1], bbar.t[:, 1], ta, ALU.add, [bbar.b(), t_a.b()], [bbar.b()])
        k.ts("dve", craw.t[:, 1], craw.t[:, 1], -1.0, ALU.mult, [craw.b()], [craw.b()])
        for p in range(16):
            pl = p % 4
            k.ve("pool", lambda e: e.memset(zst.t[:], 0.0), [], [zst.b()])
            for half in range(2):
                hp = slice(64 * half, 64 * half + 64)
                c0 = 32 * pl + 16 * half
                for q, srcT in ((0, bbar.t[hp, 0, p, :]), (1, bbar.t[hp, 1, p, :]), (2, craw.t[hp, 0, p, :]), (3, craw.t[hp, 1, p, :])):
                    k.cp("pool", zst.t[hp, q, c0:c0 + 16], srcT, [bbar.b(), craw.b()], [zst.b()])
            bk = k.bank()
            for q in range(2):
                k.tr(ps[bk].t[:, q * 128:(q + 1) * 128], zst.t[:, q, :], ident.t[:], [zst.b(), ident.b()], [ps[bk].b()])
            k.cp("dve", wst.t[:, 0:2, :], ps[bk].t[:, 0:256].rearrange("p (a b) -> p a b", b=128), [ps[bk].b()], [wst.b()])
            k.cp("dve", wst.t[:, 2:4, :], zst.t[:, 2:4, :], [zst.b()], [wst.b()])
            k.dma("sp", S["ssmw"].t[l][:, p], wst.t[:], [wst.b()], [S["ssmw"].b(l)], nc_ok=True)
            t_g = tmp()
            k.ts("dve", t_g.t[:], iota.t[:], sm["th"].t[:, p:p + 1], ALU.mult, [iota.b(), B_("th")], [t_g.b()])
            sincos(rotst.t[:, 1, :], rotst.t[:, 0, :], t_g.t[:], [128, 512], [t_g.b()], rotst.b(), rotst.b())
            k.dma("sp", S["rot"].t[l][:, p], rotst.t[:], [rotst.b()], [S["rot"].b(l)], nc_ok=True)
        for c4 in range(4):
            k.ts("dve", ddiag.t[:, c4, :], identb.t[:], pv["dsk"].t[:, c4:c4 + 1], ALU.mult, [identb.b(), pv["dsk"].b()], [ddiag.b()])
            for j0 in range(0, 32, 8):
                nj = min(8, CONV_K - j0)
                if nj <= 0:
                    continue
                for j in range(nj):
                    k.ts("pool", cdg.t[:, j, :], identb.t[:], pv["convw"].t[:, c4 * 31 + j0 + j:c4 * 31 + j0 + j + 1], ALU.mult,
                         [identb.b(), pv["convw"].b()], [cdg.b()])
                k.dma("sp", S["cdiag"].t[l][c4][:, j0:j0 + nj, :], cdg.t[:, 0:nj, :], [cdg.b()], [S["cdiag"].b(l)], nc_ok=True)
    k.layer_prep = layer_prep
    MUL, ADD, SUB = ALU.mult, ALU.add, ALU.subtract

    def lw(name, l, col0, ncol, nk=8, row0=0):
        Wt = S[name]
        return wload((Wt.t[l][row0:row0 + nk * 128, col0:col0 + ncol].rearrange("(kc p) n -> p kc n", p=128), Wt.b(l)), [128, nk, ncol])

    def load_x(l, b):
        if l == 0:
            for t4 in range(4):
                k.dma("sp", xio.t[:], I["xp"].t[b * TB + t4 * 128:b * TB + (t4 + 1) * 128, :], [I["xp"].b()], [xio.b()])
                for g2 in range(2):
                    bk = k.bank()
                    for q in range(4):
                        kc = g2 * 4 + q
                        k.tr(ps[bk].t[:, q * 128:(q + 1) * 128], xio.t[:, kc * 128:(kc + 1) * 128], ident.t[:], [xio.b(), ident.b()], [ps[bk].b()])
                    k.cp("act", x.t[:, g2 * 4:(g2 + 1) * 4, t4 * 128:(t4 + 1) * 128], ps[bk].t[:, :].rearrange("p (a b) -> p a b", b=128),
                         [ps[bk].b()], [x.b(i) for i in range(g2 * 4, g2 * 4 + 4)])
        else:
            k.dma("sp", x.t[:], xscr.t[(l - 1) % 2, b], [xscr.b(((l - 1) % 2) * NB + b)], x.all())

    def store_x(l, b):
        if l < L - 1:
            k.dma("sp", xscr.t[l % 2, b], x.t[:], x.all(), [xscr.b((l % 2) * NB + b)])
        else:
            for t4 in range(4):
                for g2 in range(2):
                    bk = k.bank()
                    for q in range(4):
                        kc = g2 * 4 + q
                        k.tr(ps[bk].t[:, q * 128:(q + 1) * 128], x.t[:, kc, t4 * 128:(t4 + 1) * 128], ident.t[:], [x.b(kc), ident.b()], [ps[bk].b()])
                    k.cp("act", xio.t[:, g2 * 512:(g2 + 1) * 512], ps[bk].t[:], [ps[bk].b()], [xio.b()])
                k.dma("sp", O["yp"].t[b * TB + t4 * 128:b * TB + (t4 + 1) * 128, :], xio.t[:], [xio.b()], [O["yp"].b()])

    def rsqrt_inplace(t_ap, tb, scale):
        k.act(t_ap, t_ap, AF.Sqrt, [tb, epsc.b()], [tb], scale=scale, bias=epsc.t[0:t_ap.shape[0], 0:1])
        k.ve("dve", lambda e: e.reciprocal(out=t_ap, in_=t_ap), [tb], [tb])

    def norm_mod(gs, shc0):
        bk = k.bank()
        for kc in range(8):
            s = sqt()
            k.act(s.t[:], x.t[:, kc, :], AF.Square, [x.b(kc)], [s.b()])
            k.mm(ps[bk].t[:], onesb.t[:], s.t[:], kc == 0, kc == 7, [onesb.b(), s.b()], [ps[bk].b()])
        k.act(rstd.t[:], ps[bk].t[:], AF.Sqrt, [ps[bk].b(), epsc.b()], [rstd.b()], scale=1.0 / D, bias=epsc.t[:, 0:1])
        k.ve("dve", lambda e: e.reciprocal(out=rstd.t[:], in_=rstd.t[:]), [rstd.b()], [rstd.b()])
        for kc in range(8):
            t = tmp()
            k.tt("dve", t.t[:], x.t[:, kc, :], rstd.t[:], MUL, [x.b(kc), rstd.b()], [t.b()])
            k.act(h.t[:, kc, :], t.t[:], AF.Identity, [t.b(), gs.b(), modT.b()], [h.b(kc)], scale=gs.t[:, kc:kc + 1],
                  bias=modT.t[:, shc0 + kc, 0:1])

    def fm(wv, wb, ci, rhsT, nk=8):
        bk = k.bank()
        for kc in range(nk):
            k.mm(ps[bk].t[:], wv[:, kc, ci * 128:(ci + 1) * 128], rhsT.t[:, kc, :], kc == 0, kc == nk - 1, [wb, rhsT.b(kc)], [ps[bk].b()])
        return bk

    def proj_a(l, b):
        if b == 0:
            k.ve("pool", lambda e: e.memset(aT.t[:, :, 0:30], 0.0), [], aT.all())
        for q in range(2):
            wv, wb = lw("w_in", l, 256 * q, 256)
            for i in range(2):
                bk = fm(wv, wb, i, h)
                k.cp("act", uT.t[:, 2 * q + i, :], ps[bk].t[:], [ps[bk].b()], [uT.b(2 * q + i)])
        for q in range(2):
            gv, gb = lw("w_in", l, 1024 + 256 * q, 256)
            av, ab = lw("w_in", l, 512 + 256 * q, 256)
            for i in range(2):
                m = 2 * q + i
                bg = fm(gv, gb, i, h)
                t = tmp()
                k.act(t.t[:], ps[bg].t[:], AF.Sigmoid, [ps[bg].b()], [t.b()])
                ba = fm(av, ab, i, h)
                k.tt("dve", aT.t[:, m, 30:30 + TB], ps[ba].t[:], t.t[:], MUL, [ps[ba].b(), t.b()], [aT.b(m)])

    def cols(gi, s, ap2):
        if gi == 0:
            return ap2[:, 128 * s:128 * (s + 1)]
        return ap2[:, s::4]

    def slot(gi, b, s):
        return (4 * b + s) % 8 if gi == 0 else ((b % 2) * 4 + s if gi == 1 else (b % 5) * 4 + s)

    def qk_norm_rope(NP, bq, cos2, sin2, ropeb):
        pp = slice(0, NP)
        t = tmp()
        k.act(t.t[pp, :], ps[bq].t[pp, :], AF.Square, [ps[bq].b()], [t.b()])
        k.ve("dve", lambda e, t=t: e.tensor_reduce(out=ssq.t[pp, :], in_=t.t[pp, :].rearrange("p (a b) -> p a b", b=64), axis=AX.X, op=ADD),
             [t.b()], [ssq.b()])
        rsqrt_inplace(ssq.t[pp, :], ssq.b(), 1.0 / 64)
        k.tt("dve", qkn.t[pp], ps[bq].t[pp, :].rearrange("p (a b) -> p a b", b=64), ssq.t[pp, :].unsqueeze(2).to_broadcast([NP, 8, 64]), MUL,
             [ps[bq].b(), ssq.b()], [qkn.b()])
        for j in range(2):
            k.tt("dve", qkn.t[pp, 4 * j:4 * j + 4, :], qkn.t[pp, 4 * j:4 * j + 4, :], gqk.t[pp, j:j + 1, :].to_broadcast([NP, 4, 64]), MUL,
                 [qkn.b(), gqk.b()], [qkn.b()])
        cs = cos2.unsqueeze(1).to_broadcast([NP, 8, 32])
        sn = sin2.unsqueeze(1).to_broadcast([NP, 8, 32])
        x1 = qkn.t[pp, :, 0:32]
        x2 = qkn.t[pp, :, 32:64]
        ta, tb_ = tmp(), tmp()
        a3 = ta.t[pp, 0:256].rearrange("p (a b) -> p a b", b=32)
        b3 = tb_.t[pp, 0:256].rearrange("p (a b) -> p a b", b=32)
        RR = [qkn.b(), ropeb]
        k.tt("dve", a3, x1, cs, MUL, RR, [ta.b()])
        k.tt("dve", b3, x2, sn, MUL, RR, [tb_.b()])
        k.tt("dve", qkr.t[pp, :, 0:32], a3, b3, SUB, [ta.b(), tb_.b()], [qkr.b()])
        k.tt("dve", a3, x1, sn, MUL, RR, [ta.b()])
        k.tt("dve", b3, x2, cs, MUL, RR, [tb_.b()])
        k.tt("dve", qkr.t[pp, :, 32:64], a3, b3, ADD, [ta.b(), tb_.b()], [qkr.b()])

    def qkv(l, b):
        k.dma("sp", ropeT.t[:], I["c_rope"].t[b], [I["c_rope"].b()], [ropeT.b()])
        for gi in range(3):
            W_, _d = WINS[gi]
            keep = min(W_, SEQ)
            base = SEQ - keep
            wq = lw("w_in", l, 1536 + 256 * gi, 256)
            wk = lw("w_in", l, 2304 + 256 * gi, 256)
            wv3 = lw("w_in", l, 3072 + 256 * gi, 256)
            for s in range(4):
                bq = k.bank()
                bv = k.bank()
                for (wv_, wb_), bank_, c0 in ((wq, bq, 0), (wk, bq, 256), (wv3, bv, 0)):
                    for kc in range(8):
                        k.mm(ps[bank_].t[:, c0:c0 + 256], cols(gi, s, h.t[:, kc, :]), wv_[:, kc, :], kc == 0, kc == 7,
                             [h.b(kc), wb_], [ps[bank_].b()])
                sl = slot(gi, b, s)
                qk_norm_rope(128, bq, ropeT.t[:, gi * 4 + s, 0:32], ropeT.t[:, gi * 4 + s, 32:64], ropeT.b())
                bt = k.bank()
                for j in range(4):
                    k.tr(ps[bt].t[:, j * 128:(j + 1) * 128], qkr.t[:, 2 * j:2 * j + 2, :].rearrange("p a b -> p (a b)"), ident.t[:],
                         [qkr.b(), ident.b()], [ps[bt].b()])
                k.cp("act", qT.t[:, gi * 4 + s, :, :], ps[bt].t[:, 0:256].rearrange("p (a b) -> p a b", b=128), [ps[bt].b()], [qT.b(gi * 4 + s)])
                k.cp("act", kT[gi].t[:, sl, :, :], ps[bt].t[:, 256:512].rearrange("p (a b) -> p a b", b=128), [ps[bt].b()], [kT[gi].b(sl)])
                k.cp("act", vR[gi].t[:, sl, :], ps[bv].t[:, 0:256], [ps[bv].b()], [vR[gi].b(sl)])
                t0 = b * TB
                tail = (t0 + 128 * s >= base) if gi == 0 else (t0 >= base)
                if tail:
                    dst = O[f"kv{W_}_p"].t[l]
                    r0 = t0 - base
                    k.cp("act", vstage.t[:], ps[bv].t[:, 0:256], [ps[bv].b()], [vstage.b()])
                    for src_ap, srcb, c0 in ((qkr.t[:, 4:8, :].rearrange("p a b -> p (a b)"), qkr.b(), 0), (vstage.t[:], vstage.b(), 256)):
                        if gi == 0:
                            k.dma("sp", dst[r0 + 128 * s:r0 + 128 * s + 128, c0:c0 + 256], src_ap, [srcb], [O[f"kv{W_}_p"].b()], nc_ok=True)
                        else:
                            k.dma("sp", dst[r0 + s:r0 + 512:4, c0:c0 + 256], src_ap, [srcb], [O[f"kv{W_}_p"].b()], nc_ok=True)

    pi_ = [0]

    def attention(b):
        acc = xio.t[0:64, :].rearrange("p (a b) -> p a b", a=2)
        for hh in range(4):
            hp = slice(64 * (hh % 2), 64 * (hh % 2) + 64)
            pr = hh // 2
            for gi in ATT_GROUPS:
                for s in range(4):
                    if gi == 0:
                        S_ = 4 * b + s
                        kts = ([((S_ - 1) % 8, 1)] if S_ >= 1 else []) + [(S_ % 8, 0)]
                    elif gi == 1:
                        kts = ([(((b - 1) % 2) * 4 + s, 1)] if b >= 1 else []) + [((b % 2) * 4 + s, 0)]
                    else:
                        kts = [(((b - j) % 5) * 4 + s, 4 if j == 4 else 3) for j in (4, 3, 2, 1) if b - j >= 0] + [((b % 5) * 4 + s, 2)]
                    n = len(kts)
                    pt = pT[pi_[0] % 2]
                    ptm = pTm[pi_[0] % 2]
                    pi_[0] += 1
                    banks = [k.bank()] + ([k.bank()] if n > 4 else [])
                    for i, (sl, mk) in enumerate(kts):
                        bk = banks[i // 4]
                        c = (i % 4) * 128
                        k.mm(ps[bk].t[:, c:c + 128], kT[gi].t[hp, sl, pr, :], qT.t[hp, gi * 4 + s, pr, :], True, True,
                             [kT[gi].b(sl), qT.b(gi * 4 + s)], [ps[bk].b()])
                    for j, bk in enumerate(banks):
                        w = min(n - 4 * j, 4) * 128
                        k.act(pt.t[:, 512 * j:512 * j + w], ps[bk].t[:, 0:w], AF.Exp, [ps[bk].b()], [pt.b()], scale=0.125)
                    for i, (sl, mk) in enumerate(kts):
                        k.tt("pool", ptm.t[:, i * 128:(i + 1) * 128], pt.t[:, i * 128:(i + 1) * 128], masks.t[:, mk, :], MUL,
                             [pt.b(), masks.b()], [ptm.b()])
                    ob = k.bank()
                    for i, (sl, mk) in enumerate(kts):
                        k.mm(ps[ob].t[0:64, 0:128], vR[gi].t[:, sl, hh * 64:(hh + 1) * 64], ptm.t[:, i * 128:(i + 1) * 128], i == 0, i == n - 1,
                             [vR[gi].b(sl), ptm.b()], [ps[ob].b()])
                    for i, (sl, mk) in enumerate(kts):
                        k.mm(ps[ob].t[0:64, 128:256], onesb.t[:, 0:64], ptm.t[:, i * 128:(i + 1) * 128], i == 0, i == n - 1,
                             [onesb.b(), ptm.b()], [ps[ob].b()])
                    src_ = ps[ob].t[0:64, 0:256].rearrange("p (a b) -> p a b", a=2)
                    dst_ = acc[:, :, 128 * s:128 * (s + 1)] if gi == 0 else acc[:, :, s::4]
                    if gi == ATT_GROUPS[0]:
                        k.cp("dve", dst_, src_, [ps[ob].b()], [xio.b()])
                    else:
                        k.tt("dve", dst_, dst_, src_, ADD, [ps[ob].b(), xio.b()], [xio.b()])
            t = tmp()
            k.ve("dve", lambda e, t=t: e.reciprocal(out=t.t[0:64, :], in_=acc[:, 1, :]), [xio.b()], [t.b()])
            k.tt("dve", yattn.t[:, hh, :], acc[:, 0, :], t.t[0:64, :], MUL, [xio.b(), t.b()], [yattn.b(hh)])

    def ssm(l, b, last):
        k.pool_banks = list(range(7))
        yb = 7
        rB = sm["r"].b()
        for c4 in range(4):
            wv, wb = wload((S["ssmw"].t[l][:, 4 * c4:4 * c4 + 4], S["ssmw"].b(l)), [128, 4, 4, 128])
            for pl in range(4):
                p = 4 * c4 + pl
                rv, rb = wload((S["rot"].t[l][:, p], S["rot"].b(l)), [128, 2, TB])
                cosv, sinv = rv[:, 0, :], rv[:, 1, :]
                bre, bim = k.bank(), k.bank()
                k.mm(ps[bre].t[:], wv[:, pl, 0, :], uT.t[:, c4, :], True, True, [wb, uT.b(c4)], [ps[bre].b()])
                k.mm(ps[bim].t[:], wv[:, pl, 1, :], uT.t[:, c4, :], True, True, [wb, uT.b(c4)], [ps[bim].b()])
                t1, t2 = tmp(), tmp()
                k.tt("dve", t1.t[:], ps[bre].t[:], cosv, MUL, [ps[bre].b(), rb], [t1.b()])
                k.tt("dve", t2.t[:], ps[bim].t[:], sinv, MUL, [ps[bim].b(), rb], [t2.b()])
                k.tt("pool", v0[0].t[:], t1.t[:], t2.t[:], ADD, [t1.b(), t2.b()], [v0[0].b()])
                t3, t4 = tmp(), tmp()
                k.tt("dve", t3.t[:], ps[bim].t[:], cosv, MUL, [ps[bim].b(), rb], [t3.b()])
                k.tt("dve", t4.t[:], ps[bre].t[:], sinv, MUL, [ps[bre].b(), rb], [t4.b()])
                k.tt("pool", v0[1].t[:], t3.t[:], t4.t[:], SUB, [t3.b(), t4.b()], [v0[1].b()])
                for ri in range(2):
                    init = 0.0 if b == 0 else vin.t[:, p, ri:ri + 1]
                    k.ve("dve", lambda e, ri=ri, p=p, init=init: e.tensor_tensor_scan(
                        out=vv[ri].t[:], data0=sm["r"].t[:, p:p + 1].to_broadcast([128, TB]), data1=v0[ri].t[:], initial=init,
                        op0=MUL, op1=ADD), [v0[ri].b(), rB, vin.b()], [vv[ri].b()])
                    k.cp("pool", vend.t[:, p, ri:ri + 1], vv[ri].t[:, TB - 1:TB], [vv[ri].b()], [vend.b()])
                t5, t6 = tmp(), tmp()
                k.tt("pool", t5.t[:], vv[0].t[:], cosv, MUL, [vv[0].b(), rb], [t5.b()])
                k.tt("pool", t6.t[:], vv[1].t[:], sinv, MUL, [vv[1].b(), rb], [t6.b()])
                k.tt("pool", sre[p % 2].t[:], t5.t[:], t6.t[:], SUB, [t5.b(), t6.b()], [sre[p % 2].b()])
                t7, t8 = tmp(), tmp()
                k.tt("pool", t7.t[:], vv[1].t[:], cosv, MUL, [vv[1].b(), rb], [t7.b()])
                k.tt("pool", t8.t[:], vv[0].t[:], sinv, MUL, [vv[0].b(), rb], [t8.b()])
                k.tt("pool", sim[p % 2].t[:], t7.t[:], t8.t[:], ADD, [t7.b(), t8.b()], [sim[p % 2].b()])
                k.mm(ps[yb].t[:], wv[:, pl, 2, :], sre[p % 2].t[:], pl == 0, False, [wb, sre[p % 2].b()], [ps[yb].b()])
                k.mm(ps[yb].t[:], wv[:, pl, 3, :], sim[p % 2].t[:], False, False, [wb, sim[p % 2].b()], [ps[yb].b()])
            k.mm(ps[yb].t[:], ddiag.t[:, c4, :], uT.t[:, c4, :], False, True, [ddiag.b(), uT.b(c4)], [ps[yb].b()])
            k.act(yg.t[:, c4, :], ps[yb].t[:], AF.Gelu_apprx_tanh, [ps[yb].b()], [yg.b(c4)])
        k.pool_banks = list(range(8))
        cn, sn_ = ("c511", "s511") if last else ("c512", "s512")
        dstT = sstate if last else vin
        vr, vi = vend.t[:, :, 0], vend.t[:, :, 1]
        A = lambda n: sm[n].t[:]
        k.tt("dve", A("t0"), vr, A(cn), MUL, [vend.b(), sm[cn].b()], [sm["t0"].b()])
        k.tt("dve", A("t1"), vi, A(sn_), MUL, [vend.b(), sm[sn_].b()], [sm["t1"].b()])
        k.tt("dve", dstT.t[:, :, 0], A("t0"), A("t1"), SUB, [sm["t0"].b(), sm["t1"].b()], [dstT.b()])
        k.tt("dve", A("t2"), vi, A(cn), MUL, [vend.b(), sm[cn].b()], [sm["t2"].b()])
        k.tt("dve", A("t3"), vr, A(sn_), MUL, [vend.b(), sm[sn_].b()], [sm["t3"].b()])
        k.tt("dve", dstT.t[:, :, 1], A("t2"), A("t3"), ADD, [sm["t2"].b(), sm["t3"].b()], [dstT.b()])
        if last:
            for half in range(2):
                hp = slice(64 * half, 64 * half + 64)
                k.dma("sp", O["ssm_p"].t[l].rearrange("(p g) n r -> g n p r", g=2)[half], sstate.t[hp, :, :], [sstate.b()], [O["ssm_p"].b()],
                      nc_ok=True)
        gv, gb = lw("ssm_w_glu", l, 0, 512, nk=4)
        for m in range(4):
            bk = fm(gv, gb, m, yg, nk=4)
            t = tmp()
            k.act(t.t[:], ps[bk].t[:], AF.Sigmoid, [ps[bk].b(), pv["bglu"].b()], [t.b()], bias=pv["bglu"].t[:, m:m + 1])
            k.tt("dve", yssm.t[:, m, :], yg.t[:, m, :], t.t[:], MUL, [yg.b(m), t.b()], [yssm.b(m)])

    def convbr(l, b, last):
        for c4 in range(4):
            bk = k.bank()
            for j0, nj in ((0, 16), (16, 15)):
                wv, wb = wload((S["cdiag"].t[l][c4][:, j0:j0 + nj, :], S["cdiag"].b(l)), [128, nj, 128])
                for j in range(nj):
                    jj = j0 + j
                    k.mm(ps[bk].t[:], wv[:, j, :], aT.t[:, c4, jj:jj + TB], jj == 0, jj == 30, [wb, aT.b(c4)], [ps[bk].b()])
            k.act(ycf.t[:, c4, :], ps[bk].t[:], AF.Identity, [ps[bk].b(), pv["convb"].b()], [ycf.b(c4)], bias=pv["convb"].t[:, c4:c4 + 1])
        if last:
            for c4 in range(4):
                k.dma("pool", O["conv_p"].t[l][:, c4 * 128:(c4 + 1) * 128].rearrange("j p -> p j"), aT.t[:, c4, TB:TB + 30], [aT.b(c4)],
                      [O["conv_p"].b()], nc_ok=True)
        else:
            k.cp("pool", aT.t[:, :, 0:30], aT.t[:, :, TB:TB + 30], aT.all(), aT.all())
        bm, bq2 = k.bank(), k.bank()
        for c4 in range(4):
            k.mm(ps[bm].t[:], onesb.t[:], ycf.t[:, c4, :], c4 == 0, c4 == 3, [onesb.b(), ycf.b(c4)], [ps[bm].b()])
        for c4 in range(4):
            s = sqt()
            k.act(s.t[:], ycf.t[:, c4, :], AF.Square, [ycf.b(c4)], [s.b()])
            k.mm(ps[bq2].t[:], onesb.t[:], s.t[:], c4 == 0, c4 == 3, [onesb.b(), s.b()], [ps[bq2].b()])
        k.ts("dve", mean.t[:], ps[bm].t[:], 1.0 / 512, MUL, [ps[bm].b()], [mean.b()])
        t = tmp()
        k.tt("dve", t.t[:], mean.t[:], mean.t[:], MUL, [mean.b()], [t.b()])
        k.stt(rstd.t[:], ps[bq2].t[:], 1.0 / 512, t.t[:], MUL, SUB, [ps[bq2].b(), t.b()], [rstd.b()])
        rsqrt_inplace(rstd.t[:], rstd.b(), 1.0)
        for c4 in range(4):
            t = tmp()
            k.tt("dve", t.t[:], ycf.t[:, c4, :], mean.t[:], SUB, [ycf.b(c4), mean.b()], [t.b()])
            k.tt("dve", t.t[:], t.t[:], rstd.t[:], MUL, [t.b(), rstd.b()], [t.b()])
            k.act(yconv.t[:, c4, :], t.t[:], AF.Silu, [t.b(), pv["lng"].b(), pv["lnb"].b()], [yconv.b(c4)],
                  scale=pv["lng"].t[:, c4:c4 + 1], bias=pv["lnb"].t[:, c4:c4 + 1])

    def merge(l):
        for mg in range(2):
            for mi in range(4):
                m = mg * 4 + mi
                sv, sbf = lw("w_br_ssm", l, m * 128, 128, nk=4)
                cv, cbf = lw("w_br_conv", l, m * 128, 128, nk=4)
                av, abf = wload((S["w_br_attn"].t[l][:, m * 128:(m + 1) * 128].rearrange("(hh p) n -> p hh n", p=64), S["w_br_attn"].b(l)),
                                [64, 4, 128])
                gts = [lw("w_gate", l, j * 1024 + m * 128, 128) for j in range(3)]
                b1 = fm(sv, sbf, 0, yssm, nk=4)
                b2 = fm(cv, cbf, 0, yconv, nk=4)
                b3 = k.bank()
                for hh in range(4):
                    k.mm(ps[b3].t[:], av[:, hh, :], yattn.t[:, hh, :], hh == 0, hh == 3, [abf, yattn.b(hh)], [ps[b3].b()])
                for j, bb in enumerate((b1, b2, b3)):
                    bg = fm(gts[j][0], gts[j][1], 0, h)
                    sg = tmp()
                    k.act(sg.t[:], ps[bg].t[:], AF.Sigmoid, [ps[bg].b(), pv["bgate"].b()], [sg.b()], bias=pv["bgate"].t[:, 8 * j + m:8 * j + m + 1])
                    if j == 0:
                        k.tt("dve", mean.t[:], ps[bb].t[:], sg.t[:], MUL, [ps[bb].b(), sg.b()], [mean.b()])
                    else:
                        k.tt("dve", rstd.t[:], ps[bb].t[:], sg.t[:], MUL, [ps[bb].b(), sg.b()], [rstd.b()])
                        if j == 1:
                            k.tt("pool", mean.t[:], mean.t[:], rstd.t[:], ADD, [mean.b(), rstd.b()], [mean.b()])
                        else:
                            k.tt("pool", actT.t[:, m, :], mean.t[:], rstd.t[:], ADD, [mean.b(), rstd.b()], [actT.b(m)])
        for mg in range(4):
            wv, wb = lw("w_out", l, mg * 256, 256)
            for mi in range(2):
                m = 2 * mg + mi
                bk = fm(wv, wb, mi, actT)
                k.stt(x.t[:, m, :], ps[bk].t[:], modT.t[:, 16 + m, 0:1], x.t[:, m, :], MUL, ADD, [ps[bk].b(), modT.b(), x.b(m)], [x.b(m)])

    def ffn(l, b, last):
        norm_mod(pv["gs2"], 24)
        if b == 0:
            k.ve("pool", lambda e: e.memset(uphalo.t[:], 0.0), [], [uphalo.b()])
        fcw, fcb = pv["fcw"], pv["fcb"]
        for jg in range(11):
            av, ab = lw("ffn_w_up", l, jg * 256, 256)
            bv, bb = lw("ffn_w_up", l, FFN_H + jg * 256, 256)
            for ji in range(2):
                j = 2 * jg + ji
                tc_ = []
                for (wv_, wb_, cj, ui) in ((av, ab, j, 0), (bv, bb, 22 + j, 1)):
                    bk = fm(wv_, wb_, ji, h)
                    us = upsb[ui]
                    k.cp("pool", us.t[:, 0:2], uphalo.t[:, cj, :], [uphalo.b()], [us.b()])
                    k.cp("act", us.t[:, 2:2 + TB], ps[bk].t[:], [ps[bk].b()], [us.b()])
                    tcv = tmp()
                    RW = [fcw.b(), fcb.b()]
                    k.act(tcv.t[:], ps[bk].t[:], AF.Identity, [ps[bk].b()] + RW, [tcv.b()], scale=fcw.t[:, cj * 3 + 2:cj * 3 + 3], bias=fcb.t[:, cj:cj + 1])
                    k.stt(tcv.t[:], us.t[:, 1:1 + TB], fcw.t[:, cj * 3 + 1:cj * 3 + 2], tcv.t[:], MUL, ADD, [us.b(), tcv.b()] + RW, [tcv.b()])
                    k.stt(tcv.t[:], us.t[:, 0:TB], fcw.t[:, cj * 3:cj * 3 + 1], tcv.t[:], MUL, ADD, [us.b(), tcv.b()] + RW, [tcv.b()])
                    k.cp("pool", uphalo.t[:, cj, :], us.t[:, TB:TB + 2], [us.b()], [uphalo.b()])
                    if last:
                        k.dma("sp", O["ffn_p"].t[l][:, cj * 128:(cj + 1) * 128].rearrange("r p -> p r"), us.t[:, TB:TB + 2], [us.b()],
                              [O["ffn_p"].b()], nc_ok=True)
                    tc_.append(tcv)
                ga = tmp()
                k.act(ga.t[:], tc_[0].t[:], AF.Gelu_apprx_tanh, [tc_[0].b()], [ga.b()])
                k.tt("dve", actT.t[:, j, :], ga.t[:], tc_[1].t[:], MUL, [ga.b(), tc_[1].b()], [actT.b(j)])
        for m in range(8):
            bk = k.bank()
            for hf in range(2):
                wv, wb = lw("ffn_w_down", l, m * 128, 128, nk=11, row0=hf * 1408)
                for kc in range(11):
                    k.mm(ps[bk].t[:], wv[:, kc, :], actT.t[:, hf * 11 + kc, :], hf == 0 and kc == 0, hf == 1 and kc == 10,
                         [wb, actT.b(hf * 11 + kc)], [ps[bk].b()])
            k.stt(x.t[:, m, :], ps[bk].t[:], modT.t[:, 40 + m, 0:1], x.t[:, m, :], MUL, ADD, [ps[bk].b(), modT.b(), x.b(m)], [x.b(m)])

    def v3(ap, n):
        return ap.rearrange("p (c s) -> p c s", s=NS)

    def linS(name, l, col0, nchunk, rhs, nk):
        bk = k.bank()
        c = 0
        while c < nchunk:
            n2 = min(2, nchunk - c)
            wv, wb = lw(name, l, col0 + c * 128, n2 * 128, nk=nk)
            for i2 in range(n2):
                for kc in range(nk):
                    k.mm(ps[bk].t[:, (c + i2) * NS:(c + i2 + 1) * NS], wv[:, kc, i2 * 128:(i2 + 1) * 128], rhs.t[:, kc, :], kc == 0, kc == nk - 1,
                         [wb, rhs.b()], [ps[bk].b()])
            c += n2
        return bk

    def bc2(ap2, n):
        return ap2.unsqueeze(2).to_broadcast([128, n, NS])

    def norm_modS(gname, sc0, sh0):
        k.act(sqS.t[:], xS.t[:], AF.Square, [xS.b()], [sqS.b()])
        bk = k.bank()
        for kc in range(8):
            k.mm(ps[bk].t[:, 0:NS], onesb.t[:], sqS.t[:, kc, :], kc == 0, kc == 7, [onesb.b(), sqS.b()], [ps[bk].b()])
        k.act(rsS.t[:], ps[bk].t[:, 0:NS], AF.Sqrt, [ps[bk].b(), epsc.b()], [rsS.b()], scale=1.0 / D, bias=epsc.t[:, 0:1])
        k.ve("dve", lambda e: e.reciprocal(out=rsS.t[:], in_=rsS.t[:]), [rsS.b()], [rsS.b()])
        t, t2 = t8S[0], t8S[1]
        k.tt("dve", t.t[:], xS.t[:], rsS.t[:, :].unsqueeze(1).to_broadcast([128, 8, NS]), MUL, [xS.b(), rsS.b()], [t.b()])
        k.tt("dve", t.t[:], t.t[:], bc2(pv[gname].t[:, :], 8), MUL, [t.b(), pv[gname].b()], [t.b()])
        k.ts("dve", t2.t[:], modT.t[:, sc0:sc0 + 8, 1:1 + NS], 1.0, ADD, [modT.b()], [t2.b()])
        k.tt("dve", t.t[:], t.t[:], t2.t[:], MUL, [t.b(), t2.b()], [t.b()])
        k.tt("dve", hS.t[:], t.t[:], modT.t[:, sh0:sh0 + 8, 1:1 + NS], ADD, [t.b(), modT.b()], [hS.b()])

    def residS(bk, gt0):
        t = t8S[0]
        k.tt("dve", t.t[:], v3(ps[bk].t[:, 0:8 * NS], 8), modT.t[:, gt0:gt0 + 8, 1:1 + NS], MUL, [ps[bk].b(), modT.b()], [t.b()])
        k.tt("dve", xS.t[:], xS.t[:], t.t[:], ADD, [xS.b(), t.b()], [xS.b()])

    def sample_layer(l):
        SP = slice(0, NS)
        if l == 0:
            for s in range(NS):
                k.dma("sp", xS.t[:, :, s], I["xs"].t[s].rearrange("(c p) -> p c", p=128), [I["xs"].b()], [xS.b()], nc_ok=True)
            k.dma("sp", ropeS.t[:], I["c_rope_s"].t[0].partition_broadcast(NS), [I["c_rope_s"].b()], [ropeS.b()], nc_ok=True)
        norm_modS("gmix", 8, 0)
        bz = linS("w_in", l, 0, 12, hS, 8)
        k.cp("act", zS.t[:, 0:12, :], v3(ps[bz].t[:, 0:12 * NS], 12), [ps[bz].b()], [zS.b()])
        k.cp("dve", uSb.t[:], zS.t[:, 0:4, :], [zS.b()], [uSb.b()])
        aS = t8S[2]
        k.act(aS.t[:, 4:8, :], zS.t[:, 8:12, :], AF.Sigmoid, [zS.b()], [aS.b()])
        k.tt("dve", aS.t[:, 0:4, :], zS.t[:, 4:8, :], aS.t[:, 4:8, :], MUL, [zS.b(), aS.b()], [aS.b()])
        for s in range(NS):
            for half in range(2):
                hp = slice(64 * half, 64 * half + 64)
                k.dma("sp", s0S.t[hp, :, s, :], I["st_ssm"].t[l][s].rearrange("(p g) n r -> g n p r", g=2)[half], [I["st_ssm"].b()], [s0S.b()],
                      nc_ok=True)
        bre, bim = k.bank(), k.bank()
        for c4 in range(4):
            wv, wb = wload((S["ssmw"].t[l][:, 4 * c4:4 * c4 + 4], S["ssmw"].b(l)), [128, 4, 4, 128])
            for pl in range(4):
                p = 4 * c4 + pl
                k.mm(ps[bre].t[:, p * NS:(p + 1) * NS], wv[:, pl, 0, :], uSb.t[:, c4, :], True, True, [wb, uSb.b()], [ps[bre].b()])
                k.mm(ps[bim].t[:, p * NS:(p + 1) * NS], wv[:, pl, 1, :], uSb.t[:, c4, :], True, True, [wb, uSb.b()], [ps[bim].b()])
        Xre = v3(ps[bre].t[:, 0:16 * NS], 16)
        Xim = v3(ps[bim].t[:, 0:16 * NS], 16)
        rc = bc2(sm["rc1"].t[:, :], 16)
        rs_ = bc2(sm["rs1"].t[:, :], 16)
        s0r, s0i = s0S.t[:, :, :, 0], s0S.t[:, :, :, 1]
        s1r, s1i = s1S.t[:, :, :, 0], s1S.t[:, :, :, 1]
        tA, tB = tmp(), tmp()
        a3 = v3(tA.t[:, 0:16 * NS], 16)
        b3 = v3(tB.t[:, 0:16 * NS], 16)
        R0 = [s0S.b(), sm["rc1"].b(), sm["rs1"].b()]
        k.tt("dve", a3, s0r, rc, MUL, R0, [tA.b()])
        k.tt("dve", b3, s0i, rs_, MUL, R0, [tB.b()])
        k.tt("dve", a3, a3, b3, SUB, [tA.b(), tB.b()], [tA.b()])
        k.tt("dve", s1r, a3, Xre, ADD, [tA.b(), ps[bre].b()], [s1S.b()])
        k.tt("dve", a3, s0r, rs_, MUL, R0, [tA.b()])
        k.tt("dve", b3, s0i, rc, MUL, R0, [tB.b()])
        k.tt("dve", a3, a3, b3, ADD, [tA.b(), tB.b()], [tA.b()])
        k.tt("dve", s1i, a3, Xim, ADD, [tA.b(), ps[bim].b()], [s1S.b()])
        k.cp("act", s1b.t[:, 0], s1r, [s1S.b()], [s1b.b()])
        k.cp("act", s1b.t[:, 1], s1i, [s1S.b()], [s1b.b()])
        for s in range(NS):
            for half in range(2):
                hp = slice(64 * half, 64 * half + 64)
                k.dma("sp", O["ssm_s"].t[l][s].rearrange("(p g) n r -> g n p r", g=2)[half], s1S.t[hp, :, s, :], [s1S.b()], [O["ssm_s"].b()],
                      nc_ok=True)
        by = k.bank()
        for c4 in range(4):
            wv, wb = wload((S["ssmw"].t[l][:, 4 * c4:4 * c4 + 4], S["ssmw"].b(l)), [128, 4, 4, 128])
            oc_ = ps[by].t[:, c4 * NS:(c4 + 1) * NS]
            for pl in range(4):
                p = 4 * c4 + pl
                k.mm(oc_, wv[:, pl, 2, :], s1b.t[:, 0, p, :], pl == 0, False, [wb, s1b.b()], [ps[by].b()])
                k.mm(oc_, wv[:, pl, 3, :], s1b.t[:, 1, p, :], False, False, [wb, s1b.b()], [ps[by].b()])
            k.mm(oc_, ddiag.t[:, c4, :], uSb.t[:, c4, :], False, True, [ddiag.b(), uSb.b()], [ps[by].b()])
        k.act(ygS.t[:], v3(ps[by].t[:, 0:4 * NS], 4), AF.Gelu_apprx_tanh, [ps[by].b()], [ygS.b()])
        bg = linS("ssm_w_glu", l, 0, 4, ygS, 4)
        tg = t8S[0]
        k.tt("dve", tg.t[:, 0:4, :], v3(ps[bg].t[:, 0:4 * NS], 4), bc2(pv["bglu"].t[:, :], 4), ADD, [ps[bg].b(), pv["bglu"].b()], [tg.b()])
        k.act(tg.t[:, 0:4, :], tg.t[:, 0:4, :], AF.Sigmoid, [tg.b()], [tg.b()])
        k.tt("dve", ysS.t[:], ygS.t[:], tg.t[:, 0:4, :], MUL, [ygS.b(), tg.b()], [ysS.b()])
        for c4 in range(4):
            for s in range(NS):
                k.dma("sp", convc.t[:, c4, s, :], I["c_conv"].t[l][s][:, c4 * 128:(c4 + 1) * 128].rearrange("j p -> p j"), [I["c_conv"].b()],
                      [convc.b()], nc_ok=True)
        for s in range(NS):
            k.dma("sp", O["conv_s"].t[l][s][0:29, :], I["c_conv"].t[l][s][1:30, :], [I["c_conv"].b()], [O["conv_s"].b()])
            k.dma("sp", O["conv_s"].t[l][s][29].rearrange("(c p) -> p c", p=128), aS.t[:, 0:4, s], [aS.b()], [O["conv_s"].b()], nc_ok=True)
        wj = pv["convw"].t[:, :].rearrange("p (c j) -> p c j", j=31)
        tP = tmp()
        prod = tP.t[:, 0:4 * NS * 30].rearrange("p (c s j) -> p c s j", c=4, s=NS)
        k.tt("dve", prod, convc.t[:], wj[:, :, 0:30].unsqueeze(2).to_broadcast([128, 4, NS, 30]), MUL, [convc.b(), pv["convw"].b()], [tP.b()])
        yc = t8S[1]
        k.ve("dve", lambda e: e.tensor_reduce(out=yc.t[:, 0:4, :], in_=prod, axis=AX.X, op=ADD), [tP.b()], [yc.b()])
        tq = t8S[0]
        k.tt("dve", tq.t[:, 0:4, :], aS.t[:, 0:4, :], wj[:, :, 30:31].to_broadcast([128, 4, NS]), MUL, [aS.b(), pv["convw"].b()], [tq.b()])
        k.tt("dve", yc.t[:, 0:4, :], yc.t[:, 0:4, :], tq.t[:, 0:4, :], ADD, [yc.b(), tq.b()], [yc.b()])
        k.tt("dve", yc.t[:, 0:4, :], yc.t[:, 0:4, :], bc2(pv["convb"].t[:, :], 4), ADD, [yc.b(), pv["convb"].b()], [yc.b()])
        k.cp("act", sqS.t[:, 0:4, :], yc.t[:, 0:4, :], [yc.b()], [sqS.b()])
        k.act(sqS.t[:, 4:8, :], yc.t[:, 0:4, :], AF.Square, [yc.b()], [sqS.b()])
        bm = k.bank()
        for c4 in range(4):
            k.mm(ps[bm].t[:, 0:NS], onesb.t[:], sqS.t[:, c4, :], c4 == 0, c4 == 3, [onesb.b(), sqS.b()], [ps[bm].b()])
        for c4 in range(4):
            k.mm(ps[bm].t[:, NS:2 * NS], onesb.t[:], sqS.t[:, 4 + c4, :], c4 == 0, c4 == 3, [onesb.b(), sqS.b()], [ps[bm].b()])
        k.ts("dve", mnS.t[:], ps[bm].t[:, 0:NS], 1.0 / 512, MUL, [ps[bm].b()], [mnS.b()])
        tv = tmp()
        k.tt("dve", tv.t[:, 0:NS], mnS.t[:], mnS.t[:], MUL, [mnS.b()], [tv.b()])
        k.stt(rsS.t[:], ps[bm].t[:, NS:2 * NS], 1.0 / 512, tv.t[:, 0:NS], MUL, SUB, [ps[bm].b(), tv.b()], [rsS.b()])
        rsqrt_inplace(rsS.t[:], rsS.b(), 1.0)
        k.tt("dve", yc.t[:, 0:4, :], yc.t[:, 0:4, :], mnS.t[:, :].unsqueeze(1).to_broadcast([128, 4, NS]), SUB, [yc.b(), mnS.b()], [yc.b()])
        k.tt("dve", yc.t[:, 0:4, :], yc.t[:, 0:4, :], rsS.t[:, :].unsqueeze(1).to_broadcast([128, 4, NS]), MUL, [yc.b(), rsS.b()], [yc.b()])
        k.tt("dve", yc.t[:, 0:4, :], yc.t[:, 0:4, :], bc2(pv["lng"].t[:, :], 4), MUL, [yc.b(), pv["lng"].b()], [yc.b()])
        k.tt("dve", yc.t[:, 0:4, :], yc.t[:, 0:4, :], bc2(pv["lnb"].t[:, :], 4), ADD, [yc.b(), pv["lnb"].b()], [yc.b()])
        k.act(ycS.t[:], yc.t[:, 0:4, :], AF.Silu, [yc.b()], [ycS.b()])
        for gi in range(3):
            W_, dil = WINS[gi]
            Ok, Ik = O[f"kv{W_}_s"], I[f"kv{W_}"]
            wq = lw("w_in", l, 1536 + 256 * gi, 256)
            wk = lw("w_in", l, 2304 + 256 * gi, 256)
            wv3 = lw("w_in", l, 3072 + 256 * gi, 256)
            bq, bv = k.bank(), k.bank()
            for (wv_, wb_), bank_, c0 in ((wq, bq, 0), (wk, bq, 256), (wv3, bv, 0)):
                for kc in range(8):
                    k.mm(ps[bank_].t[SP, c0:c0 + 256], hS.t[:, kc, :], wv_[:, kc, :], kc == 0, kc == 7, [hS.b(), wb_], [ps[bank_].b()])
            qk_norm_rope(NS, bq, ropeS.t[:, 0:32], ropeS.t[:, 32:64], ropeS.b())
            k.cp("act", vstage.t[SP, :], ps[bv].t[SP, 0:256], [ps[bv].b()], [vstage.b()])
            k.cp("act", vnb.t[:, gi * 256:(gi + 1) * 256], ps[bv].t[SP, 0:256], [ps[bv].b()], [vnb.b()])
            k.dma("sp", Ok.t[l][:, W_ - 1, 0:256], qkr.t[SP, 4:8, :].rearrange("p a b -> p (a b)"), [qkr.b()], [Ok.b()], nc_ok=True)
            k.dma("sp", Ok.t[l][:, W_ - 1, 256:512], vstage.t[SP, :], [vstage.b()], [Ok.b()], nc_ok=True)
            for s in range(NS):
                k.dma("sp", Ok.t[l][s][0:W_ - 1, :], Ik.t[l][s][1:W_, :], [Ik.b()], [Ok.b()])
            tP2 = tmp()
            pr3 = tP2.t[SP, 0:256].rearrange("p (a b) -> p a b", b=64)
            k.tt("dve", pr3, qkr.t[SP, 0:4, :], qkr.t[SP, 4:8, :], MUL, [qkr.b()], [tP2.b()])
            k.ve("dve", lambda e, gi=gi, pr3=pr3: e.tensor_reduce(out=sself.t[:, gi * 4:(gi + 1) * 4], in_=pr3, axis=AX.X, op=ADD),
                 [tP2.b()], [sself.b()])
            bt = k.bank()
            for j in range(2):
                k.tr(ps[bt].t[:, j * NS:(j + 1) * NS], qkr.t[SP, 2 * j:2 * j + 2, :].rearrange("p a b -> p (a b)"), ident.t[SP, SP],
                     [qkr.b(), ident.b()], [ps[bt].b()])
            k.cp("act", qS.t[:, gi, :, :], v3(ps[bt].t[:, 0:2 * NS], 2), [ps[bt].b()], [qS.b()])
        k.act(sself.t[:], sself.t[:], AF.Exp, [sself.b()], [sself.b()], scale=0.125)
        k.tt("dve", pdS.t[:], sself.t[:, :].unsqueeze(2).to_broadcast([NS, 12, NS]), ident.t[SP, SP].unsqueeze(1).to_broadcast([NS, 12, NS]), MUL,
             [sself.b(), ident.b()], [pdS.b()])
        Pg = pT[1]
        KTs = pT[0].t[:, 0:256].rearrange("p (a b) -> p a b", a=2)
        Vb = pTm[0].t[:, 0:256]
        for s in range(NS):
            bo = k.bank()
            for gi in range(3):
                W_, dil = WINS[gi]
                Ik = I[f"kv{W_}"]
                k.dma("sp", xio.t[:, 0:512], Ik.t[l][s][0:W_:dil, :], [Ik.b()], [xio.b()], nc_ok=True)
                bt = k.bank()
                for j in range(2):
                    k.tr(ps[bt].t[:, j * 128:(j + 1) * 128], xio.t[:, j * 128:(j + 1) * 128], ident.t[:], [xio.b(), ident.b()], [ps[bt].b()])
                k.cp("act", KTs, ps[bt].t[:, 0:256].rearrange("p (a b) -> p a b", a=2), [ps[bt].b()], [pT[0].b()])
                k.cp("dve", Vb, xio.t[:, 256:512], [xio.b()], [pTm[0].b()])
                bs = k.bank()
                for hh in range(4):
                    hp = slice(64 * (hh % 2), 64 * (hh % 2) + 64)
                    k.mm(ps[bs].t[:, hh:hh + 1], KTs[hp, hh // 2, :], qS.t[hp, gi, hh // 2, s:s + 1], True, True, [pT[0].b(), qS.b()], [ps[bs].b()])
                k.act(Pg.t[:, 0:4], ps[bs].t[:, 0:4], AF.Exp, [ps[bs].b()], [Pg.b()], scale=0.125)
                for hh in range(4):
                    col = gi * 4 + hh
                    k.mm(ps[bo].t[0:64, col:col + 1], Vb[:, hh * 64:(hh + 1) * 64], Pg.t[:, hh:hh + 1], True, False, [pTm[0].b(), Pg.b()], [ps[bo].b()])
                    k.mm(ps[bo].t[0:64, col:col + 1], vnb.t[:, col * 64:(col + 1) * 64], pdS.t[:, col, s:s + 1], False, True, [vnb.b(), pdS.b()],
                         [ps[bo].b()])
                for hh in range(4):
                    col = gi * 4 + hh
                    k.mm(ps[bo].t[0:64, 12 + col:13 + col], onesb.t[:, 0:64], Pg.t[:, hh:hh + 1], True, False, [onesb.b(), Pg.b()], [ps[bo].b()])
                    k.mm(ps[bo].t[0:64, 12 + col:13 + col], onesb.t[SP, 0:64], pdS.t[:, col, s:s + 1], False, True, [onesb.b(), pdS.b()],
                         [ps[bo].b()])
            k.cp("dve", oacc.t[:], ps[bo].t[0:64, 0:24].rearrange("p (o g h) -> p o g h", o=2, g=3), [ps[bo].b()], [oacc.b()])
            tA2 = tmp()
            a2 = tA2.t[0:64, 0:8].rearrange("p (o h) -> p o h", o=2)
            k.tt("dve", a2, oacc.t[:, :, 0, :], oacc.t[:, :, 1, :], ADD, [oacc.b()], [tA2.b()])
            k.tt("dve", a2, a2, oacc.t[:, :, 2, :], ADD, [tA2.b(), oacc.b()], [tA2.b()])
            k.ve("dve", lambda e, a2=a2: e.reciprocal(out=a2[:, 1, :], in_=a2[:, 1, :]), [tA2.b()], [tA2.b()])
            k.tt("dve", yaS.t[:, :, s], a2[:, 0, :], a2[:, 1, :], MUL, [tA2.b()], [yaS.b()])
        for m in range(8):
            sv, sbf = lw("w_br_ssm", l, m * 128, 128, nk=4)
            cv, cbf = lw("w_br_conv", l, m * 128, 128, nk=4)
            av, abf = wload((S["w_br_attn"].t[l][:, m * 128:(m + 1) * 128].rearrange("(hh p) n -> p hh n", p=64), S["w_br_attn"].b(l)), [64, 4, 128])
            gts = [lw("w_gate", l, j * 1024 + m * 128, 128) for j in range(3)]
            bb3 = []
            for wv_, wb_, rh, nk_ in ((sv, sbf, ysS, 4), (cv, cbf, ycS, 4)):
                bk = k.bank()
                for kc in range(nk_):
                    k.mm(ps[bk].t[:, 0:NS], wv_[:, kc, :], rh.t[:, kc, :], kc == 0, kc == nk_ - 1, [wb_, rh.b()], [ps[bk].b()])
                bb3.append(bk)
            bk = k.bank()
            for hh in range(4):
                k.mm(ps[bk].t[:, 0:NS], av[:, hh, :], yaS.t[:, hh, :], hh == 0, hh == 3, [abf, yaS.b()], [ps[bk].b()])
            bb3.append(bk)
            for j, bb in enumerate(bb3):
                bg_ = k.bank()
                for kc in range(8):
                    k.mm(ps[bg_].t[:, 0:NS], gts[j][0][:, kc, :], hS.t[:, kc, :], kc == 0, kc == 7, [gts[j][1], hS.b()], [ps[bg_].b()])
                sg = tmp()
                k.act(sg.t[:, 0:NS], ps[bg_].t[:, 0:NS], AF.Sigmoid, [ps[bg_].b(), pv["bgate"].b()], [sg.b()], bias=pv["bgate"].t[:, 8 * j + m:8 * j + m + 1])
                if j == 0:
                    k.tt("dve", mnS.t[:], ps[bb].t[:, 0:NS], sg.t[:, 0:NS], MUL, [ps[bb].b(), sg.b()], [mnS.b()])
                else:
                    k.tt("dve", rsS.t[:], ps[bb].t[:, 0:NS], sg.t[:, 0:NS], MUL, [ps[bb].b(), sg.b()], [rsS.b()])
                    if j == 1:
                        k.tt("dve", mnS.t[:], mnS.t[:], rsS.t[:], ADD, [mnS.b(), rsS.b()], [mnS.b()])
                    else:
                        k.tt("dve", mgS.t[:, m, :], mnS.t[:], rsS.t[:], ADD, [mnS.b(), rsS.b()], [mgS.b()])
        bo_ = linS("w_out", l, 0, 8, mgS, 8)
        residS(bo_, 16)
        norm_modS("gffn", 32, 24)
        bu = linS("ffn_w_up", l, 0, 44, hS, 8)
        k.cp("act", upS.t[:], v3(ps[bu].t[:, 0:44 * NS], 44), [ps[bu].b()], [upS.b()])
        for s in range(NS):
            for j in range(2):
                k.dma("sp", ffnc.t[:, :, s, j], I["c_ffn"].t[l][s][j].rearrange("(c p) -> p c", p=128), [I["c_ffn"].b()], [ffnc.b()], nc_ok=True)
            k.dma("sp", O["ffn_s"].t[l][s][0:1, :], I["c_ffn"].t[l][s][1:2, :], [I["c_ffn"].b()], [O["ffn_s"].b()])
            k.dma("sp", O["ffn_s"].t[l][s][1].rearrange("(c p) -> p c", p=128), upS.t[:, :, s], [upS.b()], [O["ffn_s"].b()], nc_ok=True)
        fw = pv["fcw"].t[:, :].rearrange("p (c j) -> p c j", j=3)
        bcj = lambda j: fw[:, :, j:j + 1].to_broadcast([128, 44, NS])
        tA3 = tmp()
        a4 = v3(tA3.t[:, 0:44 * NS], 44)
        RW = [pv["fcw"].b()]
        k.tt("dve", ucS.t[:], upS.t[:], bcj(2), MUL, [upS.b()] + RW, [ucS.b()])
        k.tt("dve", a4, ffnc.t[:, :, :, 1], bcj(1), MUL, [ffnc.b()] + RW, [tA3.b()])
        k.tt("dve", ucS.t[:], ucS.t[:], a4, ADD, [ucS.b(), tA3.b()], [ucS.b()])
        k.tt("dve", a4, ffnc.t[:, :, :, 0], bcj(0), MUL, [ffnc.b()] + RW, [tA3.b()])
        k.tt("dve", ucS.t[:], ucS.t[:], a4, ADD, [ucS.b(), tA3.b()], [ucS.b()])
        k.tt("dve", ucS.t[:], ucS.t[:], bc2(pv["fcb"].t[:, :], 44), ADD, [ucS.b(), pv["fcb"].b()], [ucS.b()])
        k.act(a4[:, 0:22, :], ucS.t[:, 0:22, :], AF.Gelu_apprx_tanh, [ucS.b()], [tA3.b()])
        k.tt("dve", acS.t[:], a4[:, 0:22, :], ucS.t[:, 22:44, :], MUL, [tA3.b(), ucS.b()], [acS.b()])
        bd = k.bank()
        for m in range(8):
            for hf in range(2):
                wv, wb = lw("ffn_w_down", l, m * 128, 128, nk=11, row0=hf * 1408)
                for kc in range(11):
                    k.mm(ps[bd].t[:, m * NS:(m + 1) * NS], wv[:, kc, :], acS.t[:, hf * 11 + kc, :], hf == 0 and kc == 0, hf == 1 and kc == 10,
                         [wb, acS.b()], [ps[bd].b()])
        residS(bd, 40)
        if l == L - 1:
            for s in range(NS):
                k.dma("sp", O["ys"].t[s].rearrange("(c p) -> p c", p=128), xS.t[:, :, s], [xS.b()], [O["ys"].b()], nc_ok=True)
    k.sample_layer = sample_layer

    def block(l, b):
        last = (b == NB - 1)
        load_x(l, b)
        norm_mod(pv["gs1"], 0)
        proj_a(l, b)
        qkv(l, b)
        attention(b)
        ssm(l, b, last)
        convbr(l, b, last)
        merge(l)
        if getattr(k, "debug", False) and last:
            for nm, tl, np_ in (("d_yssm", yssm, 128), ("d_yconv", yconv, 128), ("d_yattn", yattn, 64), ("d_yg", yg, 128)):
                k.dma("pool", k.dbg[nm].t[:], tl.t[:], tl.all(), [k.dbg[nm].b()])
            k.dma("pool", k.dbg["d_merged"].t[:], actT.t[:, 0:8, :], [actT.b(i) for i in range(8)], [k.dbg["d_merged"].b()])
            k.dma("sp", k.dbg["d_x"].t[:], x.t[:], x.all(), [k.dbg["d_x"].b()])
            k.dma("sp", k.dbg["d_acc"].t[:], xio.t[0:64, :], [xio.b()], [k.dbg["d_acc"].b()])
        ffn(l, b, last)
        store_x(l, b)
    k.block = block
    k.locals = dict(locals())
    return k


def host_consts(SEQ):
    NB = SEQ // TB
    c = {}
    c["c_ident"] = np.eye(128, dtype=np.float32)
    kk = np.arange(128)[:, None]
    qq = np.arange(128)[None, :]
    same = (kk % 4) == (qq % 4)
    m = np.stack([kk <= qq, kk >= qq, same & ((kk // 4) <= (qq // 4)), same, same & ((kk // 4) >= (qq // 4))]).astype(np.float32)
    c["c_masks"] = m
    half = 32
    inv = (10000.0 ** (-np.arange(half, dtype=np.float32) / half)).astype(np.float32)
    def tab(pos):
        ang = pos.astype(np.float32)[..., None] * inv
        return np.concatenate([np.cos(ang), np.sin(ang)], axis=-1).astype(np.float32)
    p = np.arange(128)
    rope = np.zeros((NB, 128, 12, 64), np.float32)
    for b in range(NB):
        t0 = b * TB
        for s in range(4):
            rope[b, :, 0 * 4 + s] = tab(t0 + 128 * s + p)
            rope[b, :, 1 * 4 + s] = tab(t0 + 4 * p + s)
            rope[b, :, 2 * 4 + s] = tab(t0 + 4 * p + s)
    c["c_rope"] = rope
    c["c_rope_s"] = tab(np.array([PAST]))
    c["c_iota"] = np.arange(512, dtype=np.float32)
    return c


WNAMES = ["w_ada", "b_ada", "g_norm_mix", "w_in", "ssm_a_re", "ssm_a_im", "ssm_log_dt", "ssm_b_re", "ssm_b_im", "ssm_c_re",
          "ssm_c_im", "ssm_d", "ssm_w_glu", "ssm_b_glu", "conv_w", "conv_b", "conv_ln_g", "conv_ln_b", "attn_gq", "attn_gk",
          "w_gate", "b_gate", "w_br_ssm", "w_br_conv", "w_br_attn", "w_out", "g_norm_ffn", "ffn_w_up", "ffn_conv_w",
          "ffn_conv_b", "ffn_w_down"]


def make_in_maps(inp, SEQ, DEPTH, NS, ncores):
    consts = host_consts(SEQ)
    maps = []
    f = lambda a: np.ascontiguousarray(a, dtype=np.float32)
    for c in range(ncores):
        b = c % inp["x_prompt"].shape[0]
        ss = slice(c * NS, (c + 1) * NS)
        m = dict(consts)
        m["xp"] = f(inp["x_prompt"][b, :SEQ])
        m["cp"] = f(inp["c_prompt"][b:b + 1])
        m["xs"] = f(inp["x_sample"][ss, 0])
        m["cs"] = f(inp["c_sample"][ss])
        m["st_ssm"] = f(inp["state_ssm"][:DEPTH, ss])
        m["c_conv"] = f(inp["cache_conv"][:DEPTH, ss])
        for w, _ in WINS:
            a = inp[f"cache_kv_w{w}"][:DEPTH, ss]
            m[f"kv{w}"] = f(a.reshape(a.shape[0], a.shape[1], a.shape[2], 512))
        m["c_ffn"] = f(inp["cache_ffn"][:DEPTH, ss])
        for n in WNAMES:
            m[n] = f(inp[n][:DEPTH])
        maps.append(m)
    return maps


def program(k):
    for l in range(k.DEPTH):
        k.cast_layer(l)
    for l in range(k.DEPTH):
        k.layer_prep(l)
        if SAMPLE[0]:
            k.sample_layer(l)
        for b in range(k.NB):
            k.block(l, b)


def kernel(**inputs):
    SEQ = inputs["x_prompt"].shape[1]
    DEPTH = inputs["w_in"].shape[0]
    NS = 4
    ncores = 8
    k = build(SEQ, DEPTH, NS)
    program(k)
    k.P.emit(k.stack)
    k.stack.close()
    maps = make_in_maps(inputs, SEQ, DEPTH, NS, ncores)
    res = run_bass_kernel_spmd(k.nc, maps, core_ids=list(range(ncores)))
    R = res.results
    B = inputs["x_prompt"].shape[0]
    yp = np.stack([R[b]["yp"] for b in range(B)], axis=0).astype(np.float32)
    ys = np.concatenate([R[c]["ys"] for c in range(ncores)], axis=0)[:, None, :].astype(np.float32)

    def pp(name, tail):
        return np.stack([R[b][name] for b in range(B)], axis=1).reshape((DEPTH, B) + tail).astype(np.float32)

    def sp_(name, tail):
        return np.concatenate([R[c][name] for c in range(ncores)], axis=1).reshape((DEPTH, ncores * NS) + tail).astype(np.float32)

    outs = [yp, ys, pp("ssm_p", (32, 64, 2)), sp_("ssm_s", (32, 64, 2)), pp("conv_p", (30, 512)), sp_("conv_s", (30, 512))]
    for w, _ in WINS:
        outs.append(pp(f"kv{w}_p", (min(w, SEQ), 2, 4, 64)))
        outs.append(sp_(f"kv{w}_s", (w, 2, 4, 64)))
    outs.append(pp("ffn_p", (2, F2)))
    outs.append(sp_("ffn_s", (2, F2)))
    return tuple(outs)
```

```python
import contextlib
import numpy as np
import concourse.bass as bass
import concourse.mybir as mybir
from concourse.bass_utils import run_bass_kernel_spmd

F32 = mybir.dt.float32
BF16 = mybir.dt.bfloat16
AF = mybir.ActivationFunctionType
ALU = mybir.AluOpType
AX = mybir.AxisListType

D = 1024
NCH = 8
TB = 512
SSM_W = 512
CONV_W = 512
CONV_K = 31
FFN_H = 2816
F2 = 2 * FFN_H
IN_W = 3840
EPS = 1e-6
WINS = ((128, 1), (512, 4), (2048, 16))
PAST = 8192
SEM_CAP = 24000
DMA_SLOTS = 8
DEBUG = [False]
ATT_GROUPS = [0, 1, 2]
SAMPLE = [True]


class Buf:
    __slots__ = ("name", "w", "r")

    def __init__(self, name):
        self.name = name
        self.w = None
        self.r = {}


class Op:
    __slots__ = ("fn", "waits", "signal", "dma", "sigpos")

    def __init__(self, fn, dma=None):
        self.fn = fn
        self.waits = []
        self.signal = False
        self.dma = dma
        self.sigpos = None


class Prog:
    ENGS = ("pe", "act", "dve", "pool", "sp")

    def __init__(self, nc):
        self.nc = nc
        self.ops = {e: [] for e in self.ENGS}
        self.seen = {e: {} for e in self.ENGS}
        self.ndma = {e: 0 for e in self.ENGS}

    def _need(self, eng, tok, op):
        if tok[0] == "E":
            key = ("E", tok[1])
            if self.seen[eng].get(key, -1) >= tok[2]:
                return
            self.seen[eng][key] = tok[2]
            self.ops[tok[1]][tok[2]].signal = True
        else:
            key = ("D", tok[1], tok[2] % DMA_SLOTS)
            if self.seen[eng].get(key, -1) >= tok[2]:
                return
            self.seen[eng][key] = tok[2]
        op.waits.append(tok)

    def add(self, eng, fn, R=(), W=(), dma=False):
        ops = self.ops[eng]
        idx = len(ops)
        if dma:
            n = self.ndma[eng]
            self.ndma[eng] += 1
            op = Op(fn, dma=n)
            me = ("D", eng, n)
            if n >= DMA_SLOTS:
                self._need(eng, ("D", eng, n - DMA_SLOTS), op)
        else:
            op = Op(fn)
            me = ("E", eng, idx)
        deps = []
        for b in R:
            if b.w is not None:
                deps.append((b.w, True))
        for b in W:
            if b.w is not None:
                deps.append((b.w, False))
            for t in b.r.values():
                deps.append((t, False))
        for t, raw in deps:
            if t[0] == "E" and t[1] == eng and not dma:
                if eng == "pe" or not raw:
                    continue
            if t == me:
                continue
            self._need(eng, t, op)
        ops.append(op)
        for b in W:
            b.w = me
            b.r = {}
        for b in R:
            b.r[(me[0], me[1])] = me
        return me

    def emit(self, stack):
        nc = self.nc
        nsem = {}
        for e in self.ENGS:
            c = 0
            for op in self.ops[e]:
                if op.signal:
                    op.sigpos = c
                    c += 1
            nsem[e] = max(1, -(-c // SEM_CAP))
        sems = {e: [stack.enter_context(nc.semaphore(f"s_{e}_{i}")) for i in range(nsem[e])] for e in self.ENGS}
        dsems = {e: [stack.enter_context(nc.semaphore(f"d_{e}_{i}")) for i in range(DMA_SLOTS)]
                 for e in self.ENGS if self.ndma[e] > 0}
        block = stack.enter_context(nc.Block())
        hooks = {"pe": block.tensor, "act": block.scalar, "dve": block.vector, "pool": block.gpsimd, "sp": block.sync}

        def body(e):
            def run(h):
                for op in self.ops[e]:
                    for t in op.waits:
                        if t[0] == "E":
                            pos = self.ops[t[1]][t[2]].sigpos
                            h.wait_ge(sems[t[1]][pos // SEM_CAP], pos % SEM_CAP + 1)
                        else:
                            h.wait_ge(dsems[t[1]][t[2] % DMA_SLOTS], 16 * (t[2] // DMA_SLOTS + 1))
                    ins = op.fn(h)
                    if op.dma is not None:
                        ins.then_inc(dsems[e][op.dma % DMA_SLOTS], 16)
                    elif op.signal:
                        ins.then_inc(sems[e][op.sigpos // SEM_CAP], 1)
                n = self.ndma[e]
                for s in range(min(n, DMA_SLOTS)):
                    last = ((n - 1 - s) // DMA_SLOTS) * DMA_SLOTS + s
                    h.wait_ge(dsems[e][s], 16 * (last // DMA_SLOTS + 1))
            return run

        for e in self.ENGS:
            hooks[e](body(e))


class Tile:
    def __init__(self, t, nsub=1):
        self.t = t
        self.bufs = [Buf(None) for _ in range(nsub)]

    def b(self, i=0):
        return self.bufs[i]

    def all(self):
        return self.bufs


class K:
    def __init__(self, SEQ, DEPTH, NS):
        self.SEQ, self.DEPTH, self.NS = SEQ, DEPTH, NS
        self.NB = SEQ // TB
        self.nc = bass.Bass("TRN2", target_bir_lowering=False)
        self.P = Prog(self.nc)
        self.stack = contextlib.ExitStack()
        self.bank_rr = 0

    def din(self, name, shape, dt=F32):
        return self.nc.dram_tensor(name, list(shape), dt, kind="ExternalInput").ap()

    def dout(self, name, shape, dt=F32):
        return self.nc.dram_tensor(name, list(shape), dt, kind="ExternalOutput").ap()

    def dscr(self, name, shape, dt, nsub=1):
        return Tile(self.nc.dram_tensor(name, list(shape), dt, kind="Internal").ap(), nsub)

    def sb(self, name, shape, dt, nsub=1):
        return Tile(self.stack.enter_context(self.nc.sbuf_tensor(name, list(shape), dt)), nsub)

    def mm(self, out, lhsT, rhs, start, stop, R, W):
        self.P.add("pe", lambda h: h.matmul(out, lhsT, rhs, start=start, stop=stop, skip_group_check=True), R, W)

    def tr(self, out, in_, ident, R, W):
        self.P.add("pe", lambda h: h.transpose(out, in_, ident), R, W)

    def act(self, out, in_, func, R, W, scale=None, bias=None):
        kw = {}
        if scale is not None:
            kw["scale"] = scale
        if bias is not None:
            kw["bias"] = bias
        self.P.add("act", lambda h: h.activation(out=out, in_=in_, func=func, **kw), R, W)

    def ve(self, eng, fn, R, W):
        self.P.add(eng, fn, R, W)

    def tt(self, eng, out, a, b, op, R, W):
        self.P.add(eng, lambda h: h.tensor_tensor(out=out, in0=a, in1=b, op=op), R, W)

    def ts(self, eng, out, a, s1, op0, R, W, s2=None, op1=None):
        if op1 is None:
            self.P.add(eng, lambda h: h.tensor_scalar(out=out, in0=a, scalar1=s1, scalar2=None, op0=op0), R, W)
        else:
            self.P.add(eng, lambda h: h.tensor_scalar(out=out, in0=a, scalar1=s1, scalar2=s2, op0=op0, op1=op1), R, W)

    def stt(self, out, a, s, b, op0, op1, R, W):
        self.P.add("dve", lambda h: h.scalar_tensor_tensor(out=out, in0=a, scalar=s, in1=b, op0=op0, op1=op1), R, W)

    def cp(self, eng, out, in_, R, W):
        if eng == "act":
            self.P.add("act", lambda h: h.activation(out=out, in_=in_, func=AF.Identity), R, W)
        else:
            self.P.add(eng, lambda h: h.tensor_copy(out=out, in_=in_), R, W)

    def dma(self, q, out, in_, R, W, nc_ok=False):
        nc = self.nc
        if nc_ok:
            def fn(h):
                with nc.allow_non_contiguous_dma(reason="small strided parameter/state transfer"):
                    return h.dma_start(out=out, in_=in_)
        else:
            def fn(h):
                return h.dma_start(out=out, in_=in_)
        self.P.add(q, fn, R, W, dma=True)

    def bank(self):
        i = self.pool_banks[self.bank_rr % len(self.pool_banks)]
        self.bank_rr += 1
        return i


def rr(ap, s, **kw):
    return ap.rearrange(s, **kw)


def build(SEQ, DEPTH, NS, with_sample=True):
    k = K(SEQ, DEPTH, NS)
    nc, P, NB = k.nc, k.P, k.NB
    L = DEPTH
    I = {}
    def inp(name, shape):
        I[name] = Tile(k.din(name, shape))
        return I[name]
    inp("xp", [SEQ, D]); inp("cp", [1, D]); inp("xs", [NS, D]); inp("cs", [NS, D])
    inp("st_ssm", [L, NS, 32, 64, 2]); inp("c_conv", [L, NS, 30, 512])
    for w, _ in WINS:
        inp(f"kv{w}", [L, NS, w, 512])
    inp("c_ffn", [L, NS, 2, F2])
    wshapes = dict(w_ada=[D, 6 * D], b_ada=[6 * D], g_norm_mix=[D], w_in=[D, IN_W], ssm_a_re=[32, 64], ssm_a_im=[32, 64],
                   ssm_log_dt=[32], ssm_b_re=[32, 64, 16], ssm_b_im=[32, 64, 16], ssm_c_re=[32, 16, 64],
                   ssm_c_im=[32, 16, 64], ssm_d=[512], ssm_w_glu=[512, 512], ssm_b_glu=[512], conv_w=[31, 512],
                   conv_b=[512], conv_ln_g=[512], conv_ln_b=[512], attn_gq=[64], attn_gk=[64], w_gate=[D, 3 * D],
                   b_gate=[3 * D], w_br_ssm=[512, D], w_br_conv=[512, D], w_br_attn=[256, D], w_out=[D, D],
                   g_norm_ffn=[D], ffn_w_up=[D, F2], ffn_conv_w=[3, F2], ffn_conv_b=[F2], ffn_w_down=[FFN_H, D])
    for n, s in wshapes.items():
        inp(n, [L] + s)
    inp("c_ident", [128, 128]); inp("c_masks", [5, 128, 128]); inp("c_rope", [NB, 128, 12, 64]); inp("c_rope_s", [1, 64])
    inp("c_iota", [512])
    O = {}
    def outp(name, shape):
        O[name] = Tile(k.dout(name, shape))
        return O[name]
    outp("yp", [SEQ, D]); outp("ys", [NS, D]); outp("ssm_p", [L, 32, 64, 2]); outp("ssm_s", [L, NS, 32, 64, 2])
    outp("conv_p", [L, 30, 512]); outp("conv_s", [L, NS, 30, 512])
    for w, _ in WINS:
        outp(f"kv{w}_p", [L, min(w, SEQ), 512]); outp(f"kv{w}_s", [L, NS, w, 512])
    outp("ffn_p", [L, 2, F2]); outp("ffn_s", [L, NS, 2, F2])
    k.dbg = {}
    if DEBUG[0]:
        k.debug = True
        for nm, shp in (("d_yssm", [128, 4, TB]), ("d_yconv", [128, 4, TB]), ("d_yattn", [64, 4, TB]), ("d_yg", [128, 4, TB]),
                        ("d_merged", [128, 8, TB]), ("d_x", [128, 8, TB]), ("d_acc", [64, 1024])):
            k.dbg[nm] = Tile(k.dout(nm, shp))
    big = ["w_ada", "w_in", "ssm_w_glu", "w_gate", "w_br_ssm", "w_br_conv", "w_br_attn", "w_out", "ffn_w_up", "ffn_w_down"]
    S = {n: k.dscr("s_" + n, [L] + wshapes[n], BF16, nsub=L) for n in big}
    S["cdiag"] = k.dscr("s_cdiag", [L, 4, 128, 32, 128], BF16, nsub=L)
    S["ssmw"] = k.dscr("s_ssmw", [L, 128, 16, 4, 128], BF16, nsub=L)
    S["rot"] = k.dscr("s_rot", [L, 128, 16, 2, TB], BF16, nsub=L)
    xscr = k.dscr("s_x", [2, NB, 128, NCH, TB], F32, nsub=2 * NB)
    k.I, k.O, k.S = I, O, S

    sb = k.sb
    ps = [Tile(k.stack.enter_context(nc.psum_tensor(f"ps{i}", [128, 512], F32))) for i in range(8)]
    k.ps = ps
    k.pool_banks = list(range(8))
    ident = sb("ident", [128, 128], F32)
    identb = sb("identb", [128, 128], BF16)
    onesb = sb("onesb", [128, 128], BF16)
    masks = sb("masks", [128, 5, 128], BF16)
    iota = sb("iota", [128, 512], F32)
    epsc = sb("epsc", [128, 1], F32)
    NWS, WSZ = 6, 2048
    wsl = [sb(f"wsl{i}", [128, WSZ], BF16) for i in range(NWS)]
    wrr = [0]

    def wload(src, shape):
        t = wsl[wrr[0] % NWS]
        wrr[0] += 1
        n = int(np.prod(shape[1:]))
        assert n <= WSZ
        view = t.t[0:shape[0], 0:n]
        if len(shape) == 3:
            view = view.rearrange("p (a b) -> p a b", b=shape[2])
        elif len(shape) == 4:
            view = view.rearrange("p (a b c) -> p a b c", b=shape[2], c=shape[3])
        k.dma("sp", view, src[0], [src[1]], [t.b()], nc_ok=True)
        return view, t.b()

    x = sb("x", [128, NCH, TB], F32, nsub=NCH)
    h = sb("h", [128, NCH, TB], BF16, nsub=NCH)
    sqb = [sb(f"sqb{i}", [128, TB], BF16) for i in range(3)]
    rstd = sb("rstd", [128, TB], F32)
    mean = sb("mean", [128, TB], F32)
    tmpf = [sb(f"tmpf{i}", [128, TB], F32) for i in range(4)]
    tf = [0]

    def tmp():
        t = tmpf[tf[0] % len(tmpf)]
        tf[0] += 1
        return t

    sq_i = [0]

    def sqt():
        t = sqb[sq_i[0] % len(sqb)]
        sq_i[0] += 1
        return t

    uT = sb("uT", [128, 4, TB], BF16, nsub=4)
    aT = sb("aT", [128, 4, 30 + TB], BF16, nsub=4)
    qT = sb("qT", [128, 12, 2, 128], BF16, nsub=12)
    NSL = (8, 8, 20)
    kT = [sb(f"kT{g}", [128, NSL[g], 2, 128], BF16, nsub=NSL[g]) for g in range(3)]
    vR = [sb(f"vR{g}", [128, NSL[g], 256], BF16, nsub=NSL[g]) for g in range(3)]
    ropeT = sb("ropeT", [128, 12, 64], F32)
    gqk = sb("gqk", [128, 2, 64], F32)
    qkn = sb("qkn", [128, 8, 64], F32)
    qkr = sb("qkr", [128, 8, 64], F32)
    vstage = sb("vstage", [128, 256], F32)
    ssq = sb("ssq", [128, 8], F32)
    yssm = sb("yssm", [128, 4, TB], BF16, nsub=4)
    yg = sb("yg", [128, 4, TB], BF16, nsub=4)
    yconv = sb("yconv", [128, 4, TB], BF16, nsub=4)
    ycf = sb("ycf", [128, 4, TB], BF16, nsub=4)
    yattn = sb("yattn", [64, 4, TB], BF16, nsub=4)
    actT = sb("actT", [128, 22, TB], BF16, nsub=22)
    upsb = [sb(f"upsb{i}", [128, 2 + TB], F32) for i in range(2)]
    uphalo = sb("uphalo", [128, 44, 2], F32)
    pT = [sb(f"pT{i}", [128, 5 * 128], BF16) for i in range(2)]
    pTm = [sb(f"pTm{i}", [128, 5 * 128], BF16) for i in range(2)]
    xio = sb("xio", [128, D], F32)
    ddiag = sb("ddiag", [128, 4, 128], BF16)
    sm = {n: sb("sm_" + n, [128, 16], F32) for n in
          ["are", "aim", "dt", "th", "r", "c1", "s1", "c511", "s511", "c512", "s512", "kre", "kim", "t0", "t1", "t2", "t3",
           "rc1", "rs1", "den"]}
    vin = sb("vin", [128, 16, 2], F32)
    vend = sb("vend", [128, 16, 2], F32)
    sstate = sb("sstate", [128, 16, 2], F32)
    v0 = [sb(f"v0_{i}", [128, TB], F32) for i in range(2)]
    vv = [sb(f"vv_{i}", [128, TB], F32) for i in range(2)]
    sre = [sb(f"sre{i}", [128, TB], BF16) for i in range(2)]
    sim = [sb(f"sim{i}", [128, TB], BF16) for i in range(2)]
    pv = {}
    for n, c in [("gmix", 8), ("gffn", 8), ("bada", 48), ("convw", 4 * 31), ("convb", 4), ("lng", 4), ("lnb", 4),
                 ("bglu", 4), ("bgate", 24), ("fcw", 44 * 3), ("fcb", 44), ("dsk", 4), ("gs1", 8), ("gs2", 8)]:
        pv[n] = sb("pv_" + n, [128, c], F32)
    NCOL = 1 + NS
    cT = sb("cT", [128, NCH, NCOL], F32)
    cTb = sb("cTb", [128, NCH, NCOL], BF16)
    modT = sb("modT", [128, 48, NCOL], F32)
    def tview(t, s, **kw):
        v = Tile(t.t[:, :].rearrange(s, **kw), 1)
        v.bufs = t.bufs
        return v
    def aview(c0, s, **kw):
        v = Tile(actT.t[:, c0:c0 + 2, :].rearrange("p a b -> p (a b)").bitcast(F32).rearrange(s, **kw), 1)
        v.bufs = [actT.b(c0)]
        return v
    braw = aview(0, "p (a b c) -> p a b c", a=2, b=16)
    bbar = aview(2, "p (a b c) -> p a b c", a=2, b=16)
    craw = aview(4, "p (a b c) -> p a b c", a=2, b=16)
    zst = aview(6, "p (a b) -> p a b", a=4)
    wst = sb("wst", [128, 4, 128], BF16)
    cdg = sb("cdg", [128, 8, 128], BF16)
    rotst = sb("rotst", [128, 2, TB], BF16)
    xS = sb("xS", [128, NCH, NS], F32)
    hS = sb("hS", [128, NCH, NS], BF16)
    zS = sb("zS", [128, 30, NS], F32)
    sqS = sb("sqS", [128, NCH, NS], BF16)
    rsS = sb("rsS", [128, NS], F32)
    mnS = sb("mnS", [128, NS], F32)
    t8S = [sb(f"t8S{i}", [128, NCH, NS], F32) for i in range(3)]
    uSb = sb("uSb", [128, 4, NS], BF16)
    ygS = sb("ygS", [128, 4, NS], BF16)
    ysS = sb("ysS", [128, 4, NS], BF16)
    ycS = sb("ycS", [128, 4, NS], BF16)
    yaS = sb("yaS", [64, 4, NS], BF16)
    mgS = sb("mgS", [128, NCH, NS], BF16)
    upS = Tile(iota.t[:, 256:256 + 44 * NS].rearrange("p (c s) -> p c s", s=NS), 1)
    upS.bufs = iota.bufs
    ucS = sb("ucS", [128, 44, NS], F32)
    acS = sb("acS", [128, 22, NS], BF16)
    s0S = Tile(iota.t[:, 0:128].rearrange("p (a s r) -> p a s r", a=16, s=NS), 1)
    s1S = Tile(iota.t[:, 128:256].rearrange("p (a s r) -> p a s r", a=16, s=NS), 1)
    s0S.bufs = iota.bufs
    s1S.bufs = iota.bufs
    s1b = sb("s1b", [128, 2, 16, NS], BF16)
    qS = sb("qS", [128, 3, 2, NS], BF16)
    ropeS = sb("ropeS", [NS, 64], F32)
    sself = sb("sself", [NS, 12], F32)
    pdS = sb("pdS", [NS, 12, NS], BF16)
    oacc = sb("oacc", [64, 2, 3, 4], F32)
    convc = Tile(rotst.t[:, :, :].rearrange("p a b -> p (a b)").bitcast(F32)[:, 0:4 * NS * 30].rearrange("p (c s j) -> p c s j", c=4, s=NS), 1)
    convc.bufs = rotst.bufs
    ffnc = Tile(cdg.t[:, :, :].rearrange("p a b -> p (a b)").bitcast(F32)[:, 0:44 * NS * 2].rearrange("p (c s j) -> p c s j", c=44, s=NS), 1)
    ffnc.bufs = cdg.bufs
    vnb = Tile(actT.t[:, 8:10, :].rearrange("p a b -> p (a b)")[0:NS, 0:768], 1)
    vnb.bufs = [actT.b(8)]

    k.dma("sp", ident.t[:], I["c_ident"].t[:, :], [I["c_ident"].b()], [ident.b()])
    k.cp("dve", identb.t[:], ident.t[:], [ident.b()], [identb.b()])
    k.ve("pool", lambda e: e.memset(onesb.t[:], 1.0), [], [onesb.b()])
    k.ve("pool", lambda e: e.memset(epsc.t[:], EPS), [], [epsc.b()])
    k.dma("pool", masks.t[:], rr(I["c_masks"].t, "m p q -> p m q"), [I["c_masks"].b()], [masks.b()], nc_ok=True)
    def cast_jobs(l):
        jobs = []
        for n in big:
            rows = wshapes[n][0]
            step = max(1, rows // 4)
            for r0 in range(0, rows, step):
                jobs.append((n, r0, step))
        return jobs

    def cast_some(l, jobs):
        for n, r0, step in jobs:
            k.dma("pool", S[n].t[l][r0:r0 + step], I[n].t[l][r0:r0 + step], [I[n].b()], [S[n].b(l)])
    k.cast_jobs, k.cast_some = cast_jobs, cast_some

    def cast_layer(l):
        cast_some(l, cast_jobs(l))
    k.cast_layer = cast_layer
    PI = float(np.pi)
    TWO_PI = float(2 * np.pi)

    def sincos(dst_sin, dst_cos, ang, shape, bufs_r, bw_sin, bw_cos, eng="dve"):
        t_k = tmp(); t_y = tmp()
        n = int(np.prod(shape[1:]))
        ki = t_k.t[:, 0:n].bitcast(mybir.dt.int32)
        kf = t_k.t[:, 0:n]
        y = t_y.t[:, 0:n]
        a2 = ang if len(shape) == 2 else ang.rearrange("p a b -> p (a b)")
        k.ts(eng, y, a2, 1.0 / TWO_PI, ALU.mult, bufs_r, [t_y.b()])
        k.cp(eng, ki, y, [t_y.b()], [t_k.b()])
        k.cp(eng, y, ki, [t_k.b()], [t_y.b()])
        k.stt(kf, y, -TWO_PI, a2, ALU.mult, ALU.add, [t_y.b()] + bufs_r, [t_k.b()])
        for shift, dst, bw in ((0.0, dst_sin, bw_sin), (PI / 2, dst_cos, bw_cos)):
            if dst is None:
                continue
            d2 = dst if len(shape) == 2 else dst.rearrange("p a b -> p (a b)")
            k.ts(eng, y, kf, shift, ALU.add, [t_k.b()], [t_y.b()])
            t_m = tmp(); m = t_m.t[:, 0:n]
            k.ts(eng, m, y, PI, ALU.is_gt, [t_y.b()], [t_m.b()])
            k.stt(y, m, -TWO_PI, y, ALU.mult, ALU.add, [t_m.b(), t_y.b()], [t_y.b()])
            k.ts(eng, m, y, -PI, ALU.is_lt, [t_y.b()], [t_m.b()])
            k.stt(y, m, TWO_PI, y, ALU.mult, ALU.add, [t_m.b(), t_y.b()], [t_y.b()])
            k.act(d2, y, AF.Sin, [t_y.b()], [bw])

    def pvload(name, src_ap, srcbuf, pat, **kw):
        k.dma("sp", pv[name].t[:], src_ap.rearrange(pat, **kw), [srcbuf], [pv[name].b()], nc_ok=True)

    def layer_prep(l):
        k.dma("sp", iota.t[:], I["c_iota"].t.partition_broadcast(128), [I["c_iota"].b()], [iota.b()], nc_ok=True)
        pvload("gmix", I["g_norm_mix"].t[l], I["g_norm_mix"].b(), "(c p) -> p c", p=128)
        pvload("gffn", I["g_norm_ffn"].t[l], I["g_norm_ffn"].b(), "(c p) -> p c", p=128)
        pvload("bada", I["b_ada"].t[l], I["b_ada"].b(), "(c p) -> p c", p=128)
        for c4 in range(4):
            k.dma("sp", pv["convw"].t[:, c4 * 31:(c4 + 1) * 31], I["conv_w"].t[l][:, c4 * 128:(c4 + 1) * 128].rearrange("j p -> p j"),
                  [I["conv_w"].b()], [pv["convw"].b()], nc_ok=True)
        pvload("convb", I["conv_b"].t[l], I["conv_b"].b(), "(c p) -> p c", p=128)
        pvload("lng", I["conv_ln_g"].t[l], I["conv_ln_g"].b(), "(c p) -> p c", p=128)
        pvload("lnb", I["conv_ln_b"].t[l], I["conv_ln_b"].b(), "(c p) -> p c", p=128)
        pvload("bglu", I["ssm_b_glu"].t[l], I["ssm_b_glu"].b(), "(c p) -> p c", p=128)
        pvload("bgate", I["b_gate"].t[l], I["b_gate"].b(), "(c p) -> p c", p=128)
        for j in range(3):
            k.dma("sp", pv["fcw"].t[:].rearrange("p (c j) -> p c j", j=3)[:, :, j], I["ffn_conv_w"].t[l][j].rearrange("(c p) -> p c", p=128),
                  [I["ffn_conv_w"].b()], [pv["fcw"].b()], nc_ok=True)
        pvload("fcb", I["ffn_conv_b"].t[l], I["ffn_conv_b"].b(), "(c p) -> p c", p=128)
        pvload("dsk", I["ssm_d"].t[l], I["ssm_d"].b(), "(c p) -> p c", p=128)
        k.dma("sp", gqk.t[:, 0, :], I["attn_gq"].t[l].partition_broadcast(128), [I["attn_gq"].b()], [gqk.b()], nc_ok=True)
        k.dma("sp", gqk.t[:, 1, :], I["attn_gk"].t[l].partition_broadcast(128), [I["attn_gk"].b()], [gqk.b()], nc_ok=True)
        if l == 0:
            k.dma("sp", cT.t[:, :, 0], I["cp"].t[0].rearrange("(c p) -> p c", p=128), [I["cp"].b()], [cT.b()], nc_ok=True)
            for s in range(NS):
                k.dma("sp", cT.t[:, :, 1 + s], I["cs"].t[s].rearrange("(c p) -> p c", p=128), [I["cs"].b()], [cT.b()], nc_ok=True)
            k.act(cTb.t[:], cT.t[:], AF.Silu, [cT.b()], [cTb.b()])
        for cg in range(24):
            wv, wb = wload((S["w_ada"].t[l][:, cg * 256:(cg + 1) * 256].rearrange("(kc p) n -> p kc n", p=128), S["w_ada"].b(l)),
                           [128, 8, 256])
            for mi in range(2):
                m = cg * 2 + mi
                bk = k.bank()
                for kc in range(8):
                    k.mm(ps[bk].t[:, 0:NCOL], wv[:, kc, mi * 128:(mi + 1) * 128], cTb.t[:, kc, :], kc == 0, kc == 7,
                         [wb, cTb.b()], [ps[bk].b()])
                k.ts("dve", modT.t[:, m, :], ps[bk].t[:, 0:NCOL], pv["bada"].t[:, m:m + 1], ALU.add,
                     [ps[bk].b(), pv["bada"].b()], [modT.b()])
        for nm, g, c0 in (("gs1", "gmix", 8), ("gs2", "gffn", 32)):
            k.stt(pv[nm].t[:], modT.t[:, c0:c0 + 8, 0], 1.0, pv[g].t[:], ALU.add, ALU.mult,
                  [modT.b(), pv[g].b()], [pv[nm].b()])
        for half in range(2):
            hp = slice(64 * half, 64 * half + 64)
            for nm, src in (("are", "ssm_a_re"), ("aim", "ssm_a_im")):
                k.dma("sp", sm[nm].t[hp, :], I[src].t[l].rearrange("(p g) n -> g n p", g=2)[half], [I[src].b()], [sm[nm].b()],
                      nc_ok=True)
            k.dma("sp", sm["dt"].t[hp, :], I["ssm_log_dt"].t[l].rearrange("(p g) -> g p", g=2)[half].partition_broadcast(64),
                  [I["ssm_log_dt"].b()], [sm["dt"].b()], nc_ok=True)
            for nm, src, tl in (("b", "ssm_b_re", braw), ("b", "ssm_b_im", braw), ("c", "ssm_c_re", craw), ("c", "ssm_c_im", craw)):
                ri = 0 if src.endswith("re") else 1
                if nm == "b":
                    sap = I[src].t[l].rearrange("(p g) n c -> g n p c", g=2)[half]
                    k.dma("sp", tl.t[hp, ri, :, :], sap, [I[src].b()], [tl.b()], nc_ok=True)
                else:
                    for p in range(16):
                        sap = I[src].t[l][2 * p + half].rearrange("c n -> n c")
                        k.dma("sp", tl.t[hp, ri, p, :], sap, [I[src].b()], [tl.b()], nc_ok=True)
        A = lambda n: sm[n].t[:]
        B_ = lambda n: sm[n].b()
        k.act(A("dt"), A("dt"), AF.Exp, [B_("dt")], [B_("dt")])
        k.tt("dve", A("th"), A("aim"), A("dt"), ALU.mult, [B_("aim"), B_("dt")], [B_("th")])
        k.tt("dve", A("t0"), A("are"), A("dt"), ALU.mult, [B_("are"), B_("dt")], [B_("t0")])
        k.act(A("r"), A("t0"), AF.Exp, [B_("t0")], [B_("r")])
        sincos(A("s1"), A("c1"), A("th"), [128, 16], [B_("th")], B_("s1"), B_("c1"))
        for mult_, sn, cn in ((511.0, "s511", "c511"), (512.0, "s512", "c512")):
            k.ts("dve", A("t1"), A("th"), mult_, ALU.mult, [B_("th")], [B_("t1")])
            sincos(A(sn), A(cn), A("t1"), [128, 16], [B_("t1")], B_(sn), B_(cn))
        k.tt("dve", A("rc1"), A("r"), A("c1"), ALU.mult, [B_("r"), B_("c1")], [B_("rc1")])
        k.tt("dve", A("rs1"), A("r"), A("s1"), ALU.mult, [B_("r"), B_("s1")], [B_("rs1")])
        k.ts("dve", A("t0"), A("rc1"), -1.0, ALU.add, [B_("rc1")], [B_("t0")])
        k.tt("dve", A("t1"), A("are"), A("are"), ALU.mult, [B_("are")], [B_("t1")])
        k.tt("dve", A("t2"), A("aim"), A("aim"), ALU.mult, [B_("aim")], [B_("t2")])
        k.tt("dve", A("den"), A("t1"), A("t2"), ALU.add, [B_("t1"), B_("t2")], [B_("den")])
        k.ve("dve", lambda e: e.reciprocal(out=A("den"), in_=A("den")), [B_("den")], [B_("den")])
        k.tt("dve", A("t1"), A("t0"), A("are"), ALU.mult, [B_("t0"), B_("are")], [B_("t1")])
        k.tt("dve", A("t2"), A("rs1"), A("aim"), ALU.mult, [B_("rs1"), B_("aim")], [B_("t2")])
        k.tt("dve", A("kre"), A("t1"), A("t2"), ALU.add, [B_("t1"), B_("t2")], [B_("kre")])
        k.tt("dve", A("kre"), A("kre"), A("den"), ALU.mult, [B_("kre"), B_("den")], [B_("kre")])
        k.tt("dve", A("t1"), A("rs1"), A("are"), ALU.mult, [B_("rs1"), B_("are")], [B_("t1")])
        k.tt("dve", A("t2"), A("t0"), A("aim"), ALU.mult, [B_("t0"), B_("aim")], [B_("t2")])
        k.tt("dve", A("kim"), A("t1"), A("t2"), ALU.subtract, [B_("t1"), B_("t2")], [B_("kim")])
        k.tt("dve", A("kim"), A("kim"), A("den"), ALU.mult, [B_("kim"), B_("den")], [B_("kim")])
        kre_b = sm["kre"].t[:, :].unsqueeze(2).to_broadcast([128, 16, 16])
        kim_b = sm["kim"].t[:, :].unsqueeze(2).to_broadcast([128, 16, 16])
        t_a = tmp(); ta = t_a.t[:, 0:256].rearrange("p (a b) -> p a b", b=16)
        k.tt("dve", bbar.t[:, 0], braw.t[:, 0], kre_b, ALU.mult, [braw.b(), B_("kre")], [bbar.b()])
        k.tt("dve", ta, braw.t[:, 1], kim_b, ALU.mult, [braw.b(), B_("kim")], [t_a.b()])
        k.tt("dve", bbar.t[:, 0], bbar.t[:, 0], ta, ALU.subtract, [bbar.b(), t_a.b()], [bbar.b()])
        k.tt("dve", bbar.t[:, 1], braw.t[:, 1], kre_b, ALU.mult, [braw.b(), B_("kre")], [bbar.b()])
        k.tt("dve", ta, braw.t[:, 0], kim_b, ALU.mult, [braw.b(), B_("kim")], [t_a.b()])
        k.tt("dve", bbar.t[:, 1], bbar.t[:, 1], ta, ALU.add, [bbar.b(), t_a.b()], [bbar.b()])
        k.ts("dve", craw.t[:, 1], craw.t[:, 1], -1.0, ALU.mult, [craw.b()], [craw.b()])
        for p in range(16):
            pl = p % 4
            k.ve("pool", lambda e: e.memset(zst.t[:], 0.0), [], [zst.b()])
            for half in range(2):
                hp = slice(64 * half, 64 * half + 64)
                c0 = 32 * pl + 16 * half
                for q, srcT in ((0, bbar.t[hp, 0, p, :]), (1, bbar.t[hp, 1, p, :]), (2, craw.t[hp, 0, p, :]), (3, craw.t[hp, 1, p, :])):
                    k.cp("pool", zst.t[hp, q, c0:c0 + 16], srcT, [bbar.b(), craw.b()], [zst.b()])
            bk = k.bank()
            for q in range(2):
                k.tr(ps[bk].t[:, q * 128:(q + 1) * 128], zst.t[:, q, :], ident.t[:], [zst.b(), ident.b()], [ps[bk].b()])
            k.cp("dve", wst.t[:, 0:2, :], ps[bk].t[:, 0:256].rearrange("p (a b) -> p a b", b=128), [ps[bk].b()], [wst.b()])
            k.cp("dve", wst.t[:, 2:4, :], zst.t[:, 2:4, :], [zst.b()], [wst.b()])
            k.dma("sp", S["ssmw"].t[l][:, p], wst.t[:], [wst.b()], [S["ssmw"].b(l)], nc_ok=True)
            t_g = tmp()
            k.ts("dve", t_g.t[:], iota.t[:], sm["th"].t[:, p:p + 1], ALU.mult, [iota.b(), B_("th")], [t_g.b()])
            sincos(rotst.t[:, 1, :], rotst.t[:, 0, :], t_g.t[:], [128, 512], [t_g.b()], rotst.b(), rotst.b())
            k.dma("sp", S["rot"].t[l][:, p], rotst.t[:], [rotst.b()], [S["rot"].b(l)], nc_ok=True)
        for c4 in range(4):
            k.ts("dve", ddiag.t[:, c4, :], identb.t[:], pv["dsk"].t[:, c4:c4 + 1], ALU.mult, [identb.b(), pv["dsk"].b()], [ddiag.b()])
            for j0 in range(0, 32, 8):
                nj = min(8, CONV_K - j0)
                if nj <= 0:
                    continue
                for j in range(nj):
                    k.ts("dve", cdg.t[:, j, :], identb.t[:], pv["convw"].t[:, c4 * 31 + j0 + j:c4 * 31 + j0 + j + 1], ALU.mult,
                         [identb.b(), pv["convw"].b()], [cdg.b()])
                k.dma("sp", S["cdiag"].t[l][c4][:, j0:j0 + nj, :], cdg.t[:, 0:nj, :], [cdg.b()], [S["cdiag"].b(l)], nc_ok=True)
    k.layer_prep = layer_prep
    MUL, ADD, SUB = ALU.mult, ALU.add, ALU.subtract

    def lw(name, l, col0, ncol, nk=8, row0=0):
        Wt = S[name]
        return wload((Wt.t[l][row0:row0 + nk * 128, col0:col0 + ncol].rearrange("(kc p) n -> p kc n", p=128), Wt.b(l)), [128, nk, ncol])

    def load_x(l, b):
        if l == 0:
            for t4 in range(4):
                k.dma("sp", xio.t[:], I["xp"].t[b * TB + t4 * 128:b * TB + (t4 + 1) * 128, :], [I["xp"].b()], [xio.b()])
                for g2 in range(2):
                    bk = k.bank()
                    for q in range(4):
                        kc = g2 * 4 + q
                        k.tr(ps[bk].t[:, q * 128:(q + 1) * 128], xio.t[:, kc * 128:(kc + 1) * 128], ident.t[:], [xio.b(), ident.b()], [ps[bk].b()])
                    k.cp("act", x.t[:, g2 * 4:(g2 + 1) * 4, t4 * 128:(t4 + 1) * 128], ps[bk].t[:, :].rearrange("p (a b) -> p a b", b=128),
                         [ps[bk].b()], [x.b(i) for i in range(g2 * 4, g2 * 4 + 4)])
        else:
            k.dma("sp", x.t[:], xscr.t[(l - 1) % 2, b], [xscr.b(((l - 1) % 2) * NB + b)], x.all())

    def store_x(l, b):
        if l < L - 1:
            k.dma("sp", xscr.t[l % 2, b], x.t[:], x.all(), [xscr.b((l % 2) * NB + b)])
        else:
            for t4 in range(4):
                for g2 in range(2):
                    bk = k.bank()
                    for q in range(4):
                        kc = g2 * 4 + q
                        k.tr(ps[bk].t[:, q * 128:(q + 1) * 128], x.t[:, kc, t4 * 128:(t4 + 1) * 128], ident.t[:], [x.b(kc), ident.b()], [ps[bk].b()])
                    k.cp("act", xio.t[:, g2 * 512:(g2 + 1) * 512], ps[bk].t[:], [ps[bk].b()], [xio.b()])
                k.dma("sp", O["yp"].t[b * TB + t4 * 128:b * TB + (t4 + 1) * 128, :], xio.t[:], [xio.b()], [O["yp"].b()])

    def rsqrt_inplace(t_ap, tb, scale):
        k.act(t_ap, t_ap, AF.Sqrt, [tb, epsc.b()], [tb], scale=scale, bias=epsc.t[0:t_ap.shape[0], 0:1])
        k.ve("dve", lambda e: e.reciprocal(out=t_ap, in_=t_ap), [tb], [tb])

    def norm_mod(gs, shc0):
        bk = k.bank()
        for kc in range(8):
            s = sqt()
            k.act(s.t[:], x.t[:, kc, :], AF.Square, [x.b(kc)], [s.b()])
            k.mm(ps[bk].t[:], onesb.t[:], s.t[:], kc == 0, kc == 7, [onesb.b(), s.b()], [ps[bk].b()])
        k.act(rstd.t[:], ps[bk].t[:], AF.Sqrt, [ps[bk].b(), epsc.b()], [rstd.b()], scale=1.0 / D, bias=epsc.t[:, 0:1])
        k.ve("dve", lambda e: e.reciprocal(out=rstd.t[:], in_=rstd.t[:]), [rstd.b()], [rstd.b()])
        for kc in range(8):
            t = tmp()
            k.tt("dve", t.t[:], x.t[:, kc, :], rstd.t[:], MUL, [x.b(kc), rstd.b()], [t.b()])
            k.act(h.t[:, kc, :], t.t[:], AF.Identity, [t.b(), gs.b(), modT.b()], [h.b(kc)], scale=gs.t[:, kc:kc + 1],
                  bias=modT.t[:, shc0 + kc, 0:1])

    def fm(wv, wb, ci, rhsT, nk=8):
        bk = k.bank()
        for kc in range(nk):
            k.mm(ps[bk].t[:], wv[:, kc, ci * 128:(ci + 1) * 128], rhsT.t[:, kc, :], kc == 0, kc == nk - 1, [wb, rhsT.b(kc)], [ps[bk].b()])
        return bk

    def proj_a(l, b):
        if b == 0:
            k.ve("pool", lambda e: e.memset(aT.t[:, :, 0:30], 0.0), [], aT.all())
        for q in range(2):
            wv, wb = lw("w_in", l, 256 * q, 256)
            for i in range(2):
                bk = fm(wv, wb, i, h)
                k.cp("act", uT.t[:, 2 * q + i, :], ps[bk].t[:], [ps[bk].b()], [uT.b(2 * q + i)])
        for q in range(2):
            gv, gb = lw("w_in", l, 1024 + 256 * q, 256)
            av, ab = lw("w_in", l, 512 + 256 * q, 256)
            for i in range(2):
                m = 2 * q + i
                bg = fm(gv, gb, i, h)
                t = tmp()
                k.act(t.t[:], ps[bg].t[:], AF.Sigmoid, [ps[bg].b()], [t.b()])
                ba = fm(av, ab, i, h)
                k.tt("dve", aT.t[:, m, 30:30 + TB], ps[ba].t[:], t.t[:], MUL, [ps[ba].b(), t.b()], [aT.b(m)])

    def cols(gi, s, ap2):
        if gi == 0:
            return ap2[:, 128 * s:128 * (s + 1)]
        return ap2[:, s::4]

    def slot(gi, b, s):
        return (4 * b + s) % 8 if gi == 0 else ((b % 2) * 4 + s if gi == 1 else (b % 5) * 4 + s)

    def qk_norm_rope(NP, bq, cos2, sin2, ropeb):
        pp = slice(0, NP)
        t = tmp()
        k.act(t.t[pp, :], ps[bq].t[pp, :], AF.Square, [ps[bq].b()], [t.b()])
        k.ve("dve", lambda e, t=t: e.tensor_reduce(out=ssq.t[pp, :], in_=t.t[pp, :].rearrange("p (a b) -> p a b", b=64), axis=AX.X, op=ADD),
             [t.b()], [ssq.b()])
        rsqrt_inplace(ssq.t[pp, :], ssq.b(), 1.0 / 64)
        k.tt("dve", qkn.t[pp], ps[bq].t[pp, :].rearrange("p (a b) -> p a b", b=64), ssq.t[pp, :].unsqueeze(2).to_broadcast([NP, 8, 64]), MUL,
             [ps[bq].b(), ssq.b()], [qkn.b()])
        for j in range(2):
            k.tt("dve", qkn.t[pp, 4 * j:4 * j + 4, :], qkn.t[pp, 4 * j:4 * j + 4, :], gqk.t[pp, j:j + 1, :].to_broadcast([NP, 4, 64]), MUL,
                 [qkn.b(), gqk.b()], [qkn.b()])
        cs = cos2.unsqueeze(1).to_broadcast([NP, 8, 32])
        sn = sin2.unsqueeze(1).to_broadcast([NP, 8, 32])
        x1 = qkn.t[pp, :, 0:32]
        x2 = qkn.t[pp, :, 32:64]
        ta, tb_ = tmp(), tmp()
        a3 = ta.t[pp, 0:256].rearrange("p (a b) -> p a b", b=32)
        b3 = tb_.t[pp, 0:256].rearrange("p (a b) -> p a b", b=32)
        RR = [qkn.b(), ropeb]
        k.tt("dve", a3, x1, cs, MUL, RR, [ta.b()])
        k.tt("dve", b3, x2, sn, MUL, RR, [tb_.b()])
        k.tt("dve", qkr.t[pp, :, 0:32], a3, b3, SUB, [ta.b(), tb_.b()], [qkr.b()])
        k.tt("dve", a3, x1, sn, MUL, RR, [ta.b()])
        k.tt("dve", b3, x2, cs, MUL, RR, [tb_.b()])
        k.tt("dve", qkr.t[pp, :, 32:64], a3, b3, ADD, [ta.b(), tb_.b()], [qkr.b()])

    def qkv(l, b):
        k.dma("sp", ropeT.t[:], I["c_rope"].t[b], [I["c_rope"].b()], [ropeT.b()])
        wts = {}

        def stage_a(gi, s):
            if s == 0:
                wts[gi] = (lw("w_in", l, 1536 + 256 * gi, 256), lw("w_in", l, 2304 + 256 * gi, 256), lw("w_in", l, 3072 + 256 * gi, 256))
            wq, wk, wv3 = wts[gi]
            bq = k.bank()
            bv = k.bank()
            for (wv_, wb_), bank_, c0 in ((wq, bq, 0), (wk, bq, 256), (wv3, bv, 0)):
                for kc in range(8):
                    k.mm(ps[bank_].t[:, c0:c0 + 256], cols(gi, s, h.t[:, kc, :]), wv_[:, kc, :], kc == 0, kc == 7,
                         [h.b(kc), wb_], [ps[bank_].b()])
            return bq, bv

        def stage_b(gi, s, bq, bv):
            W_, _d = WINS[gi]
            keep = min(W_, SEQ)
            base = SEQ - keep
            sl = slot(gi, b, s)
            qk_norm_rope(128, bq, ropeT.t[:, gi * 4 + s, 0:32], ropeT.t[:, gi * 4 + s, 32:64], ropeT.b())
            bt = k.bank()
            for j in range(4):
                k.tr(ps[bt].t[:, j * 128:(j + 1) * 128], qkr.t[:, 2 * j:2 * j + 2, :].rearrange("p a b -> p (a b)"), ident.t[:],
                     [qkr.b(), ident.b()], [ps[bt].b()])
            k.cp("act", qT.t[:, gi * 4 + s, :, :], ps[bt].t[:, 0:256].rearrange("p (a b) -> p a b", b=128), [ps[bt].b()], [qT.b(gi * 4 + s)])
            k.cp("act", kT[gi].t[:, sl, :, :], ps[bt].t[:, 256:512].rearrange("p (a b) -> p a b", b=128), [ps[bt].b()], [kT[gi].b(sl)])
            k.cp("act", vR[gi].t[:, sl, :], ps[bv].t[:, 0:256], [ps[bv].b()], [vR[gi].b(sl)])
            t0 = b * TB
            tail = (t0 + 128 * s >= base) if gi == 0 else (t0 >= base)
            if tail:
                dst = O[f"kv{W_}_p"].t[l]
                r0 = t0 - base
                k.cp("act", vstage.t[:], ps[bv].t[:, 0:256], [ps[bv].b()], [vstage.b()])
                for src_ap, srcb, c0 in ((qkr.t[:, 4:8, :].rearrange("p a b -> p (a b)"), qkr.b(), 0), (vstage.t[:], vstage.b(), 256)):
                    if gi == 0:
                        k.dma("sp", dst[r0 + 128 * s:r0 + 128 * s + 128, c0:c0 + 256], src_ap, [srcb], [O[f"kv{W_}_p"].b()], nc_ok=True)
                    else:
                        k.dma("sp", dst[r0 + s:r0 + 512:4, c0:c0 + 256], src_ap, [srcb], [O[f"kv{W_}_p"].b()], nc_ok=True)

        sets = [(gi, s) for gi in range(3) for s in range(4)]
        cur = stage_a(*sets[0])
        for n_, (gi, s) in enumerate(sets):
            nxt = stage_a(*sets[n_ + 1]) if n_ + 1 < len(sets) else None
            stage_b(gi, s, *cur)
            cur = nxt

    pi_ = [0]

    def attention(b):
        acc = xio.t[0:64, :].rearrange("p (a b) -> p a b", a=2)
        its = []
        for hh in range(4):
            for gi in ATT_GROUPS:
                for s in range(4):
                    if gi == 0:
                        S_ = 4 * b + s
                        kts = ([((S_ - 1) % 8, 1)] if S_ >= 1 else []) + [(S_ % 8, 0)]
                    elif gi == 1:
                        kts = ([(((b - 1) % 2) * 4 + s, 1)] if b >= 1 else []) + [((b % 2) * 4 + s, 0)]
                    else:
                        kts = [(((b - j) % 5) * 4 + s, 4 if j == 4 else 3) for j in (4, 3, 2, 1) if b - j >= 0] + [((b % 5) * 4 + s, 2)]
                    its.append((hh, gi, s, kts))

        def scores(it):
            hh, gi, s, kts = it
            hp = slice(64 * (hh % 2), 64 * (hh % 2) + 64)
            pr = hh // 2
            n = len(kts)
            pt = pT[pi_[0] % 2]
            ptm = pTm[pi_[0] % 2]
            pi_[0] += 1
            banks = [k.bank()] + ([k.bank()] if n > 4 else [])
            for i, (sl, mk) in enumerate(kts):
                bk = banks[i // 4]
                c = (i % 4) * 128
                k.mm(ps[bk].t[:, c:c + 128], kT[gi].t[hp, sl, pr, :], qT.t[hp, gi * 4 + s, pr, :], True, True,
                     [kT[gi].b(sl), qT.b(gi * 4 + s)], [ps[bk].b()])
            for j, bk in enumerate(banks):
                w = min(n - 4 * j, 4) * 128
                k.act(pt.t[:, 512 * j:512 * j + w], ps[bk].t[:, 0:w], AF.Exp, [ps[bk].b()], [pt.b()], scale=0.125)
            for i, (sl, mk) in enumerate(kts):
                k.tt("pool", ptm.t[:, i * 128:(i + 1) * 128], pt.t[:, i * 128:(i + 1) * 128], masks.t[:, mk, :], MUL,
                     [pt.b(), masks.b()], [ptm.b()])
            return ptm

        def pv_(it, ptm):
            hh, gi, s, kts = it
            n = len(kts)
            ob = k.bank()
            for i, (sl, mk) in enumerate(kts):
                k.mm(ps[ob].t[0:64, 0:128], vR[gi].t[:, sl, hh * 64:(hh + 1) * 64], ptm.t[:, i * 128:(i + 1) * 128], i == 0, i == n - 1,
                     [vR[gi].b(sl), ptm.b()], [ps[ob].b()])
            for i, (sl, mk) in enumerate(kts):
                k.mm(ps[ob].t[0:64, 128:256], onesb.t[:, 0:64], ptm.t[:, i * 128:(i + 1) * 128], i == 0, i == n - 1,
                     [onesb.b(), ptm.b()], [ps[ob].b()])
            src_ = ps[ob].t[0:64, 0:256].rearrange("p (a b) -> p a b", a=2)
            dst_ = acc[:, :, 128 * s:128 * (s + 1)] if gi == 0 else acc[:, :, s::4]
            if gi == ATT_GROUPS[0]:
                k.cp("dve", dst_, src_, [ps[ob].b()], [xio.b()])
            else:
                k.tt("dve", dst_, dst_, src_, ADD, [ps[ob].b(), xio.b()], [xio.b()])
            if gi == ATT_GROUPS[-1] and s == 3:
                t = tmp()
                k.ve("dve", lambda e, t=t: e.reciprocal(out=t.t[0:64, :], in_=acc[:, 1, :]), [xio.b()], [t.b()])
                k.tt("dve", yattn.t[:, hh, :], acc[:, 0, :], t.t[0:64, :], MUL, [xio.b(), t.b()], [yattn.b(hh)])

        cur = scores(its[0])
        for n_, it in enumerate(its):
            nxt = scores(its[n_ + 1]) if n_ + 1 < len(its) else None
            pv_(it, cur)
            cur = nxt

    def ssm(l, b, last):
        k.pool_banks = list(range(6))
        rB = sm["r"].b()
        wts = {}

        def emit_x(p):
            c4, pl = divmod(p, 4)
            if pl == 0:
                wts[c4] = wload((S["ssmw"].t[l][:, 4 * c4:4 * c4 + 4], S["ssmw"].b(l)), [128, 4, 4, 128])
            wv, wb = wts[c4]
            rv, rb = wload((S["rot"].t[l][:, p], S["rot"].b(l)), [128, 2, TB])
            bre, bim = k.bank(), k.bank()
            k.mm(ps[bre].t[:], wv[:, pl, 0, :], uT.t[:, c4, :], True, True, [wb, uT.b(c4)], [ps[bre].b()])
            k.mm(ps[bim].t[:], wv[:, pl, 1, :], uT.t[:, c4, :], True, True, [wb, uT.b(c4)], [ps[bim].b()])
            return rv, rb, bre, bim

        nxt = emit_x(0)
        for p in range(16):
            c4, pl = divmod(p, 4)
            yb = 6 + (c4 % 2)
            wv, wb = wts[c4]
            rv, rb, bre, bim = nxt
            if p + 1 < 16:
                nxt = emit_x(p + 1)
            cosv, sinv = rv[:, 0, :], rv[:, 1, :]
            dt_ = tmpf[2]
            vvb = (vv[0], vv[1]) if p % 2 == 0 else (tmpf[0], tmpf[1])
            k.tt("dve", dt_.t[:], ps[bre].t[:], cosv, MUL, [ps[bre].b(), rb], [dt_.b()])
            k.tt("dve", v0[0].t[:], ps[bim].t[:], sinv, MUL, [ps[bim].b(), rb], [v0[0].b()])
            k.tt("dve", v0[0].t[:], v0[0].t[:], dt_.t[:], ADD, [v0[0].b(), dt_.b()], [v0[0].b()])
            k.tt("dve", dt_.t[:], ps[bim].t[:], cosv, MUL, [ps[bim].b(), rb], [dt_.b()])
            k.tt("dve", v0[1].t[:], ps[bre].t[:], sinv, MUL, [ps[bre].b(), rb], [v0[1].b()])
            k.tt("dve", v0[1].t[:], dt_.t[:], v0[1].t[:], SUB, [v0[1].b(), dt_.b()], [v0[1].b()])
            for ri in range(2):
                init = 0.0 if b == 0 else vin.t[:, p, ri:ri + 1]
                k.ve("dve", lambda e, ri=ri, p=p, init=init, vvb=vvb: e.tensor_tensor_scan(
                    out=vvb[ri].t[:], data0=sm["r"].t[:, p:p + 1].to_broadcast([128, TB]), data1=v0[ri].t[:], initial=init,
                    op0=MUL, op1=ADD), [v0[ri].b(), rB, vin.b()], [vvb[ri].b()])
                k.cp("act", vend.t[:, p, ri:ri + 1], vvb[ri].t[:, TB - 1:TB], [vvb[ri].b()], [vend.b()])
            pa, pb = pT[0], pT[1]
            pav, pbv = pa.t[:, 0:TB], pb.t[:, 0:TB]
            k.tt("pool", pav, vvb[0].t[:], cosv, MUL, [vvb[0].b(), rb], [pa.b()])
            k.tt("pool", pbv, vvb[1].t[:], sinv, MUL, [vvb[1].b(), rb], [pb.b()])
            k.tt("pool", sre[p % 2].t[:], pav, pbv, SUB, [pa.b(), pb.b()], [sre[p % 2].b()])
            k.tt("pool", pav, vvb[1].t[:], cosv, MUL, [vvb[1].b(), rb], [pa.b()])
            k.tt("pool", pbv, vvb[0].t[:], sinv, MUL, [vvb[0].b(), rb], [pb.b()])
            k.tt("pool", sim[p % 2].t[:], pav, pbv, ADD, [pa.b(), pb.b()], [sim[p % 2].b()])
            k.mm(ps[yb].t[:], wv[:, pl, 2, :], sre[p % 2].t[:], pl == 0, False, [wb, sre[p % 2].b()], [ps[yb].b()])
            k.mm(ps[yb].t[:], wv[:, pl, 3, :], sim[p % 2].t[:], False, False, [wb, sim[p % 2].b()], [ps[yb].b()])
            if pl == 3:
                k.mm(ps[yb].t[:], ddiag.t[:, c4, :], uT.t[:, c4, :], False, True, [ddiag.b(), uT.b(c4)], [ps[yb].b()])
                k.act(yg.t[:, c4, :], ps[yb].t[:], AF.Gelu_apprx_tanh, [ps[yb].b()], [yg.b(c4)])
        k.pool_banks = list(range(8))
        cn, sn_ = ("c511", "s511") if last else ("c512", "s512")
        dstT = sstate if last else vin
        vr, vi = vend.t[:, :, 0], vend.t[:, :, 1]
        A = lambda n: sm[n].t[:]
        k.tt("dve", A("t0"), vr, A(cn), MUL, [vend.b(), sm[cn].b()], [sm["t0"].b()])
        k.tt("dve", A("t1"), vi, A(sn_), MUL, [vend.b(), sm[sn_].b()], [sm["t1"].b()])
        k.tt("dve", dstT.t[:, :, 0], A("t0"), A("t1"), SUB, [sm["t0"].b(), sm["t1"].b()], [dstT.b()])
        k.tt("dve", A("t2"), vi, A(cn), MUL, [vend.b(), sm[cn].b()], [sm["t2"].b()])
        k.tt("dve", A("t3"), vr, A(sn_), MUL, [vend.b(), sm[sn_].b()], [sm["t3"].b()])
        k.tt("dve", dstT.t[:, :, 1], A("t2"), A("t3"), ADD, [sm["t2"].b(), sm["t3"].b()], [dstT.b()])
        if last:
            for half in range(2):
                hp = slice(64 * half, 64 * half + 64)
                k.dma("sp", O["ssm_p"].t[l].rearrange("(p g) n r -> g n p r", g=2)[half], sstate.t[hp, :, :], [sstate.b()], [O["ssm_p"].b()],
                      nc_ok=True)
        gv, gb = lw("ssm_w_glu", l, 0, 512, nk=4)
        for m in range(4):
            bk = fm(gv, gb, m, yg, nk=4)
            t = tmp()
            k.act(t.t[:], ps[bk].t[:], AF.Sigmoid, [ps[bk].b(), pv["bglu"].b()], [t.b()], bias=pv["bglu"].t[:, m:m + 1])
            k.tt("dve", yssm.t[:, m, :], yg.t[:, m, :], t.t[:], MUL, [yg.b(m), t.b()], [yssm.b(m)])

    def convbr(l, b, last):
        for c4 in range(4):
            bk = k.bank()
            for j0, nj in ((0, 16), (16, 15)):
                wv, wb = wload((S["cdiag"].t[l][c4][:, j0:j0 + nj, :], S["cdiag"].b(l)), [128, nj, 128])
                for j in range(nj):
                    jj = j0 + j
                    k.mm(ps[bk].t[:], wv[:, j, :], aT.t[:, c4, jj:jj + TB], jj == 0, jj == 30, [wb, aT.b(c4)], [ps[bk].b()])
            k.act(ycf.t[:, c4, :], ps[bk].t[:], AF.Identity, [ps[bk].b(), pv["convb"].b()], [ycf.b(c4)], bias=pv["convb"].t[:, c4:c4 + 1])
        if last:
            for c4 in range(4):
                k.dma("pool", O["conv_p"].t[l][:, c4 * 128:(c4 + 1) * 128].rearrange("j p -> p j"), aT.t[:, c4, TB:TB + 30], [aT.b(c4)],
                      [O["conv_p"].b()], nc_ok=True)
        else:
            k.cp("pool", aT.t[:, :, 0:30], aT.t[:, :, TB:TB + 30], aT.all(), aT.all())
        bm, bq2 = k.bank(), k.bank()
        for c4 in range(4):
            k.mm(ps[bm].t[:], onesb.t[:], ycf.t[:, c4, :], c4 == 0, c4 == 3, [onesb.b(), ycf.b(c4)], [ps[bm].b()])
        for c4 in range(4):
            s = sqt()
            k.act(s.t[:], ycf.t[:, c4, :], AF.Square, [ycf.b(c4)], [s.b()])
            k.mm(ps[bq2].t[:], onesb.t[:], s.t[:], c4 == 0, c4 == 3, [onesb.b(), s.b()], [ps[bq2].b()])
        k.ts("dve", mean.t[:], ps[bm].t[:], 1.0 / 512, MUL, [ps[bm].b()], [mean.b()])
        t = tmp()
        k.tt("dve", t.t[:], mean.t[:], mean.t[:], MUL, [mean.b()], [t.b()])
        k.stt(rstd.t[:], ps[bq2].t[:], 1.0 / 512, t.t[:], MUL, SUB, [ps[bq2].b(), t.b()], [rstd.b()])
        rsqrt_inplace(rstd.t[:], rstd.b(), 1.0)
        for c4 in range(4):
            t = tmp()
            k.tt("dve", t.t[:], ycf.t[:, c4, :], mean.t[:], SUB, [ycf.b(c4), mean.b()], [t.b()])
            k.tt("dve", t.t[:], t.t[:], rstd.t[:], MUL, [t.b(), rstd.b()], [t.b()])
            k.act(yconv.t[:, c4, :], t.t[:], AF.Silu, [t.b(), pv["lng"].b(), pv["lnb"].b()], [yconv.b(c4)],
                  scale=pv["lng"].t[:, c4:c4 + 1], bias=pv["lnb"].t[:, c4:c4 + 1])

    def merge(l):
        for mg in range(2):
            for mi in range(4):
                m = mg * 4 + mi
                sv, sbf = lw("w_br_ssm", l, m * 128, 128, nk=4)
                cv, cbf = lw("w_br_conv", l, m * 128, 128, nk=4)
                av, abf = wload((S["w_br_attn"].t[l][:, m * 128:(m + 1) * 128].rearrange("(hh p) n -> p hh n", p=64), S["w_br_attn"].b(l)),
                                [64, 4, 128])
                gts = [lw("w_gate", l, j * 1024 + m * 128, 128) for j in range(3)]
                b1 = fm(sv, sbf, 0, yssm, nk=4)
                b2 = fm(cv, cbf, 0, yconv, nk=4)
                b3 = k.bank()
                for hh in range(4):
                    k.mm(ps[b3].t[:], av[:, hh, :], yattn.t[:, hh, :], hh == 0, hh == 3, [abf, yattn.b(hh)], [ps[b3].b()])
                for j, bb in enumerate((b1, b2, b3)):
                    bg = fm(gts[j][0], gts[j][1], 0, h)
                    sg = tmp()
                    k.act(sg.t[:], ps[bg].t[:], AF.Sigmoid, [ps[bg].b(), pv["bgate"].b()], [sg.b()], bias=pv["bgate"].t[:, 8 * j + m:8 * j + m + 1])
                    if j == 0:
                        k.tt("dve", mean.t[:], ps[bb].t[:], sg.t[:], MUL, [ps[bb].b(), sg.b()], [mean.b()])
                    else:
                        k.tt("dve", rstd.t[:], ps[bb].t[:], sg.t[:], MUL, [ps[bb].b(), sg.b()], [rstd.b()])
                        if j == 1:
                            k.tt("pool", mean.t[:], mean.t[:], rstd.t[:], ADD, [mean.b(), rstd.b()], [mean.b()])
                        else:
                            k.tt("pool", actT.t[:, m, :], mean.t[:], rstd.t[:], ADD, [mean.b(), rstd.b()], [actT.b(m)])
        for mg in range(4):
            wv, wb = lw("w_out", l, mg * 256, 256)
            for mi in range(2):
                m = 2 * mg + mi
                bk = fm(wv, wb, mi, actT)
                k.stt(x.t[:, m, :], ps[bk].t[:], modT.t[:, 16 + m, 0:1], x.t[:, m, :], MUL, ADD, [ps[bk].b(), modT.b(), x.b(m)], [x.b(m)])

    def ffn(l, b, last):
        norm_mod(pv["gs2"], 24)
        if b == 0:
            k.ve("pool", lambda e: e.memset(uphalo.t[:], 0.0), [], [uphalo.b()])
        fcw, fcb = pv["fcw"], pv["fcb"]
        for jg in range(11):
            av, ab = lw("ffn_w_up", l, jg * 256, 256)
            bv, bb = lw("ffn_w_up", l, FFN_H + jg * 256, 256)
            for ji in range(2):
                j = 2 * jg + ji
                tc_ = []
                for (wv_, wb_, cj, ui) in ((av, ab, j, 0), (bv, bb, 22 + j, 1)):
                    bk = fm(wv_, wb_, ji, h)
                    us = upsb[ui]
                    k.cp("pool", us.t[:, 0:2], uphalo.t[:, cj, :], [uphalo.b()], [us.b()])
                    k.cp("act", us.t[:, 2:2 + TB], ps[bk].t[:], [ps[bk].b()], [us.b()])
                    tcv = tmp()
                    RW = [fcw.b(), fcb.b()]
                    k.act(tcv.t[:], ps[bk].t[:], AF.Identity, [ps[bk].b()] + RW, [tcv.b()], scale=fcw.t[:, cj * 3 + 2:cj * 3 + 3], bias=fcb.t[:, cj:cj + 1])
                    k.stt(tcv.t[:], us.t[:, 1:1 + TB], fcw.t[:, cj * 3 + 1:cj * 3 + 2], tcv.t[:], MUL, ADD, [us.b(), tcv.b()] + RW, [tcv.b()])
                    k.stt(tcv.t[:], us.t[:, 0:TB], fcw.t[:, cj * 3:cj * 3 + 1], tcv.t[:], MUL, ADD, [us.b(), tcv.b()] + RW, [tcv.b()])
                    k.cp("pool", uphalo.t[:, cj, :], us.t[:, TB:TB + 2], [us.b()], [uphalo.b()])
                    if last:
                        k.dma("sp", O["ffn_p"].t[l][:, cj * 128:(cj + 1) * 128].rearrange("r p -> p r"), us.t[:, TB:TB + 2], [us.b()],
                              [O["ffn_p"].b()], nc_ok=True)
                    tc_.append(tcv)
                ga = tmp()
                k.act(ga.t[:], tc_[0].t[:], AF.Gelu_apprx_tanh, [tc_[0].b()], [ga.b()])
                k.tt("dve", actT.t[:, j, :], ga.t[:], tc_[1].t[:], MUL, [ga.b(), tc_[1].b()], [actT.b(j)])
        for m in range(8):
            bk = k.bank()
            for hf in range(2):
                wv, wb = lw("ffn_w_down", l, m * 128, 128, nk=11, row0=hf * 1408)
                for kc in range(11):
                    k.mm(ps[bk].t[:], wv[:, kc, :], actT.t[:, hf * 11 + kc, :], hf == 0 and kc == 0, hf == 1 and kc == 10,
                         [wb, actT.b(hf * 11 + kc)], [ps[bk].b()])
            k.stt(x.t[:, m, :], ps[bk].t[:], modT.t[:, 40 + m, 0:1], x.t[:, m, :], MUL, ADD, [ps[bk].b(), modT.b(), x.b(m)], [x.b(m)])

    def v3(ap, n):
        return ap.rearrange("p (c s) -> p c s", s=NS)

    def linS(name, l, col0, nchunk, rhs, nk):
        bk = k.bank()
        c = 0
        while c < nchunk:
            n2 = min(2, nchunk - c)
            wv, wb = lw(name, l, col0 + c * 128, n2 * 128, nk=nk)
            for i2 in range(n2):
                for kc in range(nk):
                    k.mm(ps[bk].t[:, (c + i2) * NS:(c + i2 + 1) * NS], wv[:, kc, i2 * 128:(i2 + 1) * 128], rhs.t[:, kc, :], kc == 0, kc == nk - 1,
                         [wb, rhs.b()], [ps[bk].b()])
            c += n2
        return bk

    def bc2(ap2, n):
        return ap2.unsqueeze(2).to_broadcast([128, n, NS])

    def norm_modS(gname, sc0, sh0):
        k.act(sqS.t[:], xS.t[:], AF.Square, [xS.b()], [sqS.b()])
        bk = k.bank()
        for kc in range(8):
            k.mm(ps[bk].t[:, 0:NS], onesb.t[:], sqS.t[:, kc, :], kc == 0, kc == 7, [onesb.b(), sqS.b()], [ps[bk].b()])
        k.act(rsS.t[:], ps[bk].t[:, 0:NS], AF.Sqrt, [ps[bk].b(), epsc.b()], [rsS.b()], scale=1.0 / D, bias=epsc.t[:, 0:1])
        k.ve("dve", lambda e: e.reciprocal(out=rsS.t[:], in_=rsS.t[:]), [rsS.b()], [rsS.b()])
        t, t2 = t8S[0], t8S[1]
        k.tt("dve", t.t[:], xS.t[:], rsS.t[:, :].unsqueeze(1).to_broadcast([128, 8, NS]), MUL, [xS.b(), rsS.b()], [t.b()])
        k.tt("dve", t.t[:], t.t[:], bc2(pv[gname].t[:, :], 8), MUL, [t.b(), pv[gname].b()], [t.b()])
        k.ts("dve", t2.t[:], modT.t[:, sc0:sc0 + 8, 1:1 + NS], 1.0, ADD, [modT.b()], [t2.b()])
        k.tt("dve", t.t[:], t.t[:], t2.t[:], MUL, [t.b(), t2.b()], [t.b()])
        k.tt("dve", hS.t[:], t.t[:], modT.t[:, sh0:sh0 + 8, 1:1 + NS], ADD, [t.b(), modT.b()], [hS.b()])

    def residS(bk, gt0):
        t = t8S[0]
        k.tt("dve", t.t[:], v3(ps[bk].t[:, 0:8 * NS], 8), modT.t[:, gt0:gt0 + 8, 1:1 + NS], MUL, [ps[bk].b(), modT.b()], [t.b()])
        k.tt("dve", xS.t[:], xS.t[:], t.t[:], ADD, [xS.b(), t.b()], [xS.b()])

    def sample_layer(l):
        SP = slice(0, NS)
        if l == 0:
            for s in range(NS):
                k.dma("sp", xS.t[:, :, s], I["xs"].t[s].rearrange("(c p) -> p c", p=128), [I["xs"].b()], [xS.b()], nc_ok=True)
            k.dma("sp", ropeS.t[:], I["c_rope_s"].t[0].partition_broadcast(NS), [I["c_rope_s"].b()], [ropeS.b()], nc_ok=True)
        norm_modS("gmix", 8, 0)
        bz = linS("w_in", l, 0, 12, hS, 8)
        k.cp("act", zS.t[:, 0:12, :], v3(ps[bz].t[:, 0:12 * NS], 12), [ps[bz].b()], [zS.b()])
        k.cp("dve", uSb.t[:], zS.t[:, 0:4, :], [zS.b()], [uSb.b()])
        aS = t8S[2]
        k.act(aS.t[:, 4:8, :], zS.t[:, 8:12, :], AF.Sigmoid, [zS.b()], [aS.b()])
        k.tt("dve", aS.t[:, 0:4, :], zS.t[:, 4:8, :], aS.t[:, 4:8, :], MUL, [zS.b(), aS.b()], [aS.b()])
        for s in range(NS):
            for half in range(2):
                hp = slice(64 * half, 64 * half + 64)
                k.dma("sp", s0S.t[hp, :, s, :], I["st_ssm"].t[l][s].rearrange("(p g) n r -> g n p r", g=2)[half], [I["st_ssm"].b()], [s0S.b()],
                      nc_ok=True)
        bre, bim = k.bank(), k.bank()
        for c4 in range(4):
            wv, wb = wload((S["ssmw"].t[l][:, 4 * c4:4 * c4 + 4], S["ssmw"].b(l)), [128, 4, 4, 128])
            for pl in range(4):
                p = 4 * c4 + pl
                k.mm(ps[bre].t[:, p * NS:(p + 1) * NS], wv[:, pl, 0, :], uSb.t[:, c4, :], True, True, [wb, uSb.b()], [ps[bre].b()])
                k.mm(ps[bim].t[:, p * NS:(p + 1) * NS], wv[:, pl, 1, :], uSb.t[:, c4, :], True, True, [wb, uSb.b()], [ps[bim].b()])
        Xre = v3(ps[bre].t[:, 0:16 * NS], 16)
        Xim = v3(ps[bim].t[:, 0:16 * NS], 16)
        rc = bc2(sm["rc1"].t[:, :], 16)
        rs_ = bc2(sm["rs1"].t[:, :], 16)
        s0r, s0i = s0S.t[:, :, :, 0], s0S.t[:, :, :, 1]
        s1r, s1i = s1S.t[:, :, :, 0], s1S.t[:, :, :, 1]
        tA, tB = tmp(), tmp()
        a3 = v3(tA.t[:, 0:16 * NS], 16)
        b3 = v3(tB.t[:, 0:16 * NS], 16)
        R0 = [s0S.b(), sm["rc1"].b(), sm["rs1"].b()]
        k.tt("dve", a3, s0r, rc, MUL, R0, [tA.b()])
        k.tt("dve", b3, s0i, rs_, MUL, R0, [tB.b()])
        k.tt("dve", a3, a3, b3, SUB, [tA.b(), tB.b()], [tA.b()])
        k.tt("dve", s1r, a3, Xre, ADD, [tA.b(), ps[bre].b()], [s1S.b()])
        k.tt("dve", a3, s0r, rs_, MUL, R0, [tA.b()])
        k.tt("dve", b3, s0i, rc, MUL, R0, [tB.b()])
        k.tt("dve", a3, a3, b3, ADD, [tA.b(), tB.b()], [tA.b()])
        k.tt("dve", s1i, a3, Xim, ADD, [tA.b(), ps[bim].b()], [s1S.b()])
        k.cp("act", s1b.t[:, 0], s1r, [s1S.b()], [s1b.b()])
        k.cp("act", s1b.t[:, 1], s1i, [s1S.b()], [s1b.b()])
        for s in range(NS):
            for half in range(2):
                hp = slice(64 * half, 64 * half + 64)
                k.dma("sp", O["ssm_s"].t[l][s].rearrange("(p g) n r -> g n p r", g=2)[half], s1S.t[hp, :, s, :], [s1S.b()], [O["ssm_s"].b()],
                      nc_ok=True)
        by = k.bank()
        for c4 in range(4):
            wv, wb = wload((S["ssmw"].t[l][:, 4 * c4:4 * c4 + 4], S["ssmw"].b(l)), [128, 4, 4, 128])
            oc_ = ps[by].t[:, c4 * NS:(c4 + 1) * NS]
            for pl in range(4):
                p = 4 * c4 + pl
                k.mm(oc_, wv[:, pl, 2, :], s1b.t[:, 0, p, :], pl == 0, False, [wb, s1b.b()], [ps[by].b()])
                k.mm(oc_, wv[:, pl, 3, :], s1b.t[:, 1, p, :], False, False, [wb, s1b.b()], [ps[by].b()])
            k.mm(oc_, ddiag.t[:, c4, :], uSb.t[:, c4, :], False, True, [ddiag.b(), uSb.b()], [ps[by].b()])
        k.act(ygS.t[:], v3(ps[by].t[:, 0:4 * NS], 4), AF.Gelu_apprx_tanh, [ps[by].b()], [ygS.b()])
        bg = linS("ssm_w_glu", l, 0, 4, ygS, 4)
        tg = t8S[0]
        k.tt("dve", tg.t[:, 0:4, :], v3(ps[bg].t[:, 0:4 * NS], 4), bc2(pv["bglu"].t[:, :], 4), ADD, [ps[bg].b(), pv["bglu"].b()], [tg.b()])
        k.act(tg.t[:, 0:4, :], tg.t[:, 0:4, :], AF.Sigmoid, [tg.b()], [tg.b()])
        k.tt("dve", ysS.t[:], ygS.t[:], tg.t[:, 0:4, :], MUL, [ygS.b(), tg.b()], [ysS.b()])
        for c4 in range(4):
            for s in range(NS):
                k.dma("sp", convc.t[:, c4, s, :], I["c_conv"].t[l][s][:, c4 * 128:(c4 + 1) * 128].rearrange("j p -> p j"), [I["c_conv"].b()],
                      [convc.b()], nc_ok=True)
        for s in range(NS):
            k.dma("sp", O["conv_s"].t[l][s][0:29, :], I["c_conv"].t[l][s][1:30, :], [I["c_conv"].b()], [O["conv_s"].b()])
            k.dma("sp", O["conv_s"].t[l][s][29].rearrange("(c p) -> p c", p=128), aS.t[:, 0:4, s], [aS.b()], [O["conv_s"].b()], nc_ok=True)
        wj = pv["convw"].t[:, :].rearrange("p (c j) -> p c j", j=31)
        tP = tmp()
        prod = tP.t[:, 0:4 * NS * 30].rearrange("p (c s j) -> p c s j", c=4, s=NS)
        k.tt("dve", prod, convc.t[:], wj[:, :, 0:30].unsqueeze(2).to_broadcast([128, 4, NS, 30]), MUL, [convc.b(), pv["convw"].b()], [tP.b()])
        yc = t8S[1]
        k.ve("dve", lambda e: e.tensor_reduce(out=yc.t[:, 0:4, :], in_=prod, axis=AX.X, op=ADD), [tP.b()], [yc.b()])
        tq = t8S[0]
        k.tt("dve", tq.t[:, 0:4, :], aS.t[:, 0:4, :], wj[:, :, 30:31].to_broadcast([128, 4, NS]), MUL, [aS.b(), pv["convw"].b()], [tq.b()])
        k.tt("dve", yc.t[:, 0:4, :], yc.t[:, 0:4, :], tq.t[:, 0:4, :], ADD, [yc.b(), tq.b()], [yc.b()])
        k.tt("dve", yc.t[:, 0:4, :], yc.t[:, 0:4, :], bc2(pv["convb"].t[:, :], 4), ADD, [yc.b(), pv["convb"].b()], [yc.b()])
        k.cp("act", sqS.t[:, 0:4, :], yc.t[:, 0:4, :], [yc.b()], [sqS.b()])
        k.act(sqS.t[:, 4:8, :], yc.t[:, 0:4, :], AF.Square, [yc.b()], [sqS.b()])
        bm = k.bank()
        for c4 in range(4):
            k.mm(ps[bm].t[:, 0:NS], onesb.t[:], sqS.t[:, c4, :], c4 == 0, c4 == 3, [onesb.b(), sqS.b()], [ps[bm].b()])
        for c4 in range(4):
            k.mm(ps[bm].t[:, NS:2 * NS], onesb.t[:], sqS.t[:, 4 + c4, :], c4 == 0, c4 == 3, [onesb.b(), sqS.b()], [ps[bm].b()])
        k.ts("dve", mnS.t[:], ps[bm].t[:, 0:NS], 1.0 / 512, MUL, [ps[bm].b()], [mnS.b()])
        tv = tmp()
        k.tt("dve", tv.t[:, 0:NS], mnS.t[:], mnS.t[:], MUL, [mnS.b()], [tv.b()])
        k.stt(rsS.t[:], ps[bm].t[:, NS:2 * NS], 1.0 / 512, tv.t[:, 0:NS], MUL, SUB, [ps[bm].b(), tv.b()], [rsS.b()])
        rsqrt_inplace(rsS.t[:], rsS.b(), 1.0)
        k.tt("dve", yc.t[:, 0:4, :], yc.t[:, 0:4, :], mnS.t[:, :].unsqueeze(1).to_broadcast([128, 4, NS]), SUB, [yc.b(), mnS.b()], [yc.b()])
        k.tt("dve", yc.t[:, 0:4, :], yc.t[:, 0:4, :], rsS.t[:, :].unsqueeze(1).to_broadcast([128, 4, NS]), MUL, [yc.b(), rsS.b()], [yc.b()])
        k.tt("dve", yc.t[:, 0:4, :], yc.t[:, 0:4, :], bc2(pv["lng"].t[:, :], 4), MUL, [yc.b(), pv["lng"].b()], [yc.b()])
        k.tt("dve", yc.t[:, 0:4, :], yc.t[:, 0:4, :], bc2(pv["lnb"].t[:, :], 4), ADD, [yc.b(), pv["lnb"].b()], [yc.b()])
        k.act(ycS.t[:], yc.t[:, 0:4, :], AF.Silu, [yc.b()], [ycS.b()])
        for gi in range(3):
            W_, dil = WINS[gi]
            Ok, Ik = O[f"kv{W_}_s"], I[f"kv{W_}"]
            wq = lw("w_in", l, 1536 + 256 * gi, 256)
            wk = lw("w_in", l, 2304 + 256 * gi, 256)
            wv3 = lw("w_in", l, 3072 + 256 * gi, 256)
            bq, bv = k.bank(), k.bank()
            for (wv_, wb_), bank_, c0 in ((wq, bq, 0), (wk, bq, 256), (wv3, bv, 0)):
                for kc in range(8):
                    k.mm(ps[bank_].t[SP, c0:c0 + 256], hS.t[:, kc, :], wv_[:, kc, :], kc == 0, kc == 7, [hS.b(), wb_], [ps[bank_].b()])
            qk_norm_rope(NS, bq, ropeS.t[:, 0:32], ropeS.t[:, 32:64], ropeS.b())
            k.cp("act", vstage.t[SP, :], ps[bv].t[SP, 0:256], [ps[bv].b()], [vstage.b()])
            k.cp("act", vnb.t[:, gi * 256:(gi + 1) * 256], ps[bv].t[SP, 0:256], [ps[bv].b()], [vnb.b()])
            k.dma("sp", Ok.t[l][:, W_ - 1, 0:256], qkr.t[SP, 4:8, :].rearrange("p a b -> p (a b)"), [qkr.b()], [Ok.b()], nc_ok=True)
            k.dma("sp", Ok.t[l][:, W_ - 1, 256:512], vstage.t[SP, :], [vstage.b()], [Ok.b()], nc_ok=True)
            for s in range(NS):
                k.dma("sp", Ok.t[l][s][0:W_ - 1, :], Ik.t[l][s][1:W_, :], [Ik.b()], [Ok.b()])
            tP2 = tmp()
            pr3 = tP2.t[SP, 0:256].rearrange("p (a b) -> p a b", b=64)
            k.tt("dve", pr3, qkr.t[SP, 0:4, :], qkr.t[SP, 4:8, :], MUL, [qkr.b()], [tP2.b()])
            k.ve("dve", lambda e, gi=gi, pr3=pr3: e.tensor_reduce(out=sself.t[:, gi * 4:(gi + 1) * 4], in_=pr3, axis=AX.X, op=ADD),
                 [tP2.b()], [sself.b()])
            bt = k.bank()
            for j in range(2):
                k.tr(ps[bt].t[:, j * NS:(j + 1) * NS], qkr.t[SP, 2 * j:2 * j + 2, :].rearrange("p a b -> p (a b)"), ident.t[SP, SP],
                     [qkr.b(), ident.b()], [ps[bt].b()])
            k.cp("act", qS.t[:, gi, :, :], v3(ps[bt].t[:, 0:2 * NS], 2), [ps[bt].b()], [qS.b()])
        k.act(sself.t[:], sself.t[:], AF.Exp, [sself.b()], [sself.b()], scale=0.125)
        k.tt("dve", pdS.t[:], sself.t[:, :].unsqueeze(2).to_broadcast([NS, 12, NS]), ident.t[SP, SP].unsqueeze(1).to_broadcast([NS, 12, NS]), MUL,
             [sself.b(), ident.b()], [pdS.b()])
        Pg = pT[1]
        KTs = pT[0].t[:, 0:256].rearrange("p (a b) -> p a b", a=2)
        Vb = pTm[0].t[:, 0:256]
        for s in range(NS):
            bo = k.bank()
            for gi in range(3):
                W_, dil = WINS[gi]
                Ik = I[f"kv{W_}"]
                k.dma("sp", xio.t[:, 0:512], Ik.t[l][s][0:W_:dil, :], [Ik.b()], [xio.b()], nc_ok=True)
                bt = k.bank()
                for j in range(2):
                    k.tr(ps[bt].t[:, j * 128:(j + 1) * 128], xio.t[:, j * 128:(j + 1) * 128], ident.t[:], [xio.b(), ident.b()], [ps[bt].b()])
                k.cp("act", KTs, ps[bt].t[:, 0:256].rearrange("p (a b) -> p a b", a=2), [ps[bt].b()], [pT[0].b()])
                k.cp("dve", Vb, xio.t[:, 256:512], [xio.b()], [pTm[0].b()])
                bs = k.bank()
                for hh in range(4):
                    hp = slice(64 * (hh % 2), 64 * (hh % 2) + 64)
                    k.mm(ps[bs].t[:, hh:hh + 1], KTs[hp, hh // 2, :], qS.t[hp, gi, hh // 2, s:s + 1], True, True, [pT[0].b(), qS.b()], [ps[bs].b()])
                k.act(Pg.t[:, 0:4], ps[bs].t[:, 0:4], AF.Exp, [ps[bs].b()], [Pg.b()], scale=0.125)
                for hh in range(4):
                    col = gi * 4 + hh
                    k.mm(ps[bo].t[0:64, col:col + 1], Vb[:, hh * 64:(hh + 1) * 64], Pg.t[:, hh:hh + 1], True, False, [pTm[0].b(), Pg.b()], [ps[bo].b()])
                    k.mm(ps[bo].t[0:64, col:col + 1], vnb.t[:, col * 64:(col + 1) * 64], pdS.t[:, col, s:s + 1], False, True, [vnb.b(), pdS.b()],
                         [ps[bo].b()])
                for hh in range(4):
                    col = gi * 4 + hh
                    k.mm(ps[bo].t[0:64, 12 + col:13 + col], onesb.t[:, 0:64], Pg.t[:, hh:hh + 1], True, False, [onesb.b(), Pg.b()], [ps[bo].b()])
                    k.mm(ps[bo].t[0:64, 12 + col:13 + col], onesb.t[SP, 0:64], pdS.t[:, col, s:s + 1], False, True, [onesb.b(), pdS.b()],
                         [ps[bo].b()])
            k.cp("dve", oacc.t[:], ps[bo].t[0:64, 0:24].rearrange("p (o g h) -> p o g h", o=2, g=3), [ps[bo].b()], [oacc.b()])
            tA2 = tmp()
            a2 = tA2.t[0:64, 0:8].rearrange("p (o h) -> p o h", o=2)
            k.tt("dve", a2, oacc.t[:, :, 0, :], oacc.t[:, :, 1, :], ADD, [oacc.b()], [tA2.b()])
            k.tt("dve", a2, a2, oacc.t[:, :, 2, :], ADD, [tA2.b(), oacc.b()], [tA2.b()])
            k.ve("dve", lambda e, a2=a2: e.reciprocal(out=a2[:, 1, :], in_=a2[:, 1, :]), [tA2.b()], [tA2.b()])
            k.tt("dve", yaS.t[:, :, s], a2[:, 0, :], a2[:, 1, :], MUL, [tA2.b()], [yaS.b()])
        for m in range(8):
            sv, sbf = lw("w_br_ssm", l, m * 128, 128, nk=4)
            cv, cbf = lw("w_br_conv", l, m * 128, 128, nk=4)
            av, abf = wload((S["w_br_attn"].t[l][:, m * 128:(m + 1) * 128].rearrange("(hh p) n -> p hh n", p=64), S["w_br_attn"].b(l)), [64, 4, 128])
            gts = [lw("w_gate", l, j * 1024 + m * 128, 128) for j in range(3)]
            bb3 = []
            for wv_, wb_, rh, nk_ in ((sv, sbf, ysS, 4), (cv, cbf, ycS, 4)):
                bk = k.bank()
                for kc in range(nk_):
                    k.mm(ps[bk].t[:, 0:NS], wv_[:, kc, :], rh.t[:, kc, :], kc == 0, kc == nk_ - 1, [wb_, rh.b()], [ps[bk].b()])
                bb3.append(bk)
            bk = k.bank()
            for hh in range(4):
                k.mm(ps[bk].t[:, 0:NS], av[:, hh, :], yaS.t[:, hh, :], hh == 0, hh == 3, [abf, yaS.b()], [ps[bk].b()])
            bb3.append(bk)
            for j, bb in enumerate(bb3):
                bg_ = k.bank()
                for kc in range(8):
                    k.mm(ps[bg_].t[:, 0:NS], gts[j][0][:, kc, :], hS.t[:, kc, :], kc == 0, kc == 7, [gts[j][1], hS.b()], [ps[bg_].b()])
                sg = tmp()
                k.act(sg.t[:, 0:NS], ps[bg_].t[:, 0:NS], AF.Sigmoid, [ps[bg_].b(), pv["bgate"].b()], [sg.b()], bias=pv["bgate"].t[:, 8 * j + m:8 * j + m + 1])
                if j == 0:
                    k.tt("dve", mnS.t[:], ps[bb].t[:, 0:NS], sg.t[:, 0:NS], MUL, [ps[bb].b(), sg.b()], [mnS.b()])
                else:
                    k.tt("dve", rsS.t[:], ps[bb].t[:, 0:NS], sg.t[:, 0:NS], MUL, [ps[bb].b(), sg.b()], [rsS.b()])
                    if j == 1:
                        k.tt("dve", mnS.t[:], mnS.t[:], rsS.t[:], ADD, [mnS.b(), rsS.b()], [mnS.b()])
                    else:
                        k.tt("dve", mgS.t[:, m, :], mnS.t[:], rsS.t[:], ADD, [mnS.b(), rsS.b()], [mgS.b()])
        bo_ = linS("w_out", l, 0, 8, mgS, 8)
        residS(bo_, 16)
        norm_modS("gffn", 32, 24)
        bu = linS("ffn_w_up", l, 0, 44, hS, 8)
        k.cp("act", upS.t[:], v3(ps[bu].t[:, 0:44 * NS], 44), [ps[bu].b()], [upS.b()])
        for s in range(NS):
            for j in range(2):
                k.dma("sp", ffnc.t[:, :, s, j], I["c_ffn"].t[l][s][j].rearrange("(c p) -> p c", p=128), [I["c_ffn"].b()], [ffnc.b()], nc_ok=True)
            k.dma("sp", O["ffn_s"].t[l][s][0:1, :], I["c_ffn"].t[l][s][1:2, :], [I["c_ffn"].b()], [O["ffn_s"].b()])
            k.dma("sp", O["ffn_s"].t[l][s][1].rearrange("(c p) -> p c", p=128), upS.t[:, :, s], [upS.b()], [O["ffn_s"].b()], nc_ok=True)
        fw = pv["fcw"].t[:, :].rearrange("p (c j) -> p c j", j=3)
        bcj = lambda j: fw[:, :, j:j + 1].to_broadcast([128, 44, NS])
        tA3 = tmp()
        a4 = v3(tA3.t[:, 0:44 * NS], 44)
        RW = [pv["fcw"].b()]
        k.tt("dve", ucS.t[:], upS.t[:], bcj(2), MUL, [upS.b()] + RW, [ucS.b()])
        k.tt("dve", a4, ffnc.t[:, :, :, 1], bcj(1), MUL, [ffnc.b()] + RW, [tA3.b()])
        k.tt("dve", ucS.t[:], ucS.t[:], a4, ADD, [ucS.b(), tA3.b()], [ucS.b()])
        k.tt("dve", a4, ffnc.t[:, :, :, 0], bcj(0), MUL, [ffnc.b()] + RW, [tA3.b()])
        k.tt("dve", ucS.t[:], ucS.t[:], a4, ADD, [ucS.b(), tA3.b()], [ucS.b()])
        k.tt("dve", ucS.t[:], ucS.t[:], bc2(pv["fcb"].t[:, :], 44), ADD, [ucS.b(), pv["fcb"].b()], [ucS.b()])
        k.act(a4[:, 0:22, :], ucS.t[:, 0:22, :], AF.Gelu_apprx_tanh, [ucS.b()], [tA3.b()])
        k.tt("dve", acS.t[:], a4[:, 0:22, :], ucS.t[:, 22:44, :], MUL, [tA3.b(), ucS.b()], [acS.b()])
        bd = k.bank()
        for m in range(8):
            for hf in range(2):
                wv, wb = lw("ffn_w_down", l, m * 128, 128, nk=11, row0=hf * 1408)
                for kc in range(11):
                    k.mm(ps[bd].t[:, m * NS:(m + 1) * NS], wv[:, kc, :], acS.t[:, hf * 11 + kc, :], hf == 0 and kc == 0, hf == 1 and kc == 10,
                         [wb, acS.b()], [ps[bd].b()])
        residS(bd, 40)
        if l == L - 1:
            for s in range(NS):
                k.dma("sp", O["ys"].t[s].rearrange("(c p) -> p c", p=128), xS.t[:, :, s], [xS.b()], [O["ys"].b()], nc_ok=True)
    k.sample_layer = sample_layer

    def block(l, b):
        last = (b == NB - 1)
        load_x(l, b)
        norm_mod(pv["gs1"], 0)
        proj_a(l, b)
        qkv(l, b)
        attention(b)
        ssm(l, b, last)
        convbr(l, b, last)
        merge(l)
        if getattr(k, "debug", False) and last:
            for nm, tl, np_ in (("d_yssm", yssm, 128), ("d_yconv", yconv, 128), ("d_yattn", yattn, 64), ("d_yg", yg, 128)):
                k.dma("pool", k.dbg[nm].t[:], tl.t[:], tl.all(), [k.dbg[nm].b()])
            k.dma("pool", k.dbg["d_merged"].t[:], actT.t[:, 0:8, :], [actT.b(i) for i in range(8)], [k.dbg["d_merged"].b()])
            k.dma("sp", k.dbg["d_x"].t[:], x.t[:], x.all(), [k.dbg["d_x"].b()])
            k.dma("sp", k.dbg["d_acc"].t[:], xio.t[0:64, :], [xio.b()], [k.dbg["d_acc"].b()])
        ffn(l, b, last)
        store_x(l, b)
    k.block = block
    k.locals = dict(locals())
    return k


def host_consts(SEQ):
    NB = SEQ // TB
    c = {}
    c["c_ident"] = np.eye(128, dtype=np.float32)
    kk = np.arange(128)[:, None]
    qq = np.arange(128)[None, :]
    same = (kk % 4) == (qq % 4)
    m = np.stack([kk <= qq, kk >= qq, same & ((kk // 4) <= (qq // 4)), same, same & ((kk // 4) >= (qq // 4))]).astype(np.float32)
    c["c_masks"] = m
    half = 32
    inv = (10000.0 ** (-np.arange(half, dtype=np.float32) / half)).astype(np.float32)
    def tab(pos):
        ang = pos.astype(np.float32)[..., None] * inv
        return np.concatenate([np.cos(ang), np.sin(ang)], axis=-1).astype(np.float32)
    p = np.arange(128)
    rope = np.zeros((NB, 128, 12, 64), np.float32)
    for b in range(NB):
        t0 = b * TB
        for s in range(4):
            rope[b, :, 0 * 4 + s] = tab(t0 + 128 * s + p)
            rope[b, :, 1 * 4 + s] = tab(t0 + 4 * p + s)
            rope[b, :, 2 * 4 + s] = tab(t0 + 4 * p + s)
    c["c_rope"] = rope
    c["c_rope_s"] = tab(np.array([PAST]))
    c["c_iota"] = np.arange(512, dtype=np.float32)
    return c


WNAMES = ["w_ada", "b_ada", "g_norm_mix", "w_in", "ssm_a_re", "ssm_a_im", "ssm_log_dt", "ssm_b_re", "ssm_b_im", "ssm_c_re",
          "ssm_c_im", "ssm_d", "ssm_w_glu", "ssm_b_glu", "conv_w", "conv_b", "conv_ln_g", "conv_ln_b", "attn_gq", "attn_gk",
          "w_gate", "b_gate", "w_br_ssm", "w_br_conv", "w_br_attn", "w_out", "g_norm_ffn", "ffn_w_up", "ffn_conv_w",
          "ffn_conv_b", "ffn_w_down"]


def make_in_maps(inp, SEQ, DEPTH, NS, ncores):
    consts = host_consts(SEQ)
    maps = []
    f = lambda a: np.ascontiguousarray(a, dtype=np.float32)
    for c in range(ncores):
        b = c % inp["x_prompt"].shape[0]
        ss = slice(c * NS, (c + 1) * NS)
        m = dict(consts)
        m["xp"] = f(inp["x_prompt"][b, :SEQ])
        m["cp"] = f(inp["c_prompt"][b:b + 1])
        m["xs"] = f(inp["x_sample"][ss, 0])
        m["cs"] = f(inp["c_sample"][ss])
        m["st_ssm"] = f(inp["state_ssm"][:DEPTH, ss])
        m["c_conv"] = f(inp["cache_conv"][:DEPTH, ss])
        for w, _ in WINS:
            a = inp[f"cache_kv_w{w}"][:DEPTH, ss]
            m[f"kv{w}"] = f(a.reshape(a.shape[0], a.shape[1], a.shape[2], 512))
        m["c_ffn"] = f(inp["cache_ffn"][:DEPTH, ss])
        for n in WNAMES:
            m[n] = f(inp[n][:DEPTH])
        maps.append(m)
    return maps


def program(k):
    k.cast_layer(0)
    for l in range(k.DEPTH):
        k.layer_prep(l)
        if SAMPLE[0]:
            k.sample_layer(l)
        jobs = k.cast_jobs(l + 1) if l + 1 < k.DEPTH else []
        per = -(-len(jobs) // max(1, k.NB - 1)) if jobs else 0
        for b in range(k.NB):
            k.block(l, b)
            if jobs:
                k.cast_some(l + 1, jobs[:per])
                jobs = jobs[per:]
        assert not jobs


def kernel(**inputs):
    SEQ = inputs["x_prompt"].shape[1]
    DEPTH = inputs["w_in"].shape[0]
    NS = 4
    ncores = 8
    k = build(SEQ, DEPTH, NS)
    program(k)
    k.P.emit(k.stack)
    k.stack.close()
    maps = make_in_maps(inputs, SEQ, DEPTH, NS, ncores)
    res = run_bass_kernel_spmd(k.nc, maps, core_ids=list(range(ncores)))
    R = res.results
    B = inputs["x_prompt"].shape[0]
    yp = np.stack([R[b]["yp"] for b in range(B)], axis=0).astype(np.float32)
    ys = np.concatenate([R[c]["ys"] for c in range(ncores)], axis=0)[:, None, :].astype(np.float32)

    def pp(name, tail):
        return np.stack([R[b][name] for b in range(B)], axis=1).reshape((DEPTH, B) + tail).astype(np.float32)

    def sp_(name, tail):
        return np.concatenate([R[c][name] for c in range(ncores)], axis=1).reshape((DEPTH, ncores * NS) + tail).astype(np.float32)

    outs = [yp, ys, pp("ssm_p", (32, 64, 2)), sp_("ssm_s", (32, 64, 2)), pp("conv_p", (30, 512)), sp_("conv_s", (30, 512))]
    for w, _ in WINS:
        outs.append(pp(f"kv{w}_p", (min(w, SEQ), 2, 4, 64)))
        outs.append(sp_(f"kv{w}_s", (w, 2, 4, 64)))
    outs.append(pp("ffn_p", (2, F2)))
    outs.append(sp_("ffn_s", (2, F2)))
    return tuple(outs)
```

```python
import contextlib
import numpy as np
import concourse.bass as bass
import concourse.mybir as mybir
from concourse.bass_utils import run_bass_kernel_spmd

F32 = mybir.dt.float32
BF16 = mybir.dt.bfloat16
AF = mybir.ActivationFunctionType
ALU = mybir.AluOpType
AX = mybir.AxisListType

D = 1024
NCH = 8
TB = 512
SSM_W = 512
CONV_W = 512
CONV_K = 31
FFN_H = 2816
F2 = 2 * FFN_H
IN_W = 3840
EPS = 1e-6
WINS = ((128, 1), (512, 4), (2048, 16))
PAST = 8192
SEM_CAP = 24000
DMA_SLOTS = 8
DEBUG = [False]
TAG = [""]
ATT_GROUPS = [0, 1, 2]
SAMPLE = [True]


class Buf:
    __slots__ = ("name", "w", "r")

    def __init__(self, name):
        self.name = name
        self.w = None
        self.r = {}


class Op:
    __slots__ = ("fn", "waits", "signal", "dma", "sigpos", "tag")

    def __init__(self, fn, dma=None):
        self.fn = fn
        self.waits = []
        self.signal = False
        self.dma = dma
        self.sigpos = None
        self.tag = TAG[0]


class Prog:
    ENGS = ("pe", "act", "dve", "pool", "sp")

    def __init__(self, nc):
        self.nc = nc
        self.ops = {e: [] for e in self.ENGS}
        self.seen = {e: {} for e in self.ENGS}
        self.ndma = {e: 0 for e in self.ENGS}

    def _need(self, eng, tok, op):
        if tok[0] == "E":
            key = ("E", tok[1])
            if self.seen[eng].get(key, -1) >= tok[2]:
                return
            self.seen[eng][key] = tok[2]
            self.ops[tok[1]][tok[2]].signal = True
        else:
            key = ("D", tok[1], tok[2] % DMA_SLOTS)
            if self.seen[eng].get(key, -1) >= tok[2]:
                return
            self.seen[eng][key] = tok[2]
        op.waits.append(tok)

    def add(self, eng, fn, R=(), W=(), dma=False):
        ops = self.ops[eng]
        idx = len(ops)
        if dma:
            n = self.ndma[eng]
            self.ndma[eng] += 1
            op = Op(fn, dma=n)
            me = ("D", eng, n)
            if n >= DMA_SLOTS:
                self._need(eng, ("D", eng, n - DMA_SLOTS), op)
        else:
            op = Op(fn)
            me = ("E", eng, idx)
        deps = []
        for b in R:
            if b.w is not None:
                deps.append((b.w, True))
        for b in W:
            if b.w is not None:
                deps.append((b.w, False))
            for t in b.r.values():
                deps.append((t, False))
        for t, raw in deps:
            if t[0] == "E" and t[1] == eng and not dma:
                if eng == "pe" or not raw:
                    continue
            if t == me:
                continue
            self._need(eng, t, op)
        ops.append(op)
        for b in W:
            b.w = me
            b.r = {}
        for b in R:
            b.r[(me[0], me[1])] = me
        return me

    def emit(self, stack):
        nc = self.nc
        nsem = {}
        for e in self.ENGS:
            c = 0
            for op in self.ops[e]:
                if op.signal:
                    op.sigpos = c
                    c += 1
            nsem[e] = max(1, -(-c // SEM_CAP))
        sems = {e: [stack.enter_context(nc.semaphore(f"s_{e}_{i}")) for i in range(nsem[e])] for e in self.ENGS}
        dsems = {e: [stack.enter_context(nc.semaphore(f"d_{e}_{i}")) for i in range(DMA_SLOTS)]
                 for e in self.ENGS if self.ndma[e] > 0}
        block = stack.enter_context(nc.Block())
        hooks = {"pe": block.tensor, "act": block.scalar, "dve": block.vector, "pool": block.gpsimd, "sp": block.sync}

        def body(e):
            def run(h):
                for op in self.ops[e]:
                    for t in op.waits:
                        if t[0] == "E":
                            pos = self.ops[t[1]][t[2]].sigpos
                            h.wait_ge(sems[t[1]][pos // SEM_CAP], pos % SEM_CAP + 1)
                        else:
                            h.wait_ge(dsems[t[1]][t[2] % DMA_SLOTS], 16 * (t[2] // DMA_SLOTS + 1))
                    ins = op.fn(h)
                    if op.dma is not None:
                        ins.then_inc(dsems[e][op.dma % DMA_SLOTS], 16)
                    elif op.signal:
                        ins.then_inc(sems[e][op.sigpos // SEM_CAP], 1)
                n = self.ndma[e]
                for s in range(min(n, DMA_SLOTS)):
                    last = ((n - 1 - s) // DMA_SLOTS) * DMA_SLOTS + s
                    h.wait_ge(dsems[e][s], 16 * (last // DMA_SLOTS + 1))
            return run

        for e in self.ENGS:
            hooks[e](body(e))


class Tile:
    def __init__(self, t, nsub=1):
        self.t = t
        self.bufs = [Buf(None) for _ in range(nsub)]

    def b(self, i=0):
        return self.bufs[i]

    def all(self):
        return self.bufs


class K:
    def __init__(self, SEQ, DEPTH, NS):
        self.SEQ, self.DEPTH, self.NS = SEQ, DEPTH, NS
        self.NB = SEQ // TB
        self.nc = bass.Bass("TRN2", target_bir_lowering=False)
        self.P = Prog(self.nc)
        self.stack = contextlib.ExitStack()
        self.bank_rr = 0

    def din(self, name, shape, dt=F32):
        return self.nc.dram_tensor(name, list(shape), dt, kind="ExternalInput").ap()

    def dout(self, name, shape, dt=F32):
        return self.nc.dram_tensor(name, list(shape), dt, kind="ExternalOutput").ap()

    def dscr(self, name, shape, dt, nsub=1):
        return Tile(self.nc.dram_tensor(name, list(shape), dt, kind="Internal").ap(), nsub)

    def sb(self, name, shape, dt, nsub=1):
        return Tile(self.stack.enter_context(self.nc.sbuf_tensor(name, list(shape), dt)), nsub)

    def mm(self, out, lhsT, rhs, start, stop, R, W):
        self.P.add("pe", lambda h: h.matmul(out, lhsT, rhs, start=start, stop=stop, skip_group_check=True), R, W)

    def tr(self, out, in_, ident, R, W):
        self.P.add("pe", lambda h: h.transpose(out, in_, ident), R, W)

    def act(self, out, in_, func, R, W, scale=None, bias=None):
        kw = {}
        if scale is not None:
            kw["scale"] = scale
        if bias is not None:
            kw["bias"] = bias
        self.P.add("act", lambda h: h.activation(out=out, in_=in_, func=func, **kw), R, W)

    def ve(self, eng, fn, R, W):
        self.P.add(eng, fn, R, W)

    def tt(self, eng, out, a, b, op, R, W):
        self.P.add(eng, lambda h: h.tensor_tensor(out=out, in0=a, in1=b, op=op), R, W)

    def ts(self, eng, out, a, s1, op0, R, W, s2=None, op1=None):
        if op1 is None:
            self.P.add(eng, lambda h: h.tensor_scalar(out=out, in0=a, scalar1=s1, scalar2=None, op0=op0), R, W)
        else:
            self.P.add(eng, lambda h: h.tensor_scalar(out=out, in0=a, scalar1=s1, scalar2=s2, op0=op0, op1=op1), R, W)

    def stt(self, out, a, s, b, op0, op1, R, W):
        self.P.add("dve", lambda h: h.scalar_tensor_tensor(out=out, in0=a, scalar=s, in1=b, op0=op0, op1=op1), R, W)

    def cp(self, eng, out, in_, R, W):
        if eng == "act":
            self.P.add("act", lambda h: h.activation(out=out, in_=in_, func=AF.Identity), R, W)
        else:
            self.P.add(eng, lambda h: h.tensor_copy(out=out, in_=in_), R, W)

    def dma(self, q, out, in_, R, W, nc_ok=False):
        nc = self.nc
        if nc_ok:
            def fn(h):
                with nc.allow_non_contiguous_dma(reason="small strided parameter/state transfer"):
                    return h.dma_start(out=out, in_=in_)
        else:
            def fn(h):
                return h.dma_start(out=out, in_=in_)
        self.P.add(q, fn, R, W, dma=True)

    def bank(self):
        i = self.pool_banks[self.bank_rr % len(self.pool_banks)]
        self.bank_rr += 1
        return i


def rr(ap, s, **kw):
    return ap.rearrange(s, **kw)


def build(SEQ, DEPTH, NS, with_sample=True):
    k = K(SEQ, DEPTH, NS)
    nc, P, NB = k.nc, k.P, k.NB
    L = DEPTH
    I = {}
    def inp(name, shape):
        I[name] = Tile(k.din(name, shape))
        return I[name]
    inp("xp", [SEQ, D]); inp("cp", [1, D]); inp("xs", [NS, D]); inp("cs", [NS, D])
    inp("st_ssm", [L, NS, 32, 64, 2]); inp("c_conv", [L, NS, 30, 512])
    for w, _ in WINS:
        inp(f"kv{w}", [L, NS, w, 512])
    inp("c_ffn", [L, NS, 2, F2])
    wshapes = dict(w_ada=[D, 6 * D], b_ada=[6 * D], g_norm_mix=[D], w_in=[D, IN_W], ssm_a_re=[32, 64], ssm_a_im=[32, 64],
                   ssm_log_dt=[32], ssm_b_re=[32, 64, 16], ssm_b_im=[32, 64, 16], ssm_c_re=[32, 16, 64],
                   ssm_c_im=[32, 16, 64], ssm_d=[512], ssm_w_glu=[512, 512], ssm_b_glu=[512], conv_w=[31, 512],
                   conv_b=[512], conv_ln_g=[512], conv_ln_b=[512], attn_gq=[64], attn_gk=[64], w_gate=[D, 3 * D],
                   b_gate=[3 * D], w_br_ssm=[512, D], w_br_conv=[512, D], w_br_attn=[256, D], w_out=[D, D],
                   g_norm_ffn=[D], ffn_w_up=[D, F2], ffn_conv_w=[3, F2], ffn_conv_b=[F2], ffn_w_down=[FFN_H, D])
    for n, s in wshapes.items():
        inp(n, [L] + s)
    inp("c_ident", [128, 128]); inp("c_masks", [5, 128, 128]); inp("c_rope", [NB, 128, 12, 64]); inp("c_rope_s", [1, 64])
    inp("c_iota", [512])
    O = {}
    def outp(name, shape):
        O[name] = Tile(k.dout(name, shape))
        return O[name]
    outp("yp", [SEQ, D]); outp("ys", [NS, D]); outp("ssm_p", [L, 32, 64, 2]); outp("ssm_s", [L, NS, 32, 64, 2])
    outp("conv_p", [L, 30, 512]); outp("conv_s", [L, NS, 30, 512])
    for w, _ in WINS:
        outp(f"kv{w}_p", [L, min(w, SEQ), 512]); outp(f"kv{w}_s", [L, NS, w, 512])
    outp("ffn_p", [L, 2, F2]); outp("ffn_s", [L, NS, 2, F2])
    k.dbg = {}
    if DEBUG[0]:
        k.debug = True
        for nm, shp in (("d_yssm", [128, 4, TB]), ("d_yconv", [128, 4, TB]), ("d_yattn", [64, 4, TB]), ("d_yg", [128, 4, TB]),
                        ("d_merged", [128, 8, TB]), ("d_x", [128, 8, TB]), ("d_acc", [64, 1024])):
            k.dbg[nm] = Tile(k.dout(nm, shp))
    big = ["w_ada", "w_in", "ssm_w_glu", "w_gate", "w_br_ssm", "w_br_conv", "w_br_attn", "w_out", "ffn_w_up", "ffn_w_down"]
    S = {n: k.dscr("s_" + n, [L] + wshapes[n], BF16, nsub=L) for n in big}
    S["cdiag"] = k.dscr("s_cdiag", [L, 4, 128, 32, 128], BF16, nsub=L)
    S["ssmw"] = k.dscr("s_ssmw", [L, 128, 16, 4, 128], BF16, nsub=L)
    S["rot"] = k.dscr("s_rot", [L, 128, 16, 2, TB], BF16, nsub=L)
    xscr = k.dscr("s_x", [2, NB, 128, NCH, TB], F32, nsub=2 * NB)
    k.I, k.O, k.S = I, O, S

    sb = k.sb
    ps = [Tile(k.stack.enter_context(nc.psum_tensor(f"ps{i}", [128, 512], F32))) for i in range(8)]
    k.ps = ps
    k.pool_banks = list(range(8))
    ident = sb("ident", [128, 128], F32)
    identb = sb("identb", [128, 128], BF16)
    onesb = sb("onesb", [128, 128], BF16)
    masks = sb("masks", [128, 5, 128], BF16)
    iota = sb("iota", [128, 512], F32)
    epsc = sb("epsc", [128, 1], F32)
    NWS, WSZ = 6, 2048
    wsl = [sb(f"wsl{i}", [128, WSZ], BF16) for i in range(NWS)]
    wrr = [0]

    def wload(src, shape):
        t = wsl[wrr[0] % NWS]
        wrr[0] += 1
        n = int(np.prod(shape[1:]))
        assert n <= WSZ
        view = t.t[0:shape[0], 0:n]
        if len(shape) == 3:
            view = view.rearrange("p (a b) -> p a b", b=shape[2])
        elif len(shape) == 4:
            view = view.rearrange("p (a b c) -> p a b c", b=shape[2], c=shape[3])
        k.dma("sp", view, src[0], [src[1]], [t.b()], nc_ok=True)
        return view, t.b()

    x = sb("x", [128, NCH, TB], F32, nsub=NCH)
    h = sb("h", [128, NCH, TB], BF16, nsub=NCH)
    sqb = [sb(f"sqb{i}", [128, TB], BF16) for i in range(3)]
    rstd = sb("rstd", [128, TB], F32)
    mean = sb("mean", [128, TB], F32)
    tmpf = [sb(f"tmpf{i}", [128, TB], F32) for i in range(4)]
    tf = [0]

    def tmp():
        t = tmpf[tf[0] % len(tmpf)]
        tf[0] += 1
        return t

    sq_i = [0]

    def sqt():
        t = sqb[sq_i[0] % len(sqb)]
        sq_i[0] += 1
        return t

    uT = sb("uT", [128, 4, TB], BF16, nsub=4)
    aT = sb("aT", [128, 4, 30 + TB], BF16, nsub=4)
    qT = sb("qT", [128, 12, 2, 128], BF16, nsub=12)
    NSL = (8, 8, 20)
    kT = [sb(f"kT{g}", [128, NSL[g], 2, 128], BF16, nsub=NSL[g]) for g in range(3)]
    vR = [sb(f"vR{g}", [128, NSL[g], 256], BF16, nsub=NSL[g]) for g in range(3)]
    ropeT = sb("ropeT", [128, 12, 64], F32)
    gqk = sb("gqk", [128, 2, 64], F32)
    qkn = sb("qkn", [128, 8, 64], F32)
    qkr = sb("qkr", [128, 8, 64], F32)
    vstage = sb("vstage", [128, 256], F32)
    ssq = sb("ssq", [128, 8], F32)
    yssm = sb("yssm", [128, 4, TB], BF16, nsub=4)
    yg = sb("yg", [128, 4, TB], BF16, nsub=4)
    yconv = sb("yconv", [128, 4, TB], BF16, nsub=4)
    ycf = sb("ycf", [128, 4, TB], BF16, nsub=4)
    yattn = sb("yattn", [64, 4, TB], BF16, nsub=4)
    actT = sb("actT", [128, 22, TB], BF16, nsub=22)
    upsb = [sb(f"upsb{i}", [128, 2 + TB], F32) for i in range(2)]
    uphalo = sb("uphalo", [128, 44, 2], F32)
    pT = [sb(f"pT{i}", [128, 5 * 128], BF16) for i in range(2)]
    pTm = [sb(f"pTm{i}", [128, 5 * 128], BF16) for i in range(2)]
    xio = sb("xio", [128, D], F32)
    ddiag = sb("ddiag", [128, 4, 128], BF16)
    sm = {n: sb("sm_" + n, [128, 16], F32) for n in
          ["are", "aim", "dt", "th", "r", "c1", "s1", "c511", "s511", "c512", "s512", "kre", "kim", "t0", "t1", "t2", "t3",
           "rc1", "rs1", "den"]}
    vin = sb("vin", [128, 16, 2], F32)
    vend = sb("vend", [128, 16, 2], F32)
    sstate = sb("sstate", [128, 16, 2], F32)
    v0 = [sb(f"v0_{i}", [128, TB], F32) for i in range(2)]
    vv = [sb(f"vv_{i}", [128, TB], F32) for i in range(2)]
    sre = [sb(f"sre{i}", [128, TB], BF16) for i in range(2)]
    sim = [sb(f"sim{i}", [128, TB], BF16) for i in range(2)]
    pv = {}
    for n, c in [("gmix", 8), ("gffn", 8), ("bada", 48), ("convw", 4 * 31), ("convb", 4), ("lng", 4), ("lnb", 4),
                 ("bglu", 4), ("bgate", 24), ("fcw", 44 * 3), ("fcb", 44), ("dsk", 4), ("gs1", 8), ("gs2", 8)]:
        pv[n] = sb("pv_" + n, [128, c], F32)
    NCOL = 1 + NS
    cT = sb("cT", [128, NCH, NCOL], F32)
    cTb = sb("cTb", [128, NCH, NCOL], BF16)
    modT = sb("modT", [128, 48, NCOL], F32)
    def tview(t, s, **kw):
        v = Tile(t.t[:, :].rearrange(s, **kw), 1)
        v.bufs = t.bufs
        return v
    def aview(c0, s, **kw):
        v = Tile(actT.t[:, c0:c0 + 2, :].rearrange("p a b -> p (a b)").bitcast(F32).rearrange(s, **kw), 1)
        v.bufs = [actT.b(c0)]
        return v
    braw = aview(0, "p (a b c) -> p a b c", a=2, b=16)
    bbar = aview(2, "p (a b c) -> p a b c", a=2, b=16)
    craw = aview(4, "p (a b c) -> p a b c", a=2, b=16)
    zst = aview(6, "p (a b) -> p a b", a=4)
    wst = sb("wst", [128, 4, 128], BF16)
    cdg = sb("cdg", [128, 8, 128], BF16)
    rotst = sb("rotst", [128, 2, TB], BF16)
    xS = sb("xS", [128, NCH, NS], F32)
    hS = sb("hS", [128, NCH, NS], BF16)
    zS = sb("zS", [128, 30, NS], F32)
    sqS = sb("sqS", [128, NCH, NS], BF16)
    rsS = sb("rsS", [128, NS], F32)
    mnS = sb("mnS", [128, NS], F32)
    t8S = [sb(f"t8S{i}", [128, NCH, NS], F32) for i in range(3)]
    uSb = sb("uSb", [128, 4, NS], BF16)
    ygS = sb("ygS", [128, 4, NS], BF16)
    ysS = sb("ysS", [128, 4, NS], BF16)
    ycS = sb("ycS", [128, 4, NS], BF16)
    yaS = sb("yaS", [64, 4, NS], BF16)
    mgS = sb("mgS", [128, NCH, NS], BF16)
    upS = Tile(iota.t[:, 256:256 + 44 * NS].rearrange("p (c s) -> p c s", s=NS), 1)
    upS.bufs = iota.bufs
    ucS = sb("ucS", [128, 44, NS], F32)
    acS = sb("acS", [128, 22, NS], BF16)
    s0S = Tile(iota.t[:, 0:128].rearrange("p (a s r) -> p a s r", a=16, s=NS), 1)
    s1S = Tile(iota.t[:, 128:256].rearrange("p (a s r) -> p a s r", a=16, s=NS), 1)
    s0S.bufs = iota.bufs
    s1S.bufs = iota.bufs
    s1b = sb("s1b", [128, 2, 16, NS], BF16)
    qS = sb("qS", [128, 3, 2, NS], BF16)
    ropeS = sb("ropeS", [NS, 64], F32)
    sself = sb("sself", [NS, 12], F32)
    pdS = sb("pdS", [NS, 12, NS], BF16)
    oacc = sb("oacc", [64, 2, 3, 4], F32)
    convc = Tile(rotst.t[:, :, :].rearrange("p a b -> p (a b)").bitcast(F32)[:, 0:4 * NS * 30].rearrange("p (c s j) -> p c s j", c=4, s=NS), 1)
    convc.bufs = rotst.bufs
    ffnc = Tile(cdg.t[:, :, :].rearrange("p a b -> p (a b)").bitcast(F32)[:, 0:44 * NS * 2].rearrange("p (c s j) -> p c s j", c=44, s=NS), 1)
    ffnc.bufs = cdg.bufs
    vnb = Tile(actT.t[:, 8:10, :].rearrange("p a b -> p (a b)")[0:NS, 0:768], 1)
    vnb.bufs = [actT.b(8)]

    k.dma("sp", ident.t[:], I["c_ident"].t[:, :], [I["c_ident"].b()], [ident.b()])
    k.cp("dve", identb.t[:], ident.t[:], [ident.b()], [identb.b()])
    k.ve("pool", lambda e: e.memset(onesb.t[:], 1.0), [], [onesb.b()])
    k.ve("pool", lambda e: e.memset(epsc.t[:], EPS), [], [epsc.b()])
    k.dma("pool", masks.t[:], rr(I["c_masks"].t, "m p q -> p m q"), [I["c_masks"].b()], [masks.b()], nc_ok=True)
    def cast_jobs(l):
        jobs = []
        for n in big:
            rows = wshapes[n][0]
            step = max(1, rows // 4)
            for r0 in range(0, rows, step):
                jobs.append((n, r0, step))
        return jobs

    def cast_some(l, jobs):
        for n, r0, step in jobs:
            k.dma("pool", S[n].t[l][r0:r0 + step], I[n].t[l][r0:r0 + step], [I[n].b()], [S[n].b(l)])
    k.cast_jobs, k.cast_some = cast_jobs, cast_some

    def cast_layer(l):
        cast_some(l, cast_jobs(l))
    k.cast_layer = cast_layer
    PI = float(np.pi)
    TWO_PI = float(2 * np.pi)

    def sincos(dst_sin, dst_cos, ang, shape, bufs_r, bw_sin, bw_cos, eng="dve"):
        t_k = tmp(); t_y = tmp()
        n = int(np.prod(shape[1:]))
        ki = t_k.t[:, 0:n].bitcast(mybir.dt.int32)
        kf = t_k.t[:, 0:n]
        y = t_y.t[:, 0:n]
        a2 = ang if len(shape) == 2 else ang.rearrange("p a b -> p (a b)")
        k.ts(eng, y, a2, 1.0 / TWO_PI, ALU.mult, bufs_r, [t_y.b()])
        k.cp(eng, ki, y, [t_y.b()], [t_k.b()])
        k.cp(eng, y, ki, [t_k.b()], [t_y.b()])
        k.stt(kf, y, -TWO_PI, a2, ALU.mult, ALU.add, [t_y.b()] + bufs_r, [t_k.b()])
        for shift, dst, bw in ((0.0, dst_sin, bw_sin), (PI / 2, dst_cos, bw_cos)):
            if dst is None:
                continue
            d2 = dst if len(shape) == 2 else dst.rearrange("p a b -> p (a b)")
            k.ts(eng, y, kf, shift, ALU.add, [t_k.b()], [t_y.b()])
            t_m = tmp(); m = t_m.t[:, 0:n]
            k.ts(eng, m, y, PI, ALU.is_gt, [t_y.b()], [t_m.b()])
            k.stt(y, m, -TWO_PI, y, ALU.mult, ALU.add, [t_m.b(), t_y.b()], [t_y.b()])
            k.ts(eng, m, y, -PI, ALU.is_lt, [t_y.b()], [t_m.b()])
            k.stt(y, m, TWO_PI, y, ALU.mult, ALU.add, [t_m.b(), t_y.b()], [t_y.b()])
            k.act(d2, y, AF.Sin, [t_y.b()], [bw])

    def pvload(name, src_ap, srcbuf, pat, **kw):
        k.dma("sp", pv[name].t[:], src_ap.rearrange(pat, **kw), [srcbuf], [pv[name].b()], nc_ok=True)

    def layer_prep(l):
        k.dma("sp", iota.t[:], I["c_iota"].t.partition_broadcast(128), [I["c_iota"].b()], [iota.b()], nc_ok=True)
        pvload("gmix", I["g_norm_mix"].t[l], I["g_norm_mix"].b(), "(c p) -> p c", p=128)
        pvload("gffn", I["g_norm_ffn"].t[l], I["g_norm_ffn"].b(), "(c p) -> p c", p=128)
        pvload("bada", I["b_ada"].t[l], I["b_ada"].b(), "(c p) -> p c", p=128)
        for c4 in range(4):
            k.dma("sp", pv["convw"].t[:, c4 * 31:(c4 + 1) * 31], I["conv_w"].t[l][:, c4 * 128:(c4 + 1) * 128].rearrange("j p -> p j"),
                  [I["conv_w"].b()], [pv["convw"].b()], nc_ok=True)
        pvload("convb", I["conv_b"].t[l], I["conv_b"].b(), "(c p) -> p c", p=128)
        pvload("lng", I["conv_ln_g"].t[l], I["conv_ln_g"].b(), "(c p) -> p c", p=128)
        pvload("lnb", I["conv_ln_b"].t[l], I["conv_ln_b"].b(), "(c p) -> p c", p=128)
        pvload("bglu", I["ssm_b_glu"].t[l], I["ssm_b_glu"].b(), "(c p) -> p c", p=128)
        pvload("bgate", I["b_gate"].t[l], I["b_gate"].b(), "(c p) -> p c", p=128)
        for j in range(3):
            k.dma("sp", pv["fcw"].t[:].rearrange("p (c j) -> p c j", j=3)[:, :, j], I["ffn_conv_w"].t[l][j].rearrange("(c p) -> p c", p=128),
                  [I["ffn_conv_w"].b()], [pv["fcw"].b()], nc_ok=True)
        pvload("fcb", I["ffn_conv_b"].t[l], I["ffn_conv_b"].b(), "(c p) -> p c", p=128)
        pvload("dsk", I["ssm_d"].t[l], I["ssm_d"].b(), "(c p) -> p c", p=128)
        k.dma("sp", gqk.t[:, 0, :], I["attn_gq"].t[l].partition_broadcast(128), [I["attn_gq"].b()], [gqk.b()], nc_ok=True)
        k.dma("sp", gqk.t[:, 1, :], I["attn_gk"].t[l].partition_broadcast(128), [I["attn_gk"].b()], [gqk.b()], nc_ok=True)
        if l == 0:
            k.dma("sp", cT.t[:, :, 0], I["cp"].t[0].rearrange("(c p) -> p c", p=128), [I["cp"].b()], [cT.b()], nc_ok=True)
            for s in range(NS):
                k.dma("sp", cT.t[:, :, 1 + s], I["cs"].t[s].rearrange("(c p) -> p c", p=128), [I["cs"].b()], [cT.b()], nc_ok=True)
            k.act(cTb.t[:], cT.t[:], AF.Silu, [cT.b()], [cTb.b()])
        for cg in range(24):
            wv, wb = wload((S["w_ada"].t[l][:, cg * 256:(cg + 1) * 256].rearrange("(kc p) n -> p kc n", p=128), S["w_ada"].b(l)),
                           [128, 8, 256])
            for mi in range(2):
                m = cg * 2 + mi
                bk = k.bank()
                for kc in range(8):
                    k.mm(ps[bk].t[:, 0:NCOL], wv[:, kc, mi * 128:(mi + 1) * 128], cTb.t[:, kc, :], kc == 0, kc == 7,
                         [wb, cTb.b()], [ps[bk].b()])
                k.ts("dve", modT.t[:, m, :], ps[bk].t[:, 0:NCOL], pv["bada"].t[:, m:m + 1], ALU.add,
                     [ps[bk].b(), pv["bada"].b()], [modT.b()])
        for nm, g, c0 in (("gs1", "gmix", 8), ("gs2", "gffn", 32)):
            k.stt(pv[nm].t[:], modT.t[:, c0:c0 + 8, 0], 1.0, pv[g].t[:], ALU.add, ALU.mult,
                  [modT.b(), pv[g].b()], [pv[nm].b()])
        for half in range(2):
            hp = slice(64 * half, 64 * half + 64)
            for nm, src in (("are", "ssm_a_re"), ("aim", "ssm_a_im")):
                k.dma("sp", sm[nm].t[hp, :], I[src].t[l].rearrange("(p g) n -> g n p", g=2)[half], [I[src].b()], [sm[nm].b()],
                      nc_ok=True)
            k.dma("sp", sm["dt"].t[hp, :], I["ssm_log_dt"].t[l].rearrange("(p g) -> g p", g=2)[half].partition_broadcast(64),
                  [I["ssm_log_dt"].b()], [sm["dt"].b()], nc_ok=True)
            for nm, src, tl in (("b", "ssm_b_re", braw), ("b", "ssm_b_im", braw), ("c", "ssm_c_re", craw), ("c", "ssm_c_im", craw)):
                ri = 0 if src.endswith("re") else 1
                if nm == "b":
                    sap = I[src].t[l].rearrange("(p g) n c -> g n p c", g=2)[half]
                    k.dma("sp", tl.t[hp, ri, :, :], sap, [I[src].b()], [tl.b()], nc_ok=True)
                else:
                    for p in range(16):
                        sap = I[src].t[l][2 * p + half].rearrange("c n -> n c")
                        k.dma("sp", tl.t[hp, ri, p, :], sap, [I[src].b()], [tl.b()], nc_ok=True)
        A = lambda n: sm[n].t[:]
        B_ = lambda n: sm[n].b()
        k.act(A("dt"), A("dt"), AF.Exp, [B_("dt")], [B_("dt")])
        k.tt("dve", A("th"), A("aim"), A("dt"), ALU.mult, [B_("aim"), B_("dt")], [B_("th")])
        k.tt("dve", A("t0"), A("are"), A("dt"), ALU.mult, [B_("are"), B_("dt")], [B_("t0")])
        k.act(A("r"), A("t0"), AF.Exp, [B_("t0")], [B_("r")])
        sincos(A("s1"), A("c1"), A("th"), [128, 16], [B_("th")], B_("s1"), B_("c1"))
        for mult_, sn, cn in ((511.0, "s511", "c511"), (512.0, "s512", "c512")):
            k.ts("dve", A("t1"), A("th"), mult_, ALU.mult, [B_("th")], [B_("t1")])
            sincos(A(sn), A(cn), A("t1"), [128, 16], [B_("t1")], B_(sn), B_(cn))
        k.tt("dve", A("rc1"), A("r"), A("c1"), ALU.mult, [B_("r"), B_("c1")], [B_("rc1")])
        k.tt("dve", A("rs1"), A("r"), A("s1"), ALU.mult, [B_("r"), B_("s1")], [B_("rs1")])
        k.ts("dve", A("t0"), A("rc1"), -1.0, ALU.add, [B_("rc1")], [B_("t0")])
        k.tt("dve", A("t1"), A("are"), A("are"), ALU.mult, [B_("are")], [B_("t1")])
        k.tt("dve", A("t2"), A("aim"), A("aim"), ALU.mult, [B_("aim")], [B_("t2")])
        k.tt("dve", A("den"), A("t1"), A("t2"), ALU.add, [B_("t1"), B_("t2")], [B_("den")])
        k.ve("dve", lambda e: e.reciprocal(out=A("den"), in_=A("den")), [B_("den")], [B_("den")])
        k.tt("dve", A("t1"), A("t0"), A("are"), ALU.mult, [B_("t0"), B_("are")], [B_("t1")])
        k.tt("dve", A("t2"), A("rs1"), A("aim"), ALU.mult, [B_("rs1"), B_("aim")], [B_("t2")])
        k.tt("dve", A("kre"), A("t1"), A("t2"), ALU.add, [B_("t1"), B_("t2")], [B_("kre")])
        k.tt("dve", A("kre"), A("kre"), A("den"), ALU.mult, [B_("kre"), B_("den")], [B_("kre")])
        k.tt("dve", A("t1"), A("rs1"), A("are"), ALU.mult, [B_("rs1"), B_("are")], [B_("t1")])
        k.tt("dve", A("t2"), A("t0"), A("aim"), ALU.mult, [B_("t0"), B_("aim")], [B_("t2")])
        k.tt("dve", A("kim"), A("t1"), A("t2"), ALU.subtract, [B_("t1"), B_("t2")], [B_("kim")])
        k.tt("dve", A("kim"), A("kim"), A("den"), ALU.mult, [B_("kim"), B_("den")], [B_("kim")])
        kre_b = sm["kre"].t[:, :].unsqueeze(2).to_broadcast([128, 16, 16])
        kim_b = sm["kim"].t[:, :].unsqueeze(2).to_broadcast([128, 16, 16])
        t_a = tmp(); ta = t_a.t[:, 0:256].rearrange("p (a b) -> p a b", b=16)
        k.tt("dve", bbar.t[:, 0], braw.t[:, 0], kre_b, ALU.mult, [braw.b(), B_("kre")], [bbar.b()])
        k.tt("dve", ta, braw.t[:, 1], kim_b, ALU.mult, [braw.b(), B_("kim")], [t_a.b()])
        k.tt("dve", bbar.t[:, 0], bbar.t[:, 0], ta, ALU.subtract, [bbar.b(), t_a.b()], [bbar.b()])
        k.tt("dve", bbar.t[:, 1], braw.t[:, 1], kre_b, ALU.mult, [braw.b(), B_("kre")], [bbar.b()])
        k.tt("dve", ta, braw.t[:, 0], kim_b, ALU.mult, [braw.b(), B_("kim")], [t_a.b()])
        k.tt("dve", bbar.t[:, 1], bbar.t[:, 1], ta, ALU.add, [bbar.b(), t_a.b()], [bbar.b()])
        k.ts("dve", craw.t[:, 1], craw.t[:, 1], -1.0, ALU.mult, [craw.b()], [craw.b()])
        for p in range(16):
            pl = p % 4
            k.ve("pool", lambda e: e.memset(zst.t[:], 0.0), [], [zst.b()])
            for half in range(2):
                hp = slice(64 * half, 64 * half + 64)
                c0 = 32 * pl + 16 * half
                for q, srcT in ((0, bbar.t[hp, 0, p, :]), (1, bbar.t[hp, 1, p, :]), (2, craw.t[hp, 0, p, :]), (3, craw.t[hp, 1, p, :])):
                    k.cp("pool", zst.t[hp, q, c0:c0 + 16], srcT, [bbar.b(), craw.b()], [zst.b()])
            bk = k.bank()
            for q in range(2):
                k.tr(ps[bk].t[:, q * 128:(q + 1) * 128], zst.t[:, q, :], ident.t[:], [zst.b(), ident.b()], [ps[bk].b()])
            k.cp("dve", wst.t[:, 0:2, :], ps[bk].t[:, 0:256].rearrange("p (a b) -> p a b", b=128), [ps[bk].b()], [wst.b()])
            k.cp("dve", wst.t[:, 2:4, :], zst.t[:, 2:4, :], [zst.b()], [wst.b()])
            k.dma("sp", S["ssmw"].t[l][:, p], wst.t[:], [wst.b()], [S["ssmw"].b(l)], nc_ok=True)
            t_g = tmp()
            k.ts("dve", t_g.t[:], iota.t[:], sm["th"].t[:, p:p + 1], ALU.mult, [iota.b(), B_("th")], [t_g.b()])
            sincos(rotst.t[:, 1, :], rotst.t[:, 0, :], t_g.t[:], [128, 512], [t_g.b()], rotst.b(), rotst.b())
            k.dma("sp", S["rot"].t[l][:, p], rotst.t[:], [rotst.b()], [S["rot"].b(l)], nc_ok=True)
        for c4 in range(4):
            k.ts("dve", ddiag.t[:, c4, :], identb.t[:], pv["dsk"].t[:, c4:c4 + 1], ALU.mult, [identb.b(), pv["dsk"].b()], [ddiag.b()])
            for j0 in range(0, 32, 8):
                nj = min(8, CONV_K - j0)
                if nj <= 0:
                    continue
                for j in range(nj):
                    k.ts("dve", cdg.t[:, j, :], identb.t[:], pv["convw"].t[:, c4 * 31 + j0 + j:c4 * 31 + j0 + j + 1], ALU.mult,
                         [identb.b(), pv["convw"].b()], [cdg.b()])
                k.dma("sp", S["cdiag"].t[l][c4][:, j0:j0 + nj, :], cdg.t[:, 0:nj, :], [cdg.b()], [S["cdiag"].b(l)], nc_ok=True)
    k.layer_prep = layer_prep
    MUL, ADD, SUB = ALU.mult, ALU.add, ALU.subtract

    def lw(name, l, col0, ncol, nk=8, row0=0):
        Wt = S[name]
        return wload((Wt.t[l][row0:row0 + nk * 128, col0:col0 + ncol].rearrange("(kc p) n -> p kc n", p=128), Wt.b(l)), [128, nk, ncol])

    def load_x(l, b):
        if l == 0:
            for t4 in range(4):
                k.dma("sp", xio.t[:], I["xp"].t[b * TB + t4 * 128:b * TB + (t4 + 1) * 128, :], [I["xp"].b()], [xio.b()])
                for g2 in range(2):
                    bk = k.bank()
                    for q in range(4):
                        kc = g2 * 4 + q
                        k.tr(ps[bk].t[:, q * 128:(q + 1) * 128], xio.t[:, kc * 128:(kc + 1) * 128], ident.t[:], [xio.b(), ident.b()], [ps[bk].b()])
                    k.cp("act", x.t[:, g2 * 4:(g2 + 1) * 4, t4 * 128:(t4 + 1) * 128], ps[bk].t[:, :].rearrange("p (a b) -> p a b", b=128),
                         [ps[bk].b()], [x.b(i) for i in range(g2 * 4, g2 * 4 + 4)])
        else:
            k.dma("sp", x.t[:], xscr.t[(l - 1) % 2, b], [xscr.b(((l - 1) % 2) * NB + b)], x.all())

    def store_x(l, b):
        if l < L - 1:
            k.dma("sp", xscr.t[l % 2, b], x.t[:], x.all(), [xscr.b((l % 2) * NB + b)])
        else:
            for t4 in range(4):
                for g2 in range(2):
                    bk = k.bank()
                    for q in range(4):
                        kc = g2 * 4 + q
                        k.tr(ps[bk].t[:, q * 128:(q + 1) * 128], x.t[:, kc, t4 * 128:(t4 + 1) * 128], ident.t[:], [x.b(kc), ident.b()], [ps[bk].b()])
                    k.cp("act", xio.t[:, g2 * 512:(g2 + 1) * 512], ps[bk].t[:], [ps[bk].b()], [xio.b()])
                k.dma("sp", O["yp"].t[b * TB + t4 * 128:b * TB + (t4 + 1) * 128, :], xio.t[:], [xio.b()], [O["yp"].b()])

    def rsqrt_inplace(t_ap, tb, scale):
        k.act(t_ap, t_ap, AF.Sqrt, [tb, epsc.b()], [tb], scale=scale, bias=epsc.t[0:t_ap.shape[0], 0:1])
        k.ve("dve", lambda e: e.reciprocal(out=t_ap, in_=t_ap), [tb], [tb])

    def norm_mod(gs, shc0):
        bk = k.bank()
        for kc in range(8):
            s = sqt()
            k.act(s.t[:], x.t[:, kc, :], AF.Square, [x.b(kc)], [s.b()])
            k.mm(ps[bk].t[:], onesb.t[:], s.t[:], kc == 0, kc == 7, [onesb.b(), s.b()], [ps[bk].b()])
        k.act(rstd.t[:], ps[bk].t[:], AF.Sqrt, [ps[bk].b(), epsc.b()], [rstd.b()], scale=1.0 / D, bias=epsc.t[:, 0:1])
        k.ve("dve", lambda e: e.reciprocal(out=rstd.t[:], in_=rstd.t[:]), [rstd.b()], [rstd.b()])
        for kc in range(8):
            t = tmp()
            k.tt("dve", t.t[:], x.t[:, kc, :], rstd.t[:], MUL, [x.b(kc), rstd.b()], [t.b()])
            k.act(h.t[:, kc, :], t.t[:], AF.Identity, [t.b(), gs.b(), modT.b()], [h.b(kc)], scale=gs.t[:, kc:kc + 1],
                  bias=modT.t[:, shc0 + kc, 0:1])

    def fm(wv, wb, ci, rhsT, nk=8):
        bk = k.bank()
        for kc in range(nk):
            k.mm(ps[bk].t[:], wv[:, kc, ci * 128:(ci + 1) * 128], rhsT.t[:, kc, :], kc == 0, kc == nk - 1, [wb, rhsT.b(kc)], [ps[bk].b()])
        return bk

    def proj_a(l, b):
        if b == 0:
            k.ve("pool", lambda e: e.memset(aT.t[:, :, 0:30], 0.0), [], aT.all())
        for q in range(2):
            wv, wb = lw("w_in", l, 256 * q, 256)
            for i in range(2):
                bk = fm(wv, wb, i, h)
                k.cp("act", uT.t[:, 2 * q + i, :], ps[bk].t[:], [ps[bk].b()], [uT.b(2 * q + i)])
        for q in range(2):
            gv, gb = lw("w_in", l, 1024 + 256 * q, 256)
            av, ab = lw("w_in", l, 512 + 256 * q, 256)
            for i in range(2):
                m = 2 * q + i
                bg = fm(gv, gb, i, h)
                t = tmp()
                k.act(t.t[:], ps[bg].t[:], AF.Sigmoid, [ps[bg].b()], [t.b()])
                ba = fm(av, ab, i, h)
                k.tt("dve", aT.t[:, m, 30:30 + TB], ps[ba].t[:], t.t[:], MUL, [ps[ba].b(), t.b()], [aT.b(m)])

    def cols(gi, s, ap2):
        if gi == 0:
            return ap2[:, 128 * s:128 * (s + 1)]
        return ap2[:, s::4]

    def slot(gi, b, s):
        return (4 * b + s) % 8 if gi == 0 else ((b % 2) * 4 + s if gi == 1 else (b % 5) * 4 + s)

    def qk_norm_rope(NP, bq, cos2, sin2, ropeb):
        pp = slice(0, NP)
        t = tmp()
        k.act(t.t[pp, :], ps[bq].t[pp, :], AF.Square, [ps[bq].b()], [t.b()])
        k.ve("dve", lambda e, t=t: e.tensor_reduce(out=ssq.t[pp, :], in_=t.t[pp, :].rearrange("p (a b) -> p a b", b=64), axis=AX.X, op=ADD),
             [t.b()], [ssq.b()])
        rsqrt_inplace(ssq.t[pp, :], ssq.b(), 1.0 / 64)
        k.tt("dve", qkn.t[pp], ps[bq].t[pp, :].rearrange("p (a b) -> p a b", b=64), ssq.t[pp, :].unsqueeze(2).to_broadcast([NP, 8, 64]), MUL,
             [ps[bq].b(), ssq.b()], [qkn.b()])
        qv4 = qkn.t[pp].rearrange("p (j a) b -> p j a b", j=2)
        k.tt("dve", qv4, qv4, gqk.t[pp, :, :].unsqueeze(2).to_broadcast([NP, 2, 4, 64]), MUL, [qkn.b(), gqk.b()], [qkn.b()])
        cs = cos2.unsqueeze(1).to_broadcast([NP, 8, 32])
        sn = sin2.unsqueeze(1).to_broadcast([NP, 8, 32])
        x1 = qkn.t[pp, :, 0:32]
        x2 = qkn.t[pp, :, 32:64]
        ta, tb_ = tmp(), tmp()
        a3 = ta.t[pp, 0:256].rearrange("p (a b) -> p a b", b=32)
        b3 = ta.t[pp, 256:512].rearrange("p (a b) -> p a b", b=32)
        c3 = tb_.t[pp, 0:256].rearrange("p (a b) -> p a b", b=32)
        d3 = tb_.t[pp, 256:512].rearrange("p (a b) -> p a b", b=32)
        RR = [qkn.b(), ropeb]
        k.tt("dve", a3, x1, cs, MUL, RR, [ta.b()])
        k.tt("dve", b3, x2, sn, MUL, RR, [ta.b()])
        k.tt("dve", qkr.t[pp, :, 0:32], a3, b3, SUB, [ta.b()], [qkr.b()])
        k.tt("pool", c3, x1, sn, MUL, RR, [tb_.b()])
        k.tt("pool", d3, x2, cs, MUL, RR, [tb_.b()])
        k.tt("pool", qkr.t[pp, :, 32:64], c3, d3, ADD, [tb_.b()], [qkr.b()])

    def qkv(l, b):
        k.dma("sp", ropeT.t[:], I["c_rope"].t[b], [I["c_rope"].b()], [ropeT.b()])
        wts = {}

        def stage_a(gi, s):
            if s == 0:
                wts[gi] = (lw("w_in", l, 1536 + 256 * gi, 256), lw("w_in", l, 2304 + 256 * gi, 256), lw("w_in", l, 3072 + 256 * gi, 256))
            wq, wk, wv3 = wts[gi]
            bq = k.bank()
            bv = k.bank()
            for (wv_, wb_), bank_, c0 in ((wq, bq, 0), (wk, bq, 256), (wv3, bv, 0)):
                for kc in range(8):
                    k.mm(ps[bank_].t[:, c0:c0 + 256], cols(gi, s, h.t[:, kc, :]), wv_[:, kc, :], kc == 0, kc == 7,
                         [h.b(kc), wb_], [ps[bank_].b()])
            return bq, bv

        def stage_b(gi, s, bq, bv):
            W_, _d = WINS[gi]
            keep = min(W_, SEQ)
            base = SEQ - keep
            sl = slot(gi, b, s)
            qk_norm_rope(128, bq, ropeT.t[:, gi * 4 + s, 0:32], ropeT.t[:, gi * 4 + s, 32:64], ropeT.b())
            bt = k.bank()
            for j in range(4):
                k.tr(ps[bt].t[:, j * 128:(j + 1) * 128], qkr.t[:, 2 * j:2 * j + 2, :].rearrange("p a b -> p (a b)"), ident.t[:],
                     [qkr.b(), ident.b()], [ps[bt].b()])
            k.cp("act", qT.t[:, gi * 4 + s, :, :], ps[bt].t[:, 0:256].rearrange("p (a b) -> p a b", b=128), [ps[bt].b()], [qT.b(gi * 4 + s)])
            k.cp("act", kT[gi].t[:, sl, :, :], ps[bt].t[:, 256:512].rearrange("p (a b) -> p a b", b=128), [ps[bt].b()], [kT[gi].b(sl)])
            k.cp("act", vR[gi].t[:, sl, :], ps[bv].t[:, 0:256], [ps[bv].b()], [vR[gi].b(sl)])
            t0 = b * TB
            tail = (t0 + 128 * s >= base) if gi == 0 else (t0 >= base)
            if tail:
                dst = O[f"kv{W_}_p"].t[l]
                r0 = t0 - base
                k.cp("act", vstage.t[:], ps[bv].t[:, 0:256], [ps[bv].b()], [vstage.b()])
                for src_ap, srcb, c0 in ((qkr.t[:, 4:8, :].rearrange("p a b -> p (a b)"), qkr.b(), 0), (vstage.t[:], vstage.b(), 256)):
                    if gi == 0:
                        k.dma("sp", dst[r0 + 128 * s:r0 + 128 * s + 128, c0:c0 + 256], src_ap, [srcb], [O[f"kv{W_}_p"].b()], nc_ok=True)
                    else:
                        k.dma("sp", dst[r0 + s:r0 + 512:4, c0:c0 + 256], src_ap, [srcb], [O[f"kv{W_}_p"].b()], nc_ok=True)

        sets = [(gi, s) for gi in range(3) for s in range(4)]
        cur = stage_a(*sets[0])
        for n_, (gi, s) in enumerate(sets):
            nxt = stage_a(*sets[n_ + 1]) if n_ + 1 < len(sets) else None
            stage_b(gi, s, *cur)
            cur = nxt

    pi_ = [0]

    def attention(b):
        acc = xio.t[0:64, :].rearrange("p (a b) -> p a b", a=2)
        its = []
        for hh in range(4):
            for gi in ATT_GROUPS:
                for s in range(4):
                    if gi == 0:
                        S_ = 4 * b + s
                        kts = ([((S_ - 1) % 8, 1)] if S_ >= 1 else []) + [(S_ % 8, 0)]
                    elif gi == 1:
                        kts = ([(((b - 1) % 2) * 4 + s, 1)] if b >= 1 else []) + [((b % 2) * 4 + s, 0)]
                    else:
                        kts = [(((b - j) % 5) * 4 + s, 4 if j == 4 else 3) for j in (4, 3, 2, 1) if b - j >= 0] + [((b % 5) * 4 + s, 2)]
                    its.append((hh, gi, s, kts))

        def scores(it):
            hh, gi, s, kts = it
            hp = slice(64 * (hh % 2), 64 * (hh % 2) + 64)
            pr = hh // 2
            n = len(kts)
            pt = pT[pi_[0] % 2]
            ptm = pTm[pi_[0] % 2]
            pi_[0] += 1
            banks = [k.bank()] + ([k.bank()] if n > 4 else [])
            for i, (sl, mk) in enumerate(kts):
                bk = banks[i // 4]
                c = (i % 4) * 128
                k.mm(ps[bk].t[:, c:c + 128], kT[gi].t[hp, sl, pr, :], qT.t[hp, gi * 4 + s, pr, :], True, True,
                     [kT[gi].b(sl), qT.b(gi * 4 + s)], [ps[bk].b()])
            for j, bk in enumerate(banks):
                w = min(n - 4 * j, 4) * 128
                k.act(pt.t[:, 512 * j:512 * j + w], ps[bk].t[:, 0:w], AF.Exp, [ps[bk].b()], [pt.b()], scale=0.125)
            for i, (sl, mk) in enumerate(kts):
                k.tt("pool", ptm.t[:, i * 128:(i + 1) * 128], pt.t[:, i * 128:(i + 1) * 128], masks.t[:, mk, :], MUL,
                     [pt.b(), masks.b()], [ptm.b()])
            return ptm

        def pv_(it, ptm):
            hh, gi, s, kts = it
            n = len(kts)
            ob = k.bank()
            for i, (sl, mk) in enumerate(kts):
                k.mm(ps[ob].t[0:64, 0:128], vR[gi].t[:, sl, hh * 64:(hh + 1) * 64], ptm.t[:, i * 128:(i + 1) * 128], i == 0, i == n - 1,
                     [vR[gi].b(sl), ptm.b()], [ps[ob].b()])
            for i, (sl, mk) in enumerate(kts):
                k.mm(ps[ob].t[0:64, 128:256], onesb.t[:, 0:64], ptm.t[:, i * 128:(i + 1) * 128], i == 0, i == n - 1,
                     [onesb.b(), ptm.b()], [ps[ob].b()])
            src_ = ps[ob].t[0:64, 0:256].rearrange("p (a b) -> p a b", a=2)
            dst_ = acc[:, :, 128 * s:128 * (s + 1)] if gi == 0 else acc[:, :, s::4]
            if gi == ATT_GROUPS[0]:
                k.cp("dve", dst_, src_, [ps[ob].b()], [xio.b()])
            else:
                k.tt("dve", dst_, dst_, src_, ADD, [ps[ob].b(), xio.b()], [xio.b()])
            if gi == ATT_GROUPS[-1] and s == 3:
                t = tmp()
                k.ve("dve", lambda e, t=t: e.reciprocal(out=t.t[0:64, :], in_=acc[:, 1, :]), [xio.b()], [t.b()])
                k.tt("dve", yattn.t[:, hh, :], acc[:, 0, :], t.t[0:64, :], MUL, [xio.b(), t.b()], [yattn.b(hh)])

        cur = scores(its[0])
        for n_, it in enumerate(its):
            nxt = scores(its[n_ + 1]) if n_ + 1 < len(its) else None
            pv_(it, cur)
            cur = nxt

    def ssm(l, b, last):
        k.pool_banks = list(range(6))
        rB = sm["r"].b()
        wts = {}

        def emit_x(p):
            c4, pl = divmod(p, 4)
            if pl == 0:
                wts[c4] = wload((S["ssmw"].t[l][:, 4 * c4:4 * c4 + 4], S["ssmw"].b(l)), [128, 4, 4, 128])
            wv, wb = wts[c4]
            rv, rb = wload((S["rot"].t[l][:, p], S["rot"].b(l)), [128, 2, TB])
            bre, bim = k.bank(), k.bank()
            k.mm(ps[bre].t[:], wv[:, pl, 0, :], uT.t[:, c4, :], True, True, [wb, uT.b(c4)], [ps[bre].b()])
            k.mm(ps[bim].t[:], wv[:, pl, 1, :], uT.t[:, c4, :], True, True, [wb, uT.b(c4)], [ps[bim].b()])
            return rv, rb, bre, bim

        nxt = emit_x(0)
        for p in range(16):
            c4, pl = divmod(p, 4)
            yb = 6 + (c4 % 2)
            wv, wb = wts[c4]
            rv, rb, bre, bim = nxt
            if p + 1 < 16:
                nxt = emit_x(p + 1)
            cosv, sinv = rv[:, 0, :], rv[:, 1, :]
            dt_ = tmpf[2]
            vvb = (vv[0], vv[1]) if p % 2 == 0 else (tmpf[0], tmpf[1])
            k.tt("dve", dt_.t[:], ps[bre].t[:], cosv, MUL, [ps[bre].b(), rb], [dt_.b()])
            k.tt("dve", v0[0].t[:], ps[bim].t[:], sinv, MUL, [ps[bim].b(), rb], [v0[0].b()])
            k.tt("dve", v0[0].t[:], v0[0].t[:], dt_.t[:], ADD, [v0[0].b(), dt_.b()], [v0[0].b()])
            k.tt("dve", dt_.t[:], ps[bim].t[:], cosv, MUL, [ps[bim].b(), rb], [dt_.b()])
            k.tt("dve", v0[1].t[:], ps[bre].t[:], sinv, MUL, [ps[bre].b(), rb], [v0[1].b()])
            k.tt("dve", v0[1].t[:], dt_.t[:], v0[1].t[:], SUB, [v0[1].b(), dt_.b()], [v0[1].b()])
            for ri in range(2):
                init = 0.0 if b == 0 else vin.t[:, p, ri:ri + 1]
                k.ve("dve", lambda e, ri=ri, p=p, init=init, vvb=vvb: e.tensor_tensor_scan(
                    out=vvb[ri].t[:], data0=sm["r"].t[:, p:p + 1].to_broadcast([128, TB]), data1=v0[ri].t[:], initial=init,
                    op0=MUL, op1=ADD), [v0[ri].b(), rB, vin.b()], [vvb[ri].b()])
                k.cp("act", vend.t[:, p, ri:ri + 1], vvb[ri].t[:, TB - 1:TB], [vvb[ri].b()], [vend.b()])
            pa, pb = pT[0], pT[1]
            pav, pbv = pa.t[:, 0:TB], pb.t[:, 0:TB]
            k.tt("pool", pav, vvb[0].t[:], cosv, MUL, [vvb[0].b(), rb], [pa.b()])
            k.tt("pool", pbv, vvb[1].t[:], sinv, MUL, [vvb[1].b(), rb], [pb.b()])
            k.tt("pool", sre[p % 2].t[:], pav, pbv, SUB, [pa.b(), pb.b()], [sre[p % 2].b()])
            k.tt("pool", pav, vvb[1].t[:], cosv, MUL, [vvb[1].b(), rb], [pa.b()])
            k.tt("pool", pbv, vvb[0].t[:], sinv, MUL, [vvb[0].b(), rb], [pb.b()])
            k.tt("pool", sim[p % 2].t[:], pav, pbv, ADD, [pa.b(), pb.b()], [sim[p % 2].b()])
            k.mm(ps[yb].t[:], wv[:, pl, 2, :], sre[p % 2].t[:], pl == 0, False, [wb, sre[p % 2].b()], [ps[yb].b()])
            k.mm(ps[yb].t[:], wv[:, pl, 3, :], sim[p % 2].t[:], False, False, [wb, sim[p % 2].b()], [ps[yb].b()])
            if pl == 3:
                k.mm(ps[yb].t[:], ddiag.t[:, c4, :], uT.t[:, c4, :], False, True, [ddiag.b(), uT.b(c4)], [ps[yb].b()])
                k.act(yg.t[:, c4, :], ps[yb].t[:], AF.Gelu_apprx_tanh, [ps[yb].b()], [yg.b(c4)])
        k.pool_banks = list(range(8))
        cn, sn_ = ("c511", "s511") if last else ("c512", "s512")
        dstT = sstate if last else vin
        vr, vi = vend.t[:, :, 0], vend.t[:, :, 1]
        A = lambda n: sm[n].t[:]
        k.tt("dve", A("t0"), vr, A(cn), MUL, [vend.b(), sm[cn].b()], [sm["t0"].b()])
        k.tt("dve", A("t1"), vi, A(sn_), MUL, [vend.b(), sm[sn_].b()], [sm["t1"].b()])
        k.tt("dve", dstT.t[:, :, 0], A("t0"), A("t1"), SUB, [sm["t0"].b(), sm["t1"].b()], [dstT.b()])
        k.tt("dve", A("t2"), vi, A(cn), MUL, [vend.b(), sm[cn].b()], [sm["t2"].b()])
        k.tt("dve", A("t3"), vr, A(sn_), MUL, [vend.b(), sm[sn_].b()], [sm["t3"].b()])
        k.tt("dve", dstT.t[:, :, 1], A("t2"), A("t3"), ADD, [sm["t2"].b(), sm["t3"].b()], [dstT.b()])
        if last:
            for half in range(2):
                hp = slice(64 * half, 64 * half + 64)
                k.dma("sp", O["ssm_p"].t[l].rearrange("(p g) n r -> g n p r", g=2)[half], sstate.t[hp, :, :], [sstate.b()], [O["ssm_p"].b()],
                      nc_ok=True)
        gv, gb = lw("ssm_w_glu", l, 0, 512, nk=4)
        for m in range(4):
            bk = fm(gv, gb, m, yg, nk=4)
            t = tmp()
            k.act(t.t[:], ps[bk].t[:], AF.Sigmoid, [ps[bk].b(), pv["bglu"].b()], [t.b()], bias=pv["bglu"].t[:, m:m + 1])
            k.tt("dve", yssm.t[:, m, :], yg.t[:, m, :], t.t[:], MUL, [yg.b(m), t.b()], [yssm.b(m)])

    def convbr(l, b, last):
        for c4 in range(4):
            bk = k.bank()
            for j0, nj in ((0, 16), (16, 15)):
                wv, wb = wload((S["cdiag"].t[l][c4][:, j0:j0 + nj, :], S["cdiag"].b(l)), [128, nj, 128])
                for j in range(nj):
                    jj = j0 + j
                    k.mm(ps[bk].t[:], wv[:, j, :], aT.t[:, c4, jj:jj + TB], jj == 0, jj == 30, [wb, aT.b(c4)], [ps[bk].b()])
            k.act(ycf.t[:, c4, :], ps[bk].t[:], AF.Identity, [ps[bk].b(), pv["convb"].b()], [ycf.b(c4)], bias=pv["convb"].t[:, c4:c4 + 1])
        if last:
            for c4 in range(4):
                k.dma("pool", O["conv_p"].t[l][:, c4 * 128:(c4 + 1) * 128].rearrange("j p -> p j"), aT.t[:, c4, TB:TB + 30], [aT.b(c4)],
                      [O["conv_p"].b()], nc_ok=True)
        else:
            k.cp("pool", aT.t[:, :, 0:30], aT.t[:, :, TB:TB + 30], aT.all(), aT.all())
        bm, bq2 = k.bank(), k.bank()
        for c4 in range(4):
            k.mm(ps[bm].t[:], onesb.t[:], ycf.t[:, c4, :], c4 == 0, c4 == 3, [onesb.b(), ycf.b(c4)], [ps[bm].b()])
        for c4 in range(4):
            s = sqt()
            k.act(s.t[:], ycf.t[:, c4, :], AF.Square, [ycf.b(c4)], [s.b()])
            k.mm(ps[bq2].t[:], onesb.t[:], s.t[:], c4 == 0, c4 == 3, [onesb.b(), s.b()], [ps[bq2].b()])
        k.ts("dve", mean.t[:], ps[bm].t[:], 1.0 / 512, MUL, [ps[bm].b()], [mean.b()])
        t = tmp()
        k.tt("dve", t.t[:], mean.t[:], mean.t[:], MUL, [mean.b()], [t.b()])
        k.stt(rstd.t[:], ps[bq2].t[:], 1.0 / 512, t.t[:], MUL, SUB, [ps[bq2].b(), t.b()], [rstd.b()])
        rsqrt_inplace(rstd.t[:], rstd.b(), 1.0)
        for c4 in range(4):
            t = tmp()
            k.tt("dve", t.t[:], ycf.t[:, c4, :], mean.t[:], SUB, [ycf.b(c4), mean.b()], [t.b()])
            k.tt("dve", t.t[:], t.t[:], rstd.t[:], MUL, [t.b(), rstd.b()], [t.b()])
            k.act(yconv.t[:, c4, :], t.t[:], AF.Silu, [t.b(), pv["lng"].b(), pv["lnb"].b()], [yconv.b(c4)],
                  scale=pv["lng"].t[:, c4:c4 + 1], bias=pv["lnb"].t[:, c4:c4 + 1])

    def merge(l):
        for mg in range(2):
            for mi in range(4):
                m = mg * 4 + mi
                sv, sbf = lw("w_br_ssm", l, m * 128, 128, nk=4)
                cv, cbf = lw("w_br_conv", l, m * 128, 128, nk=4)
                av, abf = wload((S["w_br_attn"].t[l][:, m * 128:(m + 1) * 128].rearrange("(hh p) n -> p hh n", p=64), S["w_br_attn"].b(l)),
                                [64, 4, 128])
                gts = [lw("w_gate", l, j * 1024 + m * 128, 128) for j in range(3)]
                b1 = fm(sv, sbf, 0, yssm, nk=4)
                b2 = fm(cv, cbf, 0, yconv, nk=4)
                b3 = k.bank()
                for hh in range(4):
                    k.mm(ps[b3].t[:], av[:, hh, :], yattn.t[:, hh, :], hh == 0, hh == 3, [abf, yattn.b(hh)], [ps[b3].b()])
                for j, bb in enumerate((b1, b2, b3)):
                    bg = fm(gts[j][0], gts[j][1], 0, h)
                    sg = tmp()
                    k.act(sg.t[:], ps[bg].t[:], AF.Sigmoid, [ps[bg].b(), pv["bgate"].b()], [sg.b()], bias=pv["bgate"].t[:, 8 * j + m:8 * j + m + 1])
                    if j == 0:
                        k.tt("dve", mean.t[:], ps[bb].t[:], sg.t[:], MUL, [ps[bb].b(), sg.b()], [mean.b()])
                    else:
                        k.tt("dve", rstd.t[:], ps[bb].t[:], sg.t[:], MUL, [ps[bb].b(), sg.b()], [rstd.b()])
                        if j == 1:
                            k.tt("pool", mean.t[:], mean.t[:], rstd.t[:], ADD, [mean.b(), rstd.b()], [mean.b()])
                        else:
                            k.tt("pool", actT.t[:, m, :], mean.t[:], rstd.t[:], ADD, [mean.b(), rstd.b()], [actT.b(m)])
        for mg in range(4):
            wv, wb = lw("w_out", l, mg * 256, 256)
            for mi in range(2):
                m = 2 * mg + mi
                bk = fm(wv, wb, mi, actT)
                k.stt(x.t[:, m, :], ps[bk].t[:], modT.t[:, 16 + m, 0:1], x.t[:, m, :], MUL, ADD, [ps[bk].b(), modT.b(), x.b(m)], [x.b(m)])

    def ffn(l, b, last):
        norm_mod(pv["gs2"], 24)
        if b == 0:
            k.ve("pool", lambda e: e.memset(uphalo.t[:], 0.0), [], [uphalo.b()])
        fcw, fcb = pv["fcw"], pv["fcb"]
        for jg in range(11):
            av, ab = lw("ffn_w_up", l, jg * 256, 256)
            bv, bb = lw("ffn_w_up", l, FFN_H + jg * 256, 256)
            for ji in range(2):
                j = 2 * jg + ji
                tc_ = []
                for (wv_, wb_, cj, ui) in ((av, ab, j, 0), (bv, bb, 22 + j, 1)):
                    bk = fm(wv_, wb_, ji, h)
                    us = upsb[ui]
                    k.cp("pool", us.t[:, 0:2], uphalo.t[:, cj, :], [uphalo.b()], [us.b()])
                    k.cp("act", us.t[:, 2:2 + TB], ps[bk].t[:], [ps[bk].b()], [us.b()])
                    tcv = tmp()
                    RW = [fcw.b(), fcb.b()]
                    k.act(tcv.t[:], ps[bk].t[:], AF.Identity, [ps[bk].b()] + RW, [tcv.b()], scale=fcw.t[:, cj * 3 + 2:cj * 3 + 3], bias=fcb.t[:, cj:cj + 1])
                    k.stt(tcv.t[:], us.t[:, 1:1 + TB], fcw.t[:, cj * 3 + 1:cj * 3 + 2], tcv.t[:], MUL, ADD, [us.b(), tcv.b()] + RW, [tcv.b()])
                    k.stt(tcv.t[:], us.t[:, 0:TB], fcw.t[:, cj * 3:cj * 3 + 1], tcv.t[:], MUL, ADD, [us.b(), tcv.b()] + RW, [tcv.b()])
                    k.cp("pool", uphalo.t[:, cj, :], us.t[:, TB:TB + 2], [us.b()], [uphalo.b()])
                    tc_.append(tcv)
                ga = tmp()
                k.act(ga.t[:], tc_[0].t[:], AF.Gelu_apprx_tanh, [tc_[0].b()], [ga.b()])
                k.tt("dve", actT.t[:, j, :], ga.t[:], tc_[1].t[:], MUL, [ga.b(), tc_[1].b()], [actT.b(j)])
        if last:
            for r in range(2):
                k.dma("sp", O["ffn_p"].t[l][r].rearrange("(c p) -> p c", p=128), uphalo.t[:, :, r], [uphalo.b()], [O["ffn_p"].b()], nc_ok=True)
        for m in range(8):
            bk = k.bank()
            for hf in range(2):
                wv, wb = lw("ffn_w_down", l, m * 128, 128, nk=11, row0=hf * 1408)
                for kc in range(11):
                    k.mm(ps[bk].t[:], wv[:, kc, :], actT.t[:, hf * 11 + kc, :], hf == 0 and kc == 0, hf == 1 and kc == 10,
                         [wb, actT.b(hf * 11 + kc)], [ps[bk].b()])
            k.stt(x.t[:, m, :], ps[bk].t[:], modT.t[:, 40 + m, 0:1], x.t[:, m, :], MUL, ADD, [ps[bk].b(), modT.b(), x.b(m)], [x.b(m)])

    def v3(ap, n):
        return ap.rearrange("p (c s) -> p c s", s=NS)

    def linS(name, l, col0, nchunk, rhs, nk):
        bk = k.bank()
        c = 0
        while c < nchunk:
            n2 = min(2, nchunk - c)
            wv, wb = lw(name, l, col0 + c * 128, n2 * 128, nk=nk)
            for i2 in range(n2):
                for kc in range(nk):
                    k.mm(ps[bk].t[:, (c + i2) * NS:(c + i2 + 1) * NS], wv[:, kc, i2 * 128:(i2 + 1) * 128], rhs.t[:, kc, :], kc == 0, kc == nk - 1,
                         [wb, rhs.b()], [ps[bk].b()])
            c += n2
        return bk

    def bc2(ap2, n):
        return ap2.unsqueeze(2).to_broadcast([128, n, NS])

    def norm_modS(gname, sc0, sh0):
        k.act(sqS.t[:], xS.t[:], AF.Square, [xS.b()], [sqS.b()])
        bk = k.bank()
        for kc in range(8):
            k.mm(ps[bk].t[:, 0:NS], onesb.t[:], sqS.t[:, kc, :], kc == 0, kc == 7, [onesb.b(), sqS.b()], [ps[bk].b()])
        k.act(rsS.t[:], ps[bk].t[:, 0:NS], AF.Sqrt, [ps[bk].b(), epsc.b()], [rsS.b()], scale=1.0 / D, bias=epsc.t[:, 0:1])
        k.ve("dve", lambda e: e.reciprocal(out=rsS.t[:], in_=rsS.t[:]), [rsS.b()], [rsS.b()])
        t, t2 = t8S[0], t8S[1]
        k.tt("dve", t.t[:], xS.t[:], rsS.t[:, :].unsqueeze(1).to_broadcast([128, 8, NS]), MUL, [xS.b(), rsS.b()], [t.b()])
        k.tt("dve", t.t[:], t.t[:], bc2(pv[gname].t[:, :], 8), MUL, [t.b(), pv[gname].b()], [t.b()])
        k.ts("dve", t2.t[:], modT.t[:, sc0:sc0 + 8, 1:1 + NS], 1.0, ADD, [modT.b()], [t2.b()])
        k.tt("dve", t.t[:], t.t[:], t2.t[:], MUL, [t.b(), t2.b()], [t.b()])
        k.tt("dve", hS.t[:], t.t[:], modT.t[:, sh0:sh0 + 8, 1:1 + NS], ADD, [t.b(), modT.b()], [hS.b()])

    def residS(bk, gt0):
        t = t8S[0]
        k.tt("dve", t.t[:], v3(ps[bk].t[:, 0:8 * NS], 8), modT.t[:, gt0:gt0 + 8, 1:1 + NS], MUL, [ps[bk].b(), modT.b()], [t.b()])
        k.tt("dve", xS.t[:], xS.t[:], t.t[:], ADD, [xS.b(), t.b()], [xS.b()])

    def sample_layer(l):
        SP = slice(0, NS)
        if l == 0:
            for s in range(NS):
                k.dma("sp", xS.t[:, :, s], I["xs"].t[s].rearrange("(c p) -> p c", p=128), [I["xs"].b()], [xS.b()], nc_ok=True)
            k.dma("sp", ropeS.t[:], I["c_rope_s"].t[0].partition_broadcast(NS), [I["c_rope_s"].b()], [ropeS.b()], nc_ok=True)
        norm_modS("gmix", 8, 0)
        bz = linS("w_in", l, 0, 12, hS, 8)
        k.cp("act", zS.t[:, 0:12, :], v3(ps[bz].t[:, 0:12 * NS], 12), [ps[bz].b()], [zS.b()])
        k.cp("dve", uSb.t[:], zS.t[:, 0:4, :], [zS.b()], [uSb.b()])
        aS = t8S[2]
        k.act(aS.t[:, 4:8, :], zS.t[:, 8:12, :], AF.Sigmoid, [zS.b()], [aS.b()])
        k.tt("dve", aS.t[:, 0:4, :], zS.t[:, 4:8, :], aS.t[:, 4:8, :], MUL, [zS.b(), aS.b()], [aS.b()])
        for s in range(NS):
            for half in range(2):
                hp = slice(64 * half, 64 * half + 64)
                k.dma("sp", s0S.t[hp, :, s, :], I["st_ssm"].t[l][s].rearrange("(p g) n r -> g n p r", g=2)[half], [I["st_ssm"].b()], [s0S.b()],
                      nc_ok=True)
        bre, bim = k.bank(), k.bank()
        for c4 in range(4):
            wv, wb = wload((S["ssmw"].t[l][:, 4 * c4:4 * c4 + 4], S["ssmw"].b(l)), [128, 4, 4, 128])
            for pl in range(4):
                p = 4 * c4 + pl
                k.mm(ps[bre].t[:, p * NS:(p + 1) * NS], wv[:, pl, 0, :], uSb.t[:, c4, :], True, True, [wb, uSb.b()], [ps[bre].b()])
                k.mm(ps[bim].t[:, p * NS:(p + 1) * NS], wv[:, pl, 1, :], uSb.t[:, c4, :], True, True, [wb, uSb.b()], [ps[bim].b()])
        Xre = v3(ps[bre].t[:, 0:16 * NS], 16)
        Xim = v3(ps[bim].t[:, 0:16 * NS], 16)
        rc = bc2(sm["rc1"].t[:, :], 16)
        rs_ = bc2(sm["rs1"].t[:, :], 16)
        s0r, s0i = s0S.t[:, :, :, 0], s0S.t[:, :, :, 1]
        s1r, s1i = s1S.t[:, :, :, 0], s1S.t[:, :, :, 1]
        tA, tB = tmp(), tmp()
        a3 = v3(tA.t[:, 0:16 * NS], 16)
        b3 = v3(tB.t[:, 0:16 * NS], 16)
        R0 = [s0S.b(), sm["rc1"].b(), sm["rs1"].b()]
        k.tt("dve", a3, s0r, rc, MUL, R0, [tA.b()])
        k.tt("dve", b3, s0i, rs_, MUL, R0, [tB.b()])
        k.tt("dve", a3, a3, b3, SUB, [tA.b(), tB.b()], [tA.b()])
        k.tt("dve", s1r, a3, Xre, ADD, [tA.b(), ps[bre].b()], [s1S.b()])
        k.tt("dve", a3, s0r, rs_, MUL, R0, [tA.b()])
        k.tt("dve", b3, s0i, rc, MUL, R0, [tB.b()])
        k.tt("dve", a3, a3, b3, ADD, [tA.b(), tB.b()], [tA.b()])
        k.tt("dve", s1i, a3, Xim, ADD, [tA.b(), ps[bim].b()], [s1S.b()])
        k.cp("act", s1b.t[:, 0], s1r, [s1S.b()], [s1b.b()])
        k.cp("act", s1b.t[:, 1], s1i, [s1S.b()], [s1b.b()])
        for s in range(NS):
            for half in range(2):
                hp = slice(64 * half, 64 * half + 64)
                k.dma("sp", O["ssm_s"].t[l][s].rearrange("(p g) n r -> g n p r", g=2)[half], s1S.t[hp, :, s, :], [s1S.b()], [O["ssm_s"].b()],
                      nc_ok=True)
        by = k.bank()
        for c4 in range(4):
            wv, wb = wload((S["ssmw"].t[l][:, 4 * c4:4 * c4 + 4], S["ssmw"].b(l)), [128, 4, 4, 128])
            oc_ = ps[by].t[:, c4 * NS:(c4 + 1) * NS]
            for pl in range(4):
                p = 4 * c4 + pl
                k.mm(oc_, wv[:, pl, 2, :], s1b.t[:, 0, p, :], pl == 0, False, [wb, s1b.b()], [ps[by].b()])
                k.mm(oc_, wv[:, pl, 3, :], s1b.t[:, 1, p, :], False, False, [wb, s1b.b()], [ps[by].b()])
            k.mm(oc_, ddiag.t[:, c4, :], uSb.t[:, c4, :], False, True, [ddiag.b(), uSb.b()], [ps[by].b()])
        k.act(ygS.t[:], v3(ps[by].t[:, 0:4 * NS], 4), AF.Gelu_apprx_tanh, [ps[by].b()], [ygS.b()])
        bg = linS("ssm_w_glu", l, 0, 4, ygS, 4)
        tg = t8S[0]
        k.tt("dve", tg.t[:, 0:4, :], v3(ps[bg].t[:, 0:4 * NS], 4), bc2(pv["bglu"].t[:, :], 4), ADD, [ps[bg].b(), pv["bglu"].b()], [tg.b()])
        k.act(tg.t[:, 0:4, :], tg.t[:, 0:4, :], AF.Sigmoid, [tg.b()], [tg.b()])
        k.tt("dve", ysS.t[:], ygS.t[:], tg.t[:, 0:4, :], MUL, [ygS.b(), tg.b()], [ysS.b()])
        for c4 in range(4):
            for s in range(NS):
                k.dma("sp", convc.t[:, c4, s, :], I["c_conv"].t[l][s][:, c4 * 128:(c4 + 1) * 128].rearrange("j p -> p j"), [I["c_conv"].b()],
                      [convc.b()], nc_ok=True)
        for s in range(NS):
            k.dma("sp", O["conv_s"].t[l][s][0:29, :], I["c_conv"].t[l][s][1:30, :], [I["c_conv"].b()], [O["conv_s"].b()])
            k.dma("sp", O["conv_s"].t[l][s][29].rearrange("(c p) -> p c", p=128), aS.t[:, 0:4, s], [aS.b()], [O["conv_s"].b()], nc_ok=True)
        wj = pv["convw"].t[:, :].rearrange("p (c j) -> p c j", j=31)
        tP = tmp()
        prod = tP.t[:, 0:4 * NS * 30].rearrange("p (c s j) -> p c s j", c=4, s=NS)
        k.tt("dve", prod, convc.t[:], wj[:, :, 0:30].unsqueeze(2).to_broadcast([128, 4, NS, 30]), MUL, [convc.b(), pv["convw"].b()], [tP.b()])
        yc = t8S[1]
        k.ve("dve", lambda e: e.tensor_reduce(out=yc.t[:, 0:4, :], in_=prod, axis=AX.X, op=ADD), [tP.b()], [yc.b()])
        tq = t8S[0]
        k.tt("dve", tq.t[:, 0:4, :], aS.t[:, 0:4, :], wj[:, :, 30:31].to_broadcast([128, 4, NS]), MUL, [aS.b(), pv["convw"].b()], [tq.b()])
        k.tt("dve", yc.t[:, 0:4, :], yc.t[:, 0:4, :], tq.t[:, 0:4, :], ADD, [yc.b(), tq.b()], [yc.b()])
        k.tt("dve", yc.t[:, 0:4, :], yc.t[:, 0:4, :], bc2(pv["convb"].t[:, :], 4), ADD, [yc.b(), pv["convb"].b()], [yc.b()])
        k.cp("act", sqS.t[:, 0:4, :], yc.t[:, 0:4, :], [yc.b()], [sqS.b()])
        k.act(sqS.t[:, 4:8, :], yc.t[:, 0:4, :], AF.Square, [yc.b()], [sqS.b()])
        bm = k.bank()
        for c4 in range(4):
            k.mm(ps[bm].t[:, 0:NS], onesb.t[:], sqS.t[:, c4, :], c4 == 0, c4 == 3, [onesb.b(), sqS.b()], [ps[bm].b()])
        for c4 in range(4):
            k.mm(ps[bm].t[:, NS:2 * NS], onesb.t[:], sqS.t[:, 4 + c4, :], c4 == 0, c4 == 3, [onesb.b(), sqS.b()], [ps[bm].b()])
        k.ts("dve", mnS.t[:], ps[bm].t[:, 0:NS], 1.0 / 512, MUL, [ps[bm].b()], [mnS.b()])
        tv = tmp()
        k.tt("dve", tv.t[:, 0:NS], mnS.t[:], mnS.t[:], MUL, [mnS.b()], [tv.b()])
        k.stt(rsS.t[:], ps[bm].t[:, NS:2 * NS], 1.0 / 512, tv.t[:, 0:NS], MUL, SUB, [ps[bm].b(), tv.b()], [rsS.b()])
        rsqrt_inplace(rsS.t[:], rsS.b(), 1.0)
        k.tt("dve", yc.t[:, 0:4, :], yc.t[:, 0:4, :], mnS.t[:, :].unsqueeze(1).to_broadcast([128, 4, NS]), SUB, [yc.b(), mnS.b()], [yc.b()])
        k.tt("dve", yc.t[:, 0:4, :], yc.t[:, 0:4, :], rsS.t[:, :].unsqueeze(1).to_broadcast([128, 4, NS]), MUL, [yc.b(), rsS.b()], [yc.b()])
        k.tt("dve", yc.t[:, 0:4, :], yc.t[:, 0:4, :], bc2(pv["lng"].t[:, :], 4), MUL, [yc.b(), pv["lng"].b()], [yc.b()])
        k.tt("dve", yc.t[:, 0:4, :], yc.t[:, 0:4, :], bc2(pv["lnb"].t[:, :], 4), ADD, [yc.b(), pv["lnb"].b()], [yc.b()])
        k.act(ycS.t[:], yc.t[:, 0:4, :], AF.Silu, [yc.b()], [ycS.b()])
        for gi in range(3):
            W_, dil = WINS[gi]
            Ok, Ik = O[f"kv{W_}_s"], I[f"kv{W_}"]
            wq = lw("w_in", l, 1536 + 256 * gi, 256)
            wk = lw("w_in", l, 2304 + 256 * gi, 256)
            wv3 = lw("w_in", l, 3072 + 256 * gi, 256)
            bq, bv = k.bank(), k.bank()
            for (wv_, wb_), bank_, c0 in ((wq, bq, 0), (wk, bq, 256), (wv3, bv, 0)):
                for kc in range(8):
                    k.mm(ps[bank_].t[SP, c0:c0 + 256], hS.t[:, kc, :], wv_[:, kc, :], kc == 0, kc == 7, [hS.b(), wb_], [ps[bank_].b()])
            qk_norm_rope(NS, bq, ropeS.t[:, 0:32], ropeS.t[:, 32:64], ropeS.b())
            k.cp("act", vstage.t[SP, :], ps[bv].t[SP, 0:256], [ps[bv].b()], [vstage.b()])
            k.cp("act", vnb.t[:, gi * 256:(gi + 1) * 256], ps[bv].t[SP, 0:256], [ps[bv].b()], [vnb.b()])
            k.dma("sp", Ok.t[l][:, W_ - 1, 0:256], qkr.t[SP, 4:8, :].rearrange("p a b -> p (a b)"), [qkr.b()], [Ok.b()], nc_ok=True)
            k.dma("sp", Ok.t[l][:, W_ - 1, 256:512], vstage.t[SP, :], [vstage.b()], [Ok.b()], nc_ok=True)
            for s in range(NS):
                k.dma("sp", Ok.t[l][s][0:W_ - 1, :], Ik.t[l][s][1:W_, :], [Ik.b()], [Ok.b()])
            tP2 = tmp()
            pr3 = tP2.t[SP, 0:256].rearrange("p (a b) -> p a b", b=64)
            k.tt("dve", pr3, qkr.t[SP, 0:4, :], qkr.t[SP, 4:8, :], MUL, [qkr.b()], [tP2.b()])
            k.ve("dve", lambda e, gi=gi, pr3=pr3: e.tensor_reduce(out=sself.t[:, gi * 4:(gi + 1) * 4], in_=pr3, axis=AX.X, op=ADD),
                 [tP2.b()], [sself.b()])
            bt = k.bank()
            for j in range(2):
                k.tr(ps[bt].t[:, j * NS:(j + 1) * NS], qkr.t[SP, 2 * j:2 * j + 2, :].rearrange("p a b -> p (a b)"), ident.t[SP, SP],
                     [qkr.b(), ident.b()], [ps[bt].b()])
            k.cp("act", qS.t[:, gi, :, :], v3(ps[bt].t[:, 0:2 * NS], 2), [ps[bt].b()], [qS.b()])
        k.act(sself.t[:], sself.t[:], AF.Exp, [sself.b()], [sself.b()], scale=0.125)
        k.tt("dve", pdS.t[:], sself.t[:, :].unsqueeze(2).to_broadcast([NS, 12, NS]), ident.t[SP, SP].unsqueeze(1).to_broadcast([NS, 12, NS]), MUL,
             [sself.b(), ident.b()], [pdS.b()])
        Pg = pT[1]
        KTs = pT[0].t[:, 0:256].rearrange("p (a b) -> p a b", a=2)
        Vb = pTm[0].t[:, 0:256]
        for s in range(NS):
            bo = k.bank()
            for gi in range(3):
                W_, dil = WINS[gi]
                Ik = I[f"kv{W_}"]
                k.dma("sp", xio.t[:, 0:512], Ik.t[l][s][0:W_:dil, :], [Ik.b()], [xio.b()], nc_ok=True)
                bt = k.bank()
                for j in range(2):
                    k.tr(ps[bt].t[:, j * 128:(j + 1) * 128], xio.t[:, j * 128:(j + 1) * 128], ident.t[:], [xio.b(), ident.b()], [ps[bt].b()])
                k.cp("act", KTs, ps[bt].t[:, 0:256].rearrange("p (a b) -> p a b", a=2), [ps[bt].b()], [pT[0].b()])
                k.cp("dve", Vb, xio.t[:, 256:512], [xio.b()], [pTm[0].b()])
                bs = k.bank()
                for hh in range(4):
                    hp = slice(64 * (hh % 2), 64 * (hh % 2) + 64)
                    k.mm(ps[bs].t[:, hh:hh + 1], KTs[hp, hh // 2, :], qS.t[hp, gi, hh // 2, s:s + 1], True, True, [pT[0].b(), qS.b()], [ps[bs].b()])
                k.act(Pg.t[:, 0:4], ps[bs].t[:, 0:4], AF.Exp, [ps[bs].b()], [Pg.b()], scale=0.125)
                for hh in range(4):
                    col = gi * 4 + hh
                    k.mm(ps[bo].t[0:64, col:col + 1], Vb[:, hh * 64:(hh + 1) * 64], Pg.t[:, hh:hh + 1], True, False, [pTm[0].b(), Pg.b()], [ps[bo].b()])
                    k.mm(ps[bo].t[0:64, col:col + 1], vnb.t[:, col * 64:(col + 1) * 64], pdS.t[:, col, s:s + 1], False, True, [vnb.b(), pdS.b()],
                         [ps[bo].b()])
                for hh in range(4):
                    col = gi * 4 + hh
                    k.mm(ps[bo].t[0:64, 12 + col:13 + col], onesb.t[:, 0:64], Pg.t[:, hh:hh + 1], True, False, [onesb.b(), Pg.b()], [ps[bo].b()])
                    k.mm(ps[bo].t[0:64, 12 + col:13 + col], onesb.t[SP, 0:64], pdS.t[:, col, s:s + 1], False, True, [onesb.b(), pdS.b()],
                         [ps[bo].b()])
            k.cp("dve", oacc.t[:], ps[bo].t[0:64, 0:24].rearrange("p (o g h) -> p o g h", o=2, g=3), [ps[bo].b()], [oacc.b()])
            tA2 = tmp()
            a2 = tA2.t[0:64, 0:8].rearrange("p (o h) -> p o h", o=2)
            k.tt("dve", a2, oacc.t[:, :, 0, :], oacc.t[:, :, 1, :], ADD, [oacc.b()], [tA2.b()])
            k.tt("dve", a2, a2, oacc.t[:, :, 2, :], ADD, [tA2.b(), oacc.b()], [tA2.b()])
            k.ve("dve", lambda e, a2=a2: e.reciprocal(out=a2[:, 1, :], in_=a2[:, 1, :]), [tA2.b()], [tA2.b()])
            k.tt("dve", yaS.t[:, :, s], a2[:, 0, :], a2[:, 1, :], MUL, [tA2.b()], [yaS.b()])
        for m in range(8):
            sv, sbf = lw("w_br_ssm", l, m * 128, 128, nk=4)
            cv, cbf = lw("w_br_conv", l, m * 128, 128, nk=4)
            av, abf = wload((S["w_br_attn"].t[l][:, m * 128:(m + 1) * 128].rearrange("(hh p) n -> p hh n", p=64), S["w_br_attn"].b(l)), [64, 4, 128])
            gts = [lw("w_gate", l, j * 1024 + m * 128, 128) for j in range(3)]
            bb3 = []
            for wv_, wb_, rh, nk_ in ((sv, sbf, ysS, 4), (cv, cbf, ycS, 4)):
                bk = k.bank()
                for kc in range(nk_):
                    k.mm(ps[bk].t[:, 0:NS], wv_[:, kc, :], rh.t[:, kc, :], kc == 0, kc == nk_ - 1, [wb_, rh.b()], [ps[bk].b()])
                bb3.append(bk)
            bk = k.bank()
            for hh in range(4):
                k.mm(ps[bk].t[:, 0:NS], av[:, hh, :], yaS.t[:, hh, :], hh == 0, hh == 3, [abf, yaS.b()], [ps[bk].b()])
            bb3.append(bk)
            for j, bb in enumerate(bb3):
                bg_ = k.bank()
                for kc in range(8):
                    k.mm(ps[bg_].t[:, 0:NS], gts[j][0][:, kc, :], hS.t[:, kc, :], kc == 0, kc == 7, [gts[j][1], hS.b()], [ps[bg_].b()])
                sg = tmp()
                k.act(sg.t[:, 0:NS], ps[bg_].t[:, 0:NS], AF.Sigmoid, [ps[bg_].b(), pv["bgate"].b()], [sg.b()], bias=pv["bgate"].t[:, 8 * j + m:8 * j + m + 1])
                if j == 0:
                    k.tt("dve", mnS.t[:], ps[bb].t[:, 0:NS], sg.t[:, 0:NS], MUL, [ps[bb].b(), sg.b()], [mnS.b()])
                else:
                    k.tt("dve", rsS.t[:], ps[bb].t[:, 0:NS], sg.t[:, 0:NS], MUL, [ps[bb].b(), sg.b()], [rsS.b()])
                    if j == 1:
                        k.tt("dve", mnS.t[:], mnS.t[:], rsS.t[:], ADD, [mnS.b(), rsS.b()], [mnS.b()])
                    else:
                        k.tt("dve", mgS.t[:, m, :], mnS.t[:], rsS.t[:], ADD, [mnS.b(), rsS.b()], [mgS.b()])
        bo_ = linS("w_out", l, 0, 8, mgS, 8)
        residS(bo_, 16)
        norm_modS("gffn", 32, 24)
        bu = linS("ffn_w_up", l, 0, 44, hS, 8)
        k.cp("act", upS.t[:], v3(ps[bu].t[:, 0:44 * NS], 44), [ps[bu].b()], [upS.b()])
        for s in range(NS):
            for j in range(2):
                k.dma("sp", ffnc.t[:, :, s, j], I["c_ffn"].t[l][s][j].rearrange("(c p) -> p c", p=128), [I["c_ffn"].b()], [ffnc.b()], nc_ok=True)
            k.dma("sp", O["ffn_s"].t[l][s][0:1, :], I["c_ffn"].t[l][s][1:2, :], [I["c_ffn"].b()], [O["ffn_s"].b()])
            k.dma("sp", O["ffn_s"].t[l][s][1].rearrange("(c p) -> p c", p=128), upS.t[:, :, s], [upS.b()], [O["ffn_s"].b()], nc_ok=True)
        fw = pv["fcw"].t[:, :].rearrange("p (c j) -> p c j", j=3)
        bcj = lambda j: fw[:, :, j:j + 1].to_broadcast([128, 44, NS])
        tA3 = tmp()
        a4 = v3(tA3.t[:, 0:44 * NS], 44)
        RW = [pv["fcw"].b()]
        k.tt("dve", ucS.t[:], upS.t[:], bcj(2), MUL, [upS.b()] + RW, [ucS.b()])
        k.tt("dve", a4, ffnc.t[:, :, :, 1], bcj(1), MUL, [ffnc.b()] + RW, [tA3.b()])
        k.tt("dve", ucS.t[:], ucS.t[:], a4, ADD, [ucS.b(), tA3.b()], [ucS.b()])
        k.tt("dve", a4, ffnc.t[:, :, :, 0], bcj(0), MUL, [ffnc.b()] + RW, [tA3.b()])
        k.tt("dve", ucS.t[:], ucS.t[:], a4, ADD, [ucS.b(), tA3.b()], [ucS.b()])
        k.tt("dve", ucS.t[:], ucS.t[:], bc2(pv["fcb"].t[:, :], 44), ADD, [ucS.b(), pv["fcb"].b()], [ucS.b()])
        k.act(a4[:, 0:22, :], ucS.t[:, 0:22, :], AF.Gelu_apprx_tanh, [ucS.b()], [tA3.b()])
        k.tt("dve", acS.t[:], a4[:, 0:22, :], ucS.t[:, 22:44, :], MUL, [tA3.b(), ucS.b()], [acS.b()])
        bd = k.bank()
        for m in range(8):
            for hf in range(2):
                wv, wb = lw("ffn_w_down", l, m * 128, 128, nk=11, row0=hf * 1408)
                for kc in range(11):
                    k.mm(ps[bd].t[:, m * NS:(m + 1) * NS], wv[:, kc, :], acS.t[:, hf * 11 + kc, :], hf == 0 and kc == 0, hf == 1 and kc == 10,
                         [wb, acS.b()], [ps[bd].b()])
        residS(bd, 40)
        if l == L - 1:
            for s in range(NS):
                k.dma("sp", O["ys"].t[s].rearrange("(c p) -> p c", p=128), xS.t[:, :, s], [xS.b()], [O["ys"].b()], nc_ok=True)
    k.sample_layer = sample_layer

    def block(l, b):
        last = (b == NB - 1)
        TAG[0] = f"b{b}.load_x"
        load_x(l, b)
        TAG[0] = f"b{b}.norm1"
        norm_mod(pv["gs1"], 0)
        TAG[0] = f"b{b}.proj_a"
        proj_a(l, b)
        TAG[0] = f"b{b}.qkv"
        qkv(l, b)
        TAG[0] = f"b{b}.attn"
        attention(b)
        TAG[0] = f"b{b}.ssm"
        ssm(l, b, last)
        TAG[0] = f"b{b}.conv"
        convbr(l, b, last)
        TAG[0] = f"b{b}.merge"
        merge(l)
        TAG[0] = f"b{b}.ffn"
        if getattr(k, "debug", False) and last:
            for nm, tl, np_ in (("d_yssm", yssm, 128), ("d_yconv", yconv, 128), ("d_yattn", yattn, 64), ("d_yg", yg, 128)):
                k.dma("pool", k.dbg[nm].t[:], tl.t[:], tl.all(), [k.dbg[nm].b()])
            k.dma("pool", k.dbg["d_merged"].t[:], actT.t[:, 0:8, :], [actT.b(i) for i in range(8)], [k.dbg["d_merged"].b()])
            k.dma("sp", k.dbg["d_x"].t[:], x.t[:], x.all(), [k.dbg["d_x"].b()])
            k.dma("sp", k.dbg["d_acc"].t[:], xio.t[0:64, :], [xio.b()], [k.dbg["d_acc"].b()])
        ffn(l, b, last)
        TAG[0] = f"b{b}.store_x"
        store_x(l, b)
        TAG[0] = "other"
    k.block = block
    k.locals = dict(locals())
    return k


def host_consts(SEQ):
    NB = SEQ // TB
    c = {}
    c["c_ident"] = np.eye(128, dtype=np.float32)
    kk = np.arange(128)[:, None]
    qq = np.arange(128)[None, :]
    same = (kk % 4) == (qq % 4)
    m = np.stack([kk <= qq, kk >= qq, same & ((kk // 4) <= (qq // 4)), same, same & ((kk // 4) >= (qq // 4))]).astype(np.float32)
    c["c_masks"] = m
    half = 32
    inv = (10000.0 ** (-np.arange(half, dtype=np.float32) / half)).astype(np.float32)
    def tab(pos):
        ang = pos.astype(np.float32)[..., None] * inv
        return np.concatenate([np.cos(ang), np.sin(ang)], axis=-1).astype(np.float32)
    p = np.arange(128)
    rope = np.zeros((NB, 128, 12, 64), np.float32)
    for b in range(NB):
        t0 = b * TB
        for s in range(4):
            rope[b, :, 0 * 4 + s] = tab(t0 + 128 * s + p)
            rope[b, :, 1 * 4 + s] = tab(t0 + 4 * p + s)
            rope[b, :, 2 * 4 + s] = tab(t0 + 4 * p + s)
    c["c_rope"] = rope
    c["c_rope_s"] = tab(np.array([PAST]))
    c["c_iota"] = np.arange(512, dtype=np.float32)
    return c


WNAMES = ["w_ada", "b_ada", "g_norm_mix", "w_in", "ssm_a_re", "ssm_a_im", "ssm_log_dt", "ssm_b_re", "ssm_b_im", "ssm_c_re",
          "ssm_c_im", "ssm_d", "ssm_w_glu", "ssm_b_glu", "conv_w", "conv_b", "conv_ln_g", "conv_ln_b", "attn_gq", "attn_gk",
          "w_gate", "b_gate", "w_br_ssm", "w_br_conv", "w_br_attn", "w_out", "g_norm_ffn", "ffn_w_up", "ffn_conv_w",
          "ffn_conv_b", "ffn_w_down"]


def make_in_maps(inp, SEQ, DEPTH, NS, ncores):
    consts = host_consts(SEQ)
    maps = []
    f = lambda a: np.ascontiguousarray(a, dtype=np.float32)
    for c in range(ncores):
        b = c % inp["x_prompt"].shape[0]
        ss = slice(c * NS, (c + 1) * NS)
        m = dict(consts)
        m["xp"] = f(inp["x_prompt"][b, :SEQ])
        m["cp"] = f(inp["c_prompt"][b:b + 1])
        m["xs"] = f(inp["x_sample"][ss, 0])
        m["cs"] = f(inp["c_sample"][ss])
        m["st_ssm"] = f(inp["state_ssm"][:DEPTH, ss])
        m["c_conv"] = f(inp["cache_conv"][:DEPTH, ss])
        for w, _ in WINS:
            a = inp[f"cache_kv_w{w}"][:DEPTH, ss]
            m[f"kv{w}"] = f(a.reshape(a.shape[0], a.shape[1], a.shape[2], 512))
        m["c_ffn"] = f(inp["cache_ffn"][:DEPTH, ss])
        for n in WNAMES:
            m[n] = f(inp[n][:DEPTH])
        maps.append(m)
    return maps


def program(k):
    k.cast_layer(0)
    for l in range(k.DEPTH):
        k.layer_prep(l)
        if SAMPLE[0]:
            k.sample_layer(l)
        jobs = k.cast_jobs(l + 1) if l + 1 < k.DEPTH else []
        per = -(-len(jobs) // max(1, k.NB - 1)) if jobs else 0
        for b in range(k.NB):
            k.block(l, b)
            if jobs:
                k.cast_some(l + 1, jobs[:per])
                jobs = jobs[per:]
        assert not jobs


def kernel(**inputs):
    SEQ = inputs["x_prompt"].shape[1]
    DEPTH = inputs["w_in"].shape[0]
    NS = 4
    ncores = 8
    k = build(SEQ, DEPTH, NS)
    program(k)
    k.P.emit(k.stack)
    k.stack.close()
    maps = make_in_maps(inputs, SEQ, DEPTH, NS, ncores)
    res = run_bass_kernel_spmd(k.nc, maps, core_ids=list(range(ncores)))
    R = res.results
    B = inputs["x_prompt"].shape[0]
    yp = np.stack([R[b]["yp"] for b in range(B)], axis=0).astype(np.float32)
    ys = np.concatenate([R[c]["ys"] for c in range(ncores)], axis=0)[:, None, :].astype(np.float32)

    def pp(name, tail):
        return np.stack([R[b][name] for b in range(B)], axis=1).reshape((DEPTH, B) + tail).astype(np.float32)

    def sp_(name, tail):
        return np.concatenate([R[c][name] for c in range(ncores)], axis=1).reshape((DEPTH, ncores * NS) + tail).astype(np.float32)

    outs = [yp, ys, pp("ssm_p", (32, 64, 2)), sp_("ssm_s", (32, 64, 2)), pp("conv_p", (30, 512)), sp_("conv_s", (30, 512))]
    for w, _ in WINS:
        outs.append(pp(f"kv{w}_p", (min(w, SEQ), 2, 4, 64)))
        outs.append(sp_(f"kv{w}_s", (w, 2, 4, 64)))
    outs.append(pp("ffn_p", (2, F2)))
    outs.append(sp_("ffn_s", (2, F2)))
    return tuple(outs)
```
